# Optimizing a Trainium2 kernel written in Bass

```python
import math
import jax, jax.numpy as jnp
from jax import lax
import numpy as np

D_MODEL = 2048
BATCH = 2
SEQ = 4096
DEPTH = 2

HEAD_DIM = 128
ROPE_THETA = 10000.0
Q_BLK = 128
NSA_HEADS = 8
NSA_KV_HEADS = 2
NSA_GROUP = NSA_HEADS // NSA_KV_HEADS
CMP_LEN = 32
CMP_STRIDE = 16
SEL_LEN = 64
SEL_TOPK = 16
WIN = 512
SEL_FORCE = 1.0e4
SB_HEADS = 8
DIFF_HEADS = 8
LAMBDA_STD = 0.1
N_EVEN = (DEPTH + 1) // 2
N_ODD = DEPTH // 2
DEEPNORM_ALPHA = (2.0 * DEPTH) ** 0.25
DEEPNORM_BETA = (8.0 * DEPTH) ** -0.25
LN_EPS = 1e-5
RMS_EPS = 1e-5
A_Q = NSA_HEADS * HEAD_DIM
A_KV = NSA_KV_HEADS * HEAD_DIM
A_GATES = 3 * NSA_HEADS
B_W = SB_HEADS * HEAD_DIM
EVEN_SIZES = (A_Q, A_KV, A_KV, A_KV, A_KV, A_KV, A_KV, A_GATES, A_Q, B_W, B_W, B_W, B_W)
EVEN_IN = A_Q + 6 * A_KV + A_GATES + A_Q + 4 * B_W
EVEN_OUT = A_Q + B_W
C_W = DIFF_HEADS * 2 * HEAD_DIM
ODD_SIZES = (C_W, C_W, C_W, C_W)
ODD_IN = 4 * C_W
ODD_OUT = C_W

kernel_name = 'hybrid_nsa_stickbreak_diffattn_deepnorm'


def _split(h, sizes):
    offs = np.cumsum(np.array(sizes))[:-1].tolist()
    return jnp.split(h, offs, axis=-1)


def _rope_tables(seq):
    pos = jnp.arange(seq, dtype=jnp.float32)
    inv = ROPE_THETA ** (-jnp.arange(0, HEAD_DIM, 2, dtype=jnp.float32) / HEAD_DIM)
    ang = pos[:, None] * inv[None, :]
    return jnp.cos(ang), jnp.sin(ang)


def _rope(x, cos, sin):
    xf = x.astype(jnp.float32)
    x1, x2 = jnp.split(xf, 2, axis=-1)
    c = cos[None, :, None, :]
    s = sin[None, :, None, :]
    return jnp.concatenate([x1 * c - x2 * s, x2 * c + x1 * s], axis=-1).astype(x.dtype)


def _layer_norm(x, g, b):
    xf = x.astype(jnp.float32)
    mu = jnp.mean(xf, axis=-1, keepdims=True)
    var = jnp.mean(jnp.square(xf - mu), axis=-1, keepdims=True)
    y = (xf - mu) * lax.rsqrt(var + LN_EPS) * g.astype(jnp.float32) + b.astype(jnp.float32)
    return y.astype(x.dtype)


def _masked_softmax(s, mask):
    s = jnp.where(mask, s.astype(jnp.float32), -jnp.inf)
    m = jnp.max(s, axis=-1, keepdims=True)
    m = jnp.where(jnp.isfinite(m), m, 0.0)
    e = jnp.exp(s - m)
    den = jnp.sum(e, axis=-1, keepdims=True)
    return e / jnp.where(den > 0.0, den, 1.0)


def _nsa_compress(kv, pe, w1, w2):
    b, s, hk, d = kv.shape
    n_sub = CMP_LEN // CMP_STRIDE
    chunks = kv.reshape(b, s // CMP_STRIDE, CMP_STRIDE, hk, d)
    n_cmp = s // CMP_STRIDE - n_sub + 1
    blocks = jnp.concatenate([chunks[:, m:m + n_cmp] for m in range(n_sub)], axis=2)
    blocks = blocks + pe[None, None, :, None, :]
    flat = blocks.transpose(0, 1, 3, 2, 4).reshape(b, n_cmp, hk, CMP_LEN * d)
    return jax.nn.silu(flat @ w1) @ w2


def _nsa(q, kc, vc, ks, vs, kw, vw, gates, cos, sin, pe_k, pe_v, w1k, w2k, w1v, w2v):
    b, s = q.shape[0], q.shape[1]
    hk, g, d = NSA_KV_HEADS, NSA_GROUP, HEAD_DIM
    scale = HEAD_DIM ** -0.5
    t = jnp.arange(s)
    k_cmp = _nsa_compress(kc, pe_k, w1k, w2k)
    v_cmp = _nsa_compress(vc, pe_v, w1v, w2v)
    n_cmp = k_cmp.shape[1]
    qg = q.reshape(b, s, hk, g, d)
    sc = jnp.einsum('bthgd,bnhd->bhgtn', qg, k_cmp).astype(jnp.float32) * scale
    blk_end = jnp.arange(n_cmp) * CMP_STRIDE + CMP_LEN - 1
    p_cmp = _masked_softmax(sc, blk_end[None, :] <= t[:, None])
    o_cmp = jnp.einsum('bhgtn,bnhd->bthgd', p_cmp.astype(v_cmp.dtype), v_cmp)
    n_sel_blk = s // SEL_LEN
    c_start = jnp.arange(n_cmp) * CMP_STRIDE
    s_start = jnp.arange(n_sel_blk) * SEL_LEN
    overlap = ((c_start[:, None] < s_start[None, :] + SEL_LEN)
               & (c_start[:, None] + CMP_LEN > s_start[None, :])).astype(jnp.float32)
    imp = jnp.einsum('bhgtn,nj->bhtj', p_cmp, overlap)
    j_cur = t // SEL_LEN
    jj = jnp.arange(n_sel_blk)
    valid = jj[None, :] <= j_cur[:, None]
    forced = (jj[None, :] == 0) | (jj[None, :] == j_cur[:, None]) | (jj[None, :] == j_cur[:, None] - 1)
    score = jnp.where(forced, SEL_FORCE, jnp.where(valid, imp, -SEL_FORCE))
    n_sel = min(SEL_TOPK, n_sel_blk)
    _, idx = lax.top_k(score, n_sel)
    qr = _rope(q, cos, sin).reshape(b, s, hk, g, d).transpose(0, 2, 3, 1, 4)
    ks_b = _rope(ks, cos, sin).transpose(0, 2, 1, 3).reshape(b, hk, n_sel_blk, SEL_LEN, d)
    vs_b = vs.transpose(0, 2, 1, 3).reshape(b, hk, n_sel_blk, SEL_LEN, d)
    pad = ((0, 0), (0, 0), (WIN, 0), (0, 0))
    kw_p = jnp.pad(_rope(kw, cos, sin).transpose(0, 2, 1, 3), pad)
    vw_p = jnp.pad(vw.transpose(0, 2, 1, 3), pad)
    gather = jax.vmap(jax.vmap(lambda tab, ix: tab[ix]))

    def block(i):
        q0 = i * Q_BLK
        tq = q0 + jnp.arange(Q_BLK)
        qb = lax.dynamic_slice_in_dim(qr, q0, Q_BLK, axis=3)
        ib = lax.dynamic_slice_in_dim(idx, q0, Q_BLK, axis=2)
        kg = gather(ks_b, ib)
        vg = gather(vs_b, ib)
        s_sel = jnp.einsum('bhgqd,bhqnjd->bhgqnj', qb, kg).astype(jnp.float32) * scale
        pos = ib[..., None] * SEL_LEN + jnp.arange(SEL_LEN)
        m_sel = (pos <= tq[None, None, :, None, None]).reshape(b, hk, 1, Q_BLK, n_sel * SEL_LEN)
        p_sel = _masked_softmax(s_sel.reshape(b, hk, g, Q_BLK, n_sel * SEL_LEN), m_sel)
        p_sel = p_sel.reshape(b, hk, g, Q_BLK, n_sel, SEL_LEN).astype(vg.dtype)
        o_sel = jnp.einsum('bhgqnj,bhqnjd->bhgqd', p_sel, vg)
        kwb = lax.dynamic_slice_in_dim(kw_p, q0, WIN + Q_BLK, axis=2)
        vwb = lax.dynamic_slice_in_dim(vw_p, q0, WIN + Q_BLK, axis=2)
        spos = q0 - WIN + jnp.arange(WIN + Q_BLK)
        m_w = (spos[None, :] >= 0) & (spos[None, :] <= tq[:, None]) & (tq[:, None] - spos[None, :] < WIN)
        s_w = jnp.einsum('bhgqd,bhkd->bhgqk', qb, kwb).astype(jnp.float32) * scale
        p_w = _masked_softmax(s_w, m_w).astype(vwb.dtype)
        o_w = jnp.einsum('bhgqk,bhkd->bhgqd', p_w, vwb)
        return o_sel, o_w

    o_sel, o_win = lax.map(block, jnp.arange(s // Q_BLK))
    o_sel = o_sel.transpose(1, 0, 4, 2, 3, 5).reshape(b, s, hk, g, d)
    o_win = o_win.transpose(1, 0, 4, 2, 3, 5).reshape(b, s, hk, g, d)
    gt = jax.nn.sigmoid(gates.astype(jnp.float32)).reshape(b, s, hk, g, 3)
    o = gt[..., 0:1] * o_cmp + gt[..., 1:2] * o_sel + gt[..., 2:3] * o_win
    return o.reshape(b, s, hk * g * d).astype(q.dtype)


def _stick_breaking(q, k, v):
    b, s, h, d = q.shape
    scale = HEAD_DIM ** -0.5
    qh = q.transpose(0, 2, 1, 3)
    kh = k.transpose(0, 2, 1, 3)
    vh = v.transpose(0, 2, 1, 3)
    s_idx = jnp.arange(s)

    def block(i):
        q0 = i * Q_BLK
        tq = q0 + jnp.arange(Q_BLK)
        qb = lax.dynamic_slice_in_dim(qh, q0, Q_BLK, axis=2)
        z = jnp.einsum('bhqd,bhkd->bhqk', qb, kh).astype(jnp.float32) * scale
        mask = s_idx[None, :] < tq[:, None]
        log1m = jnp.where(mask, jax.nn.log_sigmoid(-z), 0.0)
        between = lax.cumsum(log1m, axis=3, reverse=True) - log1m
        a = jnp.where(mask, jnp.exp(jax.nn.log_sigmoid(z) + between), 0.0)
        return jnp.einsum('bhqk,bhkd->bhqd', a.astype(vh.dtype), vh)

    o = lax.map(block, jnp.arange(s // Q_BLK))
    return o.transpose(1, 0, 3, 2, 4).reshape(b, s, h * d)


def _diff_attn(q, k, v, lq1, lk1, lq2, lk2, gn_g, lambda_init, cos, sin):
    b, s = q.shape[0], q.shape[1]
    h, d = DIFF_HEADS, HEAD_DIM
    scale = HEAD_DIM ** -0.5
    qh = _rope(q.reshape(b, s, 2 * h, d), cos, sin).reshape(b, s, h, 2, d).transpose(0, 2, 3, 1, 4)
    kh = _rope(k.reshape(b, s, 2 * h, d), cos, sin).reshape(b, s, h, 2, d).transpose(0, 2, 3, 1, 4)
    vh = v.reshape(b, s, h, 2 * d).transpose(0, 2, 1, 3)
    lam = (jnp.exp(jnp.sum(lq1.astype(jnp.float32) * lk1.astype(jnp.float32)))
           - jnp.exp(jnp.sum(lq2.astype(jnp.float32) * lk2.astype(jnp.float32))) + lambda_init)
    s_idx = jnp.arange(s)

    def block(i):
        q0 = i * Q_BLK
        tq = q0 + jnp.arange(Q_BLK)
        qb = lax.dynamic_slice_in_dim(qh, q0, Q_BLK, axis=3)
        sc = jnp.einsum('bhcqd,bhckd->bhcqk', qb, kh).astype(jnp.float32) * scale
        p = _masked_softmax(sc, s_idx[None, :] <= tq[:, None])
        a = p[:, :, 0] - lam * p[:, :, 1]
        return jnp.einsum('bhqk,bhke->bhqe', a.astype(vh.dtype), vh)

    o = lax.map(block, jnp.arange(s // Q_BLK))
    o = o.transpose(1, 0, 3, 2, 4).reshape(b, s, h, 2 * d).astype(jnp.float32)
    o = o * lax.rsqrt(jnp.mean(jnp.square(o), axis=-1, keepdims=True) + RMS_EPS)
    o = o * gn_g.astype(jnp.float32).reshape(h, 2 * d) * (1.0 - lambda_init)
    return o.reshape(b, s, h * 2 * d).astype(q.dtype)


def _even_layer(x, w_in, pe_k, pe_v, w1k, w2k, w1v, w2v, w_out, ln_g, ln_b, cos, sin):
    b, s, _ = x.shape
    hproj = x @ w_in
    (qa, kc, vc, ks, vs, kw, vw, ga, gate_a, qb, kb, vb, gate_b) = _split(hproj, EVEN_SIZES)
    r = lambda t, hh: t.reshape(b, s, hh, HEAD_DIM)
    o_a = _nsa(r(qa, NSA_HEADS), r(kc, NSA_KV_HEADS), r(vc, NSA_KV_HEADS), r(ks, NSA_KV_HEADS),
               r(vs, NSA_KV_HEADS), r(kw, NSA_KV_HEADS), r(vw, NSA_KV_HEADS), ga, cos, sin,
               pe_k, pe_v, w1k, w2k, w1v, w2v) * jax.nn.silu(gate_a)
    o_b = _stick_breaking(r(qb, SB_HEADS), r(kb, SB_HEADS), r(vb, SB_HEADS)) * jax.nn.silu(gate_b)
    y = jnp.concatenate([o_a, o_b], axis=-1) @ w_out
    return _layer_norm(DEEPNORM_ALPHA * x + y, ln_g, ln_b)


def _odd_layer(x, w_in, lq1, lk1, lq2, lk2, gn_g, w_out, ln_g, ln_b, lambda_init, cos, sin):
    q, k, v, gate = _split(x @ w_in, ODD_SIZES)
    o = _diff_attn(q, k, v, lq1, lk1, lq2, lk2, gn_g, lambda_init, cos, sin) * jax.nn.silu(gate)
    return _layer_norm(DEEPNORM_ALPHA * x + o @ w_out, ln_g, ln_b)


def setup_inputs(seed: int = 0) -> dict:
    key = jax.random.key(seed)
    k = jax.random.split(key, 24)
    nrm = lambda kk, shape, sc: jax.random.normal(kk, shape, jnp.float32) * sc
    d = HEAD_DIM
    return {
        'x': nrm(k[0], (BATCH, SEQ, D_MODEL), 1.0),
        'ev_w_in': nrm(k[1], (N_EVEN, D_MODEL, EVEN_IN), D_MODEL ** -0.5),
        'ev_pe_k': nrm(k[2], (N_EVEN, CMP_LEN, d), 0.1),
        'ev_pe_v': nrm(k[3], (N_EVEN, CMP_LEN, d), 0.1),
        'ev_w1_k': nrm(k[4], (N_EVEN, CMP_LEN * d, d), (CMP_LEN * d) ** -0.5),
        'ev_w2_k': nrm(k[5], (N_EVEN, d, d), d ** -0.5),
        'ev_w1_v': nrm(k[6], (N_EVEN, CMP_LEN * d, d), (CMP_LEN * d) ** -0.5),
        'ev_w2_v': nrm(k[7], (N_EVEN, d, d), d ** -0.5),
        'ev_w_out': nrm(k[8], (N_EVEN, EVEN_OUT, D_MODEL), EVEN_OUT ** -0.5 * DEEPNORM_BETA),
        'ev_ln_g': 1.0 + nrm(k[9], (N_EVEN, D_MODEL), 0.02),
        'ev_ln_b': nrm(k[10], (N_EVEN, D_MODEL), 0.02),
        'od_w_in': nrm(k[11], (N_ODD, D_MODEL, ODD_IN), D_MODEL ** -0.5),
        'od_lq1': nrm(k[12], (N_ODD, d), LAMBDA_STD),
        'od_lk1': nrm(k[13], (N_ODD, d), LAMBDA_STD),
        'od_lq2': nrm(k[14], (N_ODD, d), LAMBDA_STD),
        'od_lk2': nrm(k[15], (N_ODD, d), LAMBDA_STD),
        'od_gn_g': 1.0 + nrm(k[16], (N_ODD, DIFF_HEADS * 2 * d), 0.02),
        'od_w_out': nrm(k[17], (N_ODD, ODD_OUT, D_MODEL), ODD_OUT ** -0.5 * DEEPNORM_BETA),
        'od_ln_g': 1.0 + nrm(k[18], (N_ODD, D_MODEL), 0.02),
        'od_ln_b': nrm(k[19], (N_ODD, D_MODEL), 0.02),
    }


def reference(x, ev_w_in, ev_pe_k, ev_pe_v, ev_w1_k, ev_w2_k, ev_w1_v, ev_w2_v, ev_w_out, ev_ln_g, ev_ln_b,
              od_w_in, od_lq1, od_lk1, od_lq2, od_lk2, od_gn_g, od_w_out, od_ln_g, od_ln_b):
    cos, sin = _rope_tables(x.shape[1])
    for layer in range(DEPTH):
        i = layer // 2
        if layer % 2 == 0:
            x = _even_layer(x, ev_w_in[i], ev_pe_k[i], ev_pe_v[i], ev_w1_k[i], ev_w2_k[i], ev_w1_v[i],
                            ev_w2_v[i], ev_w_out[i], ev_ln_g[i], ev_ln_b[i], cos, sin)
        else:
            lambda_init = 0.8 - 0.6 * math.exp(-0.3 * layer)
            x = _odd_layer(x, od_w_in[i], od_lq1[i], od_lk1[i], od_lq2[i], od_lk2[i], od_gn_g[i],
                           od_w_out[i], od_ln_g[i], od_ln_b[i], lambda_init, cos, sin)
    return x
```

```python
import math
from contextlib import ExitStack
import numpy as np
import concourse.bass as bass
import concourse.mybir as mybir
from concourse.bass_utils import run_bass_kernel_spmd

ACT = mybir.ActivationFunctionType
ALU = mybir.AluOpType
AX = mybir.AxisListType
F32 = mybir.dt.float32
BF16 = mybir.dt.bfloat16

D = 2048
S = 4096
NCH = 8
NT = 32
SCALE = 128 ** -0.5
NEG = -32768.0
ALPHA = 4.0 ** 0.25
LAMBDA_INIT = 0.8 - 0.6 * math.exp(-0.3)


class _Op:
    __slots__ = ("eng", "fn", "deps", "isdma", "sem", "val", "sig", "waits", "prewait")


class Prog:
    COMPUTE = ("pe", "act", "dve", "pool")
    QUEUES = ("sp", "pool")
    NDMASEM = 8

    def __init__(self, nc, st):
        self.nc = nc
        self.ops = []
        self.last_w = {}
        self.readers = {}
        self.sems = {e: st.enter_context(nc.semaphore("p_" + e)) for e in self.COMPUTE}
        self.dsems = {q: [st.enter_context(nc.semaphore("d_%s%d" % (q, i))) for i in range(self.NDMASEM)]
                      for q in self.QUEUES}
        self.cnt = {e: 0 for e in self.COMPUTE}
        self.dcnt = {q: 0 for q in self.QUEUES}
        self.nphase = 0

    def add(self, eng, fn, reads=(), writes=(), dma=False):
        op = _Op()
        op.eng = eng
        op.fn = fn
        op.isdma = dma
        op.sig = False
        deps = []
        for t in reads:
            w = self.last_w.get(t)
            if w is not None:
                deps.append(w)
        for t in writes:
            w = self.last_w.get(t)
            if w is not None:
                deps.append(w)
            deps.extend(self.readers.get(t, ()))
        op.deps = deps
        for t in reads:
            self.readers.setdefault(t, []).append(op)
        for t in writes:
            self.last_w[t] = op
            self.readers[t] = []
        self.ops.append(op)
        return op

    def dma(self, q, out, in_, reads=(), writes=()):
        return self.add(q, lambda e, o=out, i=in_: e.dma_start(out=o, in_=i), reads, writes, dma=True)

    def emit(self):
        nc = self.nc
        ops = self.ops
        for op in ops:
            for d in op.deps:
                if d.isdma:
                    continue
                if d.eng == "pe" and op.eng == "pe" and not op.isdma:
                    continue
                d.sig = True
        for op in ops:
            op.prewait = None
            if op.isdma:
                i = self.dcnt[op.eng]
                self.dcnt[op.eng] += 1
                s = self.dsems[op.eng][i % self.NDMASEM]
                n = i // self.NDMASEM
                op.sem = s
                op.val = 16 * (n + 1)
                if n > 0:
                    op.prewait = (s, 16 * n)
            elif op.sig:
                self.cnt[op.eng] += 1
                op.sem = self.sems[op.eng]
                op.val = self.cnt[op.eng]
            else:
                op.sem = None
                op.val = 0
        known = {e: {} for e in ("pe", "act", "dve", "pool", "sp")}
        clocks = {}
        for op in ops:
            kn = known[op.eng]
            waits = {}

            def need(sem, val, clk):
                if kn.get(sem, 0) >= val:
                    return
                if waits.get(sem, 0) < val:
                    waits[sem] = val
                kn[sem] = val
                if clk is not None:
                    for s2, v2 in clk.items():
                        if kn.get(s2, 0) < v2:
                            kn[s2] = v2

            if op.prewait is not None:
                need(op.prewait[0], op.prewait[1], None)
            for d in op.deps:
                if d.sem is None:
                    continue
                if (not d.isdma) and d.eng == "pe" and op.eng == "pe" and not op.isdma:
                    continue
                need(d.sem, d.val, clocks.get(id(d)))
            op.waits = list(waits.items())
            if op.sem is not None:
                clk = dict(kn)
                clk[op.sem] = op.val
                clocks[id(op)] = clk
        fw = []
        for q in self.QUEUES:
            for i in range(self.NDMASEM):
                tot = (self.dcnt[q] - i + self.NDMASEM - 1) // self.NDMASEM
                if tot > 0:
                    fw.append((self.dsems[q][i], 16 * tot))
        engmap = {"pe": "tensor", "act": "scalar", "dve": "vector", "pool": "gpsimd", "sp": "sync"}
        with nc.Block("ph%d" % self.nphase) as block:
            for ename, bname in engmap.items():
                mine = [o for o in ops if o.eng == ename]
                if not mine and ename != "sp":
                    continue

                def body(eng, mine=mine, ename=ename):
                    for o in mine:
                        for s, v in o.waits:
                            eng.wait_ge(s, v)
                        ins = o.fn(eng)
                        if o.sem is not None:
                            ins.then_inc(o.sem, 16 if o.isdma else 1)
                    if ename == "sp":
                        for s, v in fw:
                            eng.wait_ge(s, v)

                getattr(block, bname)(body)
        self.nphase += 1
        self.ops = []
        self.last_w = {}
        self.readers = {}


class Rot:
    def __init__(self, name, tensors, keys=None):
        self.name = name
        self.t = tensors
        self.k = keys or ["%s%d" % (name, i) for i in range(len(tensors))]
        self.i = 0

    def next(self):
        k = self.i % len(self.t)
        self.i += 1
        return self.t[k], self.k[k]


class Ctx:
    def __init__(self, nc, st):
        self.nc = nc
        self.st = st
        self.p = Prog(nc, st)
        self.PB = [self.ps("PB%d" % i) for i in range(8)]
        self.PBK = ["PB%d" % i for i in range(8)]

    def sb(self, name, shape, dt, st=None):
        self._uid = getattr(self, "_uid", 0) + 1
        return (st or self.st).enter_context(self.nc.sbuf_tensor("%s_u%d" % (name, self._uid), list(shape), dt))

    def banks(self, idx, name):
        return Rot(name, [self.PB[i] for i in idx], [self.PBK[i] for i in idx])

    def ps(self, name, shape=(128, 512), dt=F32):
        return self.st.enter_context(self.nc.psum_tensor(name, list(shape), dt))

    def rot_sb(self, name, n, shape, dt, st=None):
        return Rot(name, [self.sb("%s_%d" % (name, i), shape, dt, st) for i in range(n)])

    def dram(self, name, shape, dt, kind=None):
        if kind is None:
            return self.nc.dram_tensor(name, list(shape), dt).ap()
        return self.nc.dram_tensor(name, list(shape), dt, kind=kind).ap()

    def mm(self, out, lhsT, rhs, start, stop, reads, writes):
        self.p.add("pe", lambda e, o=out, l=lhsT, r=rhs, s=start, t=stop: e.matmul(o, l, r, start=s, stop=t),
                   reads, writes)

    def actf(self, out, in_, func, reads, writes, bias=None, scale=None, accum_out=None):
        kw = {}
        if bias is not None:
            kw["bias"] = bias
        if scale is not None:
            kw["scale"] = scale
        if accum_out is not None:
            kw["accum_out"] = accum_out
        self.p.add("act", lambda e, o=out, i=in_, f=func, kw=kw: e.activation(out=o, in_=i, func=f, **kw),
                   reads, writes)

    def tt(self, out, in0, in1, op, reads, writes, eng="dve"):
        self.p.add(eng, lambda e, o=out, a=in0, b=in1, op=op: e.tensor_tensor(out=o, in0=a, in1=b, op=op),
                   reads, writes)

    def ts(self, out, in0, s1, op0, reads, writes, s2=None, op1=None, eng="dve"):
        if op1 is None:
            self.p.add(eng, lambda e, o=out, a=in0, s1=s1, op0=op0: e.tensor_scalar(
                out=o, in0=a, scalar1=s1, scalar2=None, op0=op0), reads, writes)
        else:
            self.p.add(eng, lambda e, o=out, a=in0, s1=s1, s2=s2, op0=op0, op1=op1: e.tensor_scalar(
                out=o, in0=a, scalar1=s1, scalar2=s2, op0=op0, op1=op1), reads, writes)

    def stt(self, out, in0, scalar, in1, op0, op1, reads, writes):
        self.p.add("dve", lambda e, o=out, a=in0, s=scalar, b=in1, op0=op0, op1=op1: e.scalar_tensor_tensor(
            out=o, in0=a, scalar=s, in1=b, op0=op0, op1=op1), reads, writes)

    def copy(self, out, in_, reads, writes, eng="dve"):
        self.p.add(eng, lambda e, o=out, i=in_: e.tensor_copy(out=o, in_=i), reads, writes)

    def recip(self, out, in_, reads, writes):
        self.p.add("dve", lambda e, o=out, i=in_: e.reciprocal(out=o, in_=i), reads, writes)

    def memset(self, ap, val, writes, eng="dve"):
        self.p.add(eng, lambda e, a=ap, v=val: e.memset(a, v), (), writes)

    def load(self, out, in_, reads, writes, q="sp"):
        return self.p.dma(q, out, in_, reads, writes)

    def store(self, out, in_, reads, writes, q="sp"):
        return self.p.dma(q, out, in_, reads, writes)


def _consts():
    c = {}
    pos = np.arange(S, dtype=np.float32)
    inv = (np.float32(10000.0) ** (-np.arange(0, 128, 2, dtype=np.float32) / np.float32(128))).astype(np.float32)
    ang = (pos[:, None] * inv[None, :]).astype(np.float32)
    cs, sn = np.cos(ang).T, np.sin(ang).T
    c["ropec"] = np.ascontiguousarray(np.concatenate([cs, cs], 0), dtype=np.float32)
    c["ropes"] = np.ascontiguousarray(np.concatenate([-sn, sn], 0), dtype=np.float32)
    m = np.arange(128)
    psw = np.zeros((128, 128), np.float32)
    psw[(m + 64) % 128, m] = 1.0
    c["pswap"] = psw
    c["ident"] = np.eye(128, dtype=np.float32)
    c["ident32k"] = np.eye(128, dtype=np.float32) * 32768.0
    sl = np.arange(128)[:, None]
    tl = np.arange(512)[None, :]
    c["mcaus"] = np.stack([(128 * i + sl <= tl) for i in range(4)], 1).astype(np.float32)
    c["mstrict"] = np.stack([(128 * i + sl < tl) for i in range(4)], 1).astype(np.float32)
    c["mfar"] = np.stack([(128 * i + sl > tl) for i in range(4)], 1).astype(np.float32)
    c["mcmp"] = np.stack([(tl >= -512 * k + 16 * sl + 31) for k in range(5)], 1).astype(np.float32)
    j = np.arange(64)[:, None]
    s_ = np.arange(S)[None, :]
    c["ebig"] = (s_ // 64 == j).astype(np.float32)
    jt = np.arange(128)[:, None]
    c["negtri"] = -(jt >= np.arange(128)[None, :]).astype(np.float32)
    tlc = np.arange(128)[:, None]
    jj = np.arange(512)[None, :]
    c["mfull"] = np.where((jj - 256) <= np.floor((tlc - 31) / 16.0), 0.0, NEG).astype(np.float32)
    jj = np.arange(128)[None, :]
    hi = (tlc >= 64).astype(np.int64)
    rel = jj - 64
    forced1 = rel == hi
    forced2 = rel == hi - 1
    invalid = rel > hi
    c["tkval"] = (~(forced1 | forced2 | invalid)).astype(np.float32)
    c["tkbias"] = (forced1 * 2.0e4 + forced2 * 1.0e4 + invalid * (-1.0e4)).astype(np.float32)
    return c


_CONST = None


def consts():
    global _CONST
    if _CONST is None:
        _CONST = _consts()
    return _CONST


def phase_proj(cx, xT, wF, wT, wG, ropec, ropes, pswap, FO, TO, GO, spec, nG):
    with ExitStack() as ls:
        NF = len(spec)
        NTC = TO.shape[1]
        psw = cx.sb("psw", [128, 128], BF16, ls)
        cx.load(psw[:], pswap, (), ["pswap"], q="pool")
        Wf = cx.sb("Wf", [128, 16, NF * 128], BF16, ls)
        Wt = cx.sb("Wt", [128, 16, NTC], BF16, ls)
        wFv = wF.rearrange("(k p) c -> p k c", p=128)
        wTv = wT.rearrange("(k p) c -> p k c", p=128)
        for k in range(16):
            cx.load(Wf[:, k, :], wFv[:, k, :], (), ["Wf%d" % k], q="pool")
        for k in range(16):
            cx.load(Wt[:, k, :], wTv[:, k, :], (), ["Wt%d" % k], q="pool")
        if nG:
            Wg = cx.sb("Wg", [128, 16, 8], BF16, ls)
            wGv = wG.rearrange("(k p) c -> p k c", p=128)
            cx.load(Wg[:, :, 0:nG], wGv, (), ["Wg"], q="pool")
        xb = cx.rot_sb("xb", 2, [128, 16, 512], BF16, ls)
        xTv = xT.rearrange("(k p) t -> p k t", p=128)
        pj = cx.banks([0, 1, 2, 3], "pj")
        pr = cx.banks([4, 5], "pr")
        pt = cx.banks([6, 7], "pt")
        ost = cx.rot_sb("ost", 6, [128, 512], BF16, ls)
        qtmp = cx.rot_sb("qtmp", 2, [128, 512], BF16, ls)
        rt1 = cx.rot_sb("rt1", 2, [128, 512], F32, ls)
        rt2 = cx.rot_sb("rt2", 2, [128, 512], F32, ls)
        rc = cx.rot_sb("rc", 2, [128, 512], F32, ls)
        rs = cx.rot_sb("rs", 2, [128, 512], F32, ls)
        gst = cx.rot_sb("gst", 2, [8, 512], F32, ls)
        has_rope = any(s[0].startswith("rope") for s in spec)
        for c in range(NCH):
            tsl = slice(c * 512, (c + 1) * 512)
            x_t, x_k = xb.next()
            for k4 in range(4):
                cx.load(x_t[:, 4 * k4:4 * k4 + 4, :], xTv[:, 4 * k4:4 * k4 + 4, tsl], (),
                        [x_k + "_%d" % k4], q="pool")
            xk = [x_k + "_%d" % (k // 4) for k in range(16)]
            if has_rope:
                rc_t, rc_k = rc.next()
                rs_t, rs_k = rs.next()
                cx.load(rc_t[:], ropec[:, tsl], (), [rc_k], q="sp")
                cx.load(rs_t[:], ropes[:, tsl], (), [rs_k], q="sp")

            def rope(src_bf, src_k, out_idx):
                r_t, r_k = pr.next()
                cx.mm(r_t[:], psw[:], src_bf[:], True, True, [src_k, "pswap"], [r_k])
                a_t, a_k = rt1.next()
                b_t, b_k = rt2.next()
                cx.tt(a_t[:], src_bf[:], rc_t[:], ALU.mult, [src_k, rc_k], [a_k])
                cx.tt(b_t[:], r_t[:], rs_t[:], ALU.mult, [r_k, rs_k], [b_k])
                o_t, o_k = ost.next()
                cx.tt(o_t[:], a_t[:], b_t[:], ALU.add, [a_k, b_k], [o_k])
                cx.store(FO[out_idx, :, tsl], o_t[:], [o_k], [("FO", out_idx, c)], q="sp")

            for fi, (kind, oi, oi2) in enumerate(spec):
                b_t, b_k = pj.next()
                for k in range(16):
                    cx.mm(b_t[:], Wf[:, k, fi * 128:(fi + 1) * 128], x_t[:, k, :], k == 0, k == 15,
                          ["Wf%d" % k, xk[k]], [b_k])
                if kind in ("copy", "scale", "silu"):
                    o_t, o_k = ost.next()
                    if kind == "copy":
                        cx.copy(o_t[:], b_t[:], [b_k], [o_k])
                    elif kind == "scale":
                        cx.actf(o_t[:], b_t[:], ACT.Copy, [b_k], [o_k], scale=SCALE)
                    else:
                        cx.actf(o_t[:], b_t[:], ACT.Silu, [b_k], [o_k])
                    cx.store(FO[oi, :, tsl], o_t[:], [o_k], [("FO", oi, c)], q="sp")
                elif kind == "rope":
                    q_t, q_k = qtmp.next()
                    cx.copy(q_t[:], b_t[:], [b_k], [q_k])
                    rope(q_t, q_k, oi)
                elif kind == "rope_scale_both":
                    o_t, o_k = ost.next()
                    cx.actf(o_t[:], b_t[:], ACT.Copy, [b_k], [o_k], scale=SCALE)
                    cx.store(FO[oi, :, tsl], o_t[:], [o_k], [("FO", oi, c)], q="sp")
                    rope(o_t, o_k, oi2)
                elif kind == "rope_scale":
                    q_t, q_k = qtmp.next()
                    cx.actf(q_t[:], b_t[:], ACT.Copy, [b_k], [q_k], scale=SCALE)
                    rope(q_t, q_k, oi)
            for tt in range(4):
                b_t, b_k = pt.next()
                for k in range(16):
                    cx.mm(b_t[:, 0:NTC], x_t[:, k, tt * 128:(tt + 1) * 128], Wt[:, k, :], k == 0, k == 15,
                          ["Wt%d" % k, xk[k]], [b_k])
                o_t, o_k = ost.next()
                cx.copy(o_t[:, 0:NTC], b_t[:, 0:NTC], [b_k], [o_k])
                tok = c * 4 + tt
                cx.store(TO[tok * 128:(tok + 1) * 128, :], o_t[:, 0:NTC], [o_k], [("TO", tok)], q="sp")
            if nG:
                b_t, b_k = pt.next()
                for k in range(16):
                    cx.mm(b_t[0:nG, :], Wg[:, k, 0:nG], x_t[:, k, :], k == 0, k == 15, ["Wg", xk[k]], [b_k])
                g_t, g_k = gst.next()
                cx.actf(g_t[0:nG, :], b_t[0:nG, :], ACT.Sigmoid, [b_k], [g_k])
                cx.store(GO[0:nG, tsl], g_t[0:nG, :], [g_k], [("GO", c)], q="sp")
        cx.p.emit()


EV_OFF = {}
_o = 0
for _n, _sz in zip(("qa", "kc", "vc", "ks", "vs", "kw", "vw", "ga", "gate_a", "qb", "kb", "vb", "gate_b"),
                   (1024, 256, 256, 256, 256, 256, 256, 24, 1024, 1024, 1024, 1024, 1024)):
    EV_OFF[_n] = _o
    _o += _sz

SPEC0 = ([("rope_scale_both", 0, 4), ("rope_scale_both", 1, 5), ("scale", 2, None), ("scale", 3, None),
          ("copy", 6, None), ("copy", 7, None), ("rope", 8, None), ("rope", 9, None),
          ("silu", 10, None), ("silu", 11, None), ("scale", 12, None), ("scale", 13, None),
          ("copy", 14, None), ("copy", 15, None), ("silu", 16, None), ("silu", 17, None)])
NFO0 = 18


def l0_weights(w_in, g):
    hk = g // 2
    own = [2 * g, 2 * g + 1]
    oth = [h for h in range(4 * hk, 4 * hk + 4) if h not in own]
    hc = lambda name, h: w_in[:, EV_OFF[name] + h * 128: EV_OFF[name] + (h + 1) * 128]
    cols = [hc("qa", own[0]), hc("qa", own[1]), hc("qa", oth[0]), hc("qa", oth[1]),
            hc("kc", hk), hc("vc", hk), hc("ks", hk), hc("kw", hk),
            hc("gate_a", own[0]), hc("gate_a", own[1]),
            hc("qb", own[0]), hc("qb", own[1]), hc("kb", own[0]), hc("kb", own[1]),
            hc("gate_b", own[0]), hc("gate_b", own[1])]
    wF = np.ascontiguousarray(np.concatenate(cols, 1))
    wT = np.ascontiguousarray(np.concatenate([hc("vs", hk), hc("vw", hk), hc("vb", own[0]), hc("vb", own[1])], 1))
    g0 = EV_OFF["ga"]
    wG = np.ascontiguousarray(w_in[:, g0 + 3 * own[0]: g0 + 3 * own[0] + 6])
    return wF, wT, wG


def phase_nsa_prep(cx, FO, w1k, w1v, w2k, w2v, pekT, pevT, CD, KCMP, VCMP, SELB):
    with ExitStack() as ls:
        sb = lambda n, s, d: cx.sb(n, s, d, ls)
        W1 = [sb("W1k", [128, 32, 128], BF16), sb("W1v", [128, 32, 128], BF16)]
        W2 = [sb("W2k", [128, 128], BF16), sb("W2v", [128, 128], BF16)]
        PE_ = [sb("pek", [128, 32], BF16), sb("pev", [128, 32], BF16)]
        src = [sb("kcs", [128, 256, 16], BF16), sb("vcs", [128, 256, 16], BF16)]
        for i, (w1, w2, pe) in enumerate(((w1k, w2k, pekT), (w1v, w2v, pevT))):
            cx.load(W1[i][:], w1.rearrange("(l d) f -> d l f", d=128), (), ["W1_%d" % i], q="pool")
            cx.load(W2[i][:], w2, (), ["W2_%d" % i], q="pool")
            cx.load(PE_[i][:], pe, (), ["PE_%d" % i], q="pool")
            cx.load(src[i][:], FO[6 + i].rearrange("p (n r) -> p n r", r=16), (), ["src%d" % i], q="sp")
        kcmpT = sb("kcmpT", [128, 256], BF16)
        vcmp = sb("vcmp", [128, 2, 128], BF16)
        cx.memset(kcmpT[:, 255:256], 0.0, ["kcmpT"])
        cx.memset(vcmp[:], 0.0, ["vcmp"])
        sact = [sb("sk", [128, 256], BF16), sb("sv", [128, 256], BF16)]
        bias = [sb("bk", [128, 1], F32), sb("bv", [128, 1], F32)]
        PB, PK = cx.PB, cx.PBK
        for i in range(2):
            for l in range(32):
                cx.mm(PB[0][:, 0:1], W1[i][:, l, :], PE_[i][:, l:l + 1], l == 0, l == 31,
                      ["W1_%d" % i, "PE_%d" % i], [PK[0]])
            cx.copy(bias[i][:], PB[0][:, 0:1], [PK[0]], ["bias%d" % i])
            for l in range(32):
                n0, r = (0, l) if l < 16 else (1, l - 16)
                cx.mm(PB[1][:, 0:255], W1[i][:, l, :], src[i][:, n0:n0 + 255, r], l == 0, l == 31,
                      ["W1_%d" % i, "src%d" % i], [PK[1]])
            cx.actf(sact[i][:, 0:255], PB[1][:, 0:255], ACT.Silu, [PK[1], "bias%d" % i], ["sact%d" % i],
                    bias=bias[i][:])
        cx.mm(PB[2][:, 0:255], W2[0][:], sact[0][:, 0:255], True, True, ["W2_0", "sact0"], [PK[2]])
        cx.copy(kcmpT[:, 0:255], PB[2][:, 0:255], [PK[2]], ["kcmpT"])
        for i, M in ((0, 128), (1, 127)):
            cx.mm(PB[3][0:M, i * 128:(i + 1) * 128], sact[1][:, i * 128:i * 128 + M], W2[1][:], True, True,
                  ["W2_1", "sact1"], [PK[3]])
            cx.copy(vcmp[0:M, i, :], PB[3][0:M, i * 128:(i + 1) * 128], [PK[3]], ["vcmp"])
        cx.store(KCMP, kcmpT[:], ["kcmpT"], ["KCMP"], q="sp")
        cx.store(VCMP, vcmp[:], ["vcmp"], ["VCMP"], q="sp")
        qu = [sb("qu%d" % g, [128, S], BF16) for g in range(4)]
        for g in range(4):
            cx.load(qu[g][:], FO[g], (), ["qu%d" % g], q="sp")
        mfull = sb("mfull", [128, 512], F32)
        tkval = sb("tkval", [128, 128], F32)
        tkbias = sb("tkbias", [128, 128], F32)
        id32 = sb("id32", [128, 128], BF16)
        cx.load(mfull[:], CD["mfull"], (), ["mfull"], q="sp")
        cx.load(tkval[:], CD["tkval"], (), ["tkval"], q="sp")
        cx.load(tkbias[:], CD["tkbias"], (), ["tkbias"], q="sp")
        cx.load(id32[:], CD["ident32k"], (), ["id32"], q="pool")
        selbT = sb("selbT", [64, S], BF16)
        pairs = Rot("pp", [(0, 1), (2, 3)])
        pT = cx.banks([4, 5], "pT")
        Psum = cx.rot_sb("Psum", 2, [128, 64, 4], F32, ls)
        for t_ in Psum.t:
            cx.memset(t_[:], 0.0, [Psum.k[Psum.t.index(t_)]])
        sm = cx.rot_sb("sm", 3, [128, 255], F32, ls)
        ee = cx.rot_sb("ee", 3, [128, 255], F32, ls)
        sml = cx.rot_sb("sml", 12, [128, 1], F32, ls)
        imp = cx.rot_sb("imp", 2, [128, 64], F32, ls)
        sc = cx.rot_sb("sc", 2, [128, 64], F32, ls)
        sc2 = cx.rot_sb("sc2", 2, [128, 64], F32, ls)
        m8 = cx.rot_sb("m8", 4, [128, 8], F32, ls)
        selb = cx.rot_sb("selb", 2, [128, 64], BF16, ls)
        for tt in range(NT):
            (ia, ib), _ = pairs.next()
            for g in range(4):
                bi = ia if g < 2 else ib
                off = (g % 2) * 256
                cx.mm(PB[bi][:, off:off + 255], qu[g][:, tt * 128:(tt + 1) * 128], kcmpT[:, 0:255], True, True,
                      ["qu%d" % g, "kcmpT"], [PK[bi]])
            P_t, P_k = Psum.next()
            P2 = P_t[:].rearrange("p a b -> p (a b)")
            for g in range(4):
                bi = ia if g < 2 else ib
                off = (g % 2) * 256
                s_t, s_k = sm.next()
                cx.tt(s_t[:], PB[bi][:, off:off + 255], mfull[:, 256 - 8 * tt: 256 - 8 * tt + 255], ALU.add,
                      [PK[bi], "mfull"], [s_k])
                mx, mx_k = sml.next()
                cx.p.add("dve", lambda e, o=mx[:], i=s_t[:]: e.reduce_max(out=o, in_=i, axis=AX.X), [s_k], [mx_k])
                nm, nm_k = sml.next()
                cx.ts(nm[:], mx[:], -1000.0, ALU.max, [mx_k], [nm_k], s2=-1.0, op1=ALU.mult)
                e_t, e_k = ee.next()
                dn, dn_k = sml.next()
                cx.actf(e_t[:], s_t[:], ACT.Exp, [s_k, nm_k], [e_k, dn_k], bias=nm[:], accum_out=dn[:])
                cx.ts(dn[:], dn[:], 1e-30, ALU.max, [dn_k], [dn_k])
                rd, rd_k = sml.next()
                cx.recip(rd[:], dn[:], [dn_k], [rd_k])
                if g == 0:
                    cx.ts(P2[:, 0:255], e_t[:], rd[:], ALU.mult, [e_k, rd_k], [P_k])
                else:
                    cx.stt(P2[:, 0:255], e_t[:], rd[:], P2[:, 0:255], ALU.mult, ALU.add, [e_k, rd_k, P_k], [P_k])
            i_t, i_k = imp.next()
            cx.p.add("dve", lambda e, o=i_t[:], i=P_t[:]: e.tensor_reduce(out=o, in_=i, axis=AX.X, op=ALU.add),
                     [P_k], [i_k])
            cx.tt(i_t[:, 1:64], i_t[:, 1:64], P_t[:, 0:63, 3], ALU.add, [i_k, P_k], [i_k])
            c_t, c_k = sc.next()
            cx.tt(c_t[:], i_t[:], tkval[:, 64 - 2 * tt:128 - 2 * tt], ALU.mult, [i_k, "tkval"], [c_k])
            cx.tt(c_t[:], c_t[:], tkbias[:, 64 - 2 * tt:128 - 2 * tt], ALU.add, [c_k, "tkbias"], [c_k])
            cx.memset(c_t[:, 0:1], 3.0e4, [c_k])
            m1, m1_k = m8.next()
            cx.p.add("dve", lambda e, o=m1[:], i=c_t[:]: e.max(out=o, in_=i), [c_k], [m1_k])
            d_t, d_k = sc2.next()
            cx.p.add("dve", lambda e, o=d_t[:], r=m1[:], v=c_t[:]: e.match_replace(
                out=o, in_to_replace=r, in_values=v, imm_value=-1.0e9), [c_k, m1_k], [d_k])
            m2, m2_k = m8.next()
            cx.p.add("dve", lambda e, o=m2[:], i=d_t[:]: e.max(out=o, in_=i), [d_k], [m2_k])
            sb_t, sb_k = selb.next()
            cx.ts(sb_t[:], c_t[:], m2[:, 7:8], ALU.is_ge, [c_k, m2_k], [sb_k], s2=1.0, op1=ALU.subtract)
            t_t, t_k = pT.next()
            cx.mm(t_t[0:64, 0:128], sb_t[:], id32[:], True, True, [sb_k, "id32"], [t_k])
            cx.copy(selbT[:, tt * 128:(tt + 1) * 128], t_t[0:64, 0:128], [t_k], ["selbT"])
        cx.store(SELB, selbT[:], ["selbT"], ["SELB"], q="sp")
        cx.p.emit()


CONST_A = ("ropec", "ropes", "pswap", "ident32k", "mcaus", "mstrict", "mfar", "mcmp", "ebig", "negtri",
           "mfull", "tkval", "tkbias")


def build_A(level=9):
    nc = bass.Bass("TRN2", target_bir_lowering=False)
    with ExitStack() as st:
        cx = Ctx(nc, st)
        C = consts()
        xT = cx.dram("xT", [D, S], F32, "ExternalInput")
        wF = cx.dram("wF", [D, 16 * 128], F32, "ExternalInput")
        wT = cx.dram("wT", [D, 512], F32, "ExternalInput")
        wG = cx.dram("wG", [D, 6], F32, "ExternalInput")
        w1k = cx.dram("w1k", [4096, 128], F32, "ExternalInput")
        w1v = cx.dram("w1v", [4096, 128], F32, "ExternalInput")
        w2k = cx.dram("w2k", [128, 128], F32, "ExternalInput")
        w2v = cx.dram("w2v", [128, 128], F32, "ExternalInput")
        pekT = cx.dram("pekT", [128, 32], F32, "ExternalInput")
        pevT = cx.dram("pevT", [128, 32], F32, "ExternalInput")
        CD = {n: cx.dram("c_" + n, list(C[n].shape), F32, "ExternalInput") for n in CONST_A}
        dbg = "ExternalOutput" if level < 9 else None
        FO = cx.dram("FO", [NFO0, 128, S], BF16, dbg)
        TO = cx.dram("TO", [S, 512], BF16, dbg)
        GO = cx.dram("GO", [8, S], F32, dbg)
        KCMP = cx.dram("KCMP", [128, 256], BF16, dbg)
        VCMP = cx.dram("VCMP", [128, 2, 128], BF16, dbg)
        SELB = cx.dram("SELB", [64, S], BF16, dbg)
        OUT = cx.dram("OUT", [512, S], BF16, "ExternalOutput")
        phase_proj(cx, xT, wF, wT, wG, CD["ropec"], CD["ropes"], CD["pswap"], FO, TO, GO, SPEC0, 6)
        if level >= 2:
            phase_nsa_prep(cx, FO, w1k, w1v, w2k, w2v, pekT, pevT, CD, KCMP, VCMP, SELB)
        if level >= 3:
            phase_nsa_attn(cx, FO, TO, GO, CD, KCMP, VCMP, SELB, OUT)
        if level >= 4:
            phase_sb_attn(cx, FO, TO, CD, OUT)
    return nc


def inputs_A(x, ev, c):
    b, g = c // 4, c % 4
    C = consts()
    wF, wT, wG = l0_weights(ev["w_in"], g)
    m = {"xT": np.ascontiguousarray(x[b].T), "wF": wF, "wT": wT, "wG": wG,
         "w1k": ev["w1_k"], "w1v": ev["w1_v"], "w2k": ev["w2_k"], "w2v": ev["w2_v"],
         "pekT": np.ascontiguousarray(ev["pe_k"].T), "pevT": np.ascontiguousarray(ev["pe_v"].T)}
    for n in CONST_A:
        m["c_" + n] = C[n]
    return m


def attn_tiles(cx, q_ap, q_keys, tiles, oaccs, den, zrot, erot, ones, ident=None):
    n = len(tiles)
    for i, t in enumerate(tiles):
        z_t, z_k = zrot.next()
        sel = t.get("sel")
        cx.mm(z_t[:], t["kT"], q_ap, True, sel is None, list(t["kkeys"]) + list(q_keys), [z_k])
        if sel is not None:
            cx.mm(z_t[:], sel[0], sel[1], False, True, list(sel[2]), [z_k])
        e_t, e_k = erot.next()
        cx.actf(e_t[:], z_t[:], ACT.Exp, [z_k], [e_k])
        if t.get("mask") is not None:
            cx.tt(e_t[:], e_t[:], t["mask"], ALU.mult, [e_k] + list(t["mkeys"]), [e_k])
        for (o_ap, o_k), v in zip(oaccs, t["vs"]):
            cx.mm(o_ap, v, e_t[:], i == 0, i == n - 1, [e_k] + list(t["vkeys"]), [o_k])
        cx.mm(den[0], ones, e_t[:], i == 0, i == n - 1, [e_k, "ones"], [den[1]])


def phase_nsa_attn(cx, FO, TO, GO, CD, KCMP, VCMP, SELB, OUT):
    with ExitStack() as ls:
        sb = lambda n, s, d: cx.sb(n, s, d, ls)
        PB, PK = cx.PB, cx.PBK
        qu = [sb("qu%d" % h, [128, S], BF16) for h in range(2)]
        qr = [sb("qr%d" % h, [128, S], BF16) for h in range(2)]
        gas = [sb("gas%d" % h, [128, S], BF16) for h in range(2)]
        for h in range(2):
            cx.load(qu[h][:], FO[h], (), ["qu%d" % h])
            cx.load(qr[h][:], FO[4 + h], (), ["qr%d" % h])
            cx.load(gas[h][:], FO[10 + h], (), ["gas%d" % h])
        ksT = sb("ksT", [128, S], BF16)
        kwT = sb("kwT", [128, S], BF16)
        cx.load(ksT[:], FO[8], (), ["ksT"])
        cx.load(kwT[:], FO[9], (), ["kwT"])
        vs = sb("vs", [128, NT, 128], BF16)
        vw = sb("vw", [128, NT, 128], BF16)
        TOv = TO.rearrange("(k p) c -> p k c", p=128)
        cx.load(vs[:], TOv[:, :, 0:128], (), ["vs"])
        cx.load(vw[:], TOv[:, :, 128:256], (), ["vw"])
        kcmpT = sb("kcmpT", [128, 256], BF16)
        vcmp = sb("vcmp", [128, 2, 128], BF16)
        selbT = sb("selbT", [64, S], BF16)
        ebig = sb("ebig", [64, S], BF16)
        cx.load(kcmpT[:], KCMP, (), ["kcmpT"])
        cx.load(vcmp[:], VCMP, (), ["vcmp"])
        cx.load(selbT[:], SELB, (), ["selbT"])
        cx.load(ebig[:], CD["ebig"], (), ["ebig"], q="pool")
        mcaus = sb("mcaus", [128, 4, 512], BF16)
        mfar = sb("mfar", [128, 4, 512], BF16)
        mcmp = sb("mcmp", [128, 5, 512], BF16)
        cx.load(mcaus[:], CD["mcaus"], (), ["mcaus"], q="pool")
        cx.load(mfar[:], CD["mfar"], (), ["mfar"], q="pool")
        cx.load(mcmp[:], CD["mcmp"], (), ["mcmp"], q="pool")
        ones = sb("ones", [128, 128], BF16)
        cx.memset(ones[:], 1.0, ["ones"])
        zrot = cx.banks([0, 1, 6, 7], "z")
        sets = Rot("sets", [(2, 3), (4, 5)])
        erot = cx.rot_sb("e", 4, [128, 512], BF16, ls)
        gb = cx.rot_sb("gb", 4, [128, 512], F32, ls)
        dnc = cx.rot_sb("dnc", 2, [128, 512], F32, ls)
        tmp = cx.rot_sb("tmp", 2, [128, 512], F32, ls)
        osum = cx.rot_sb("osum", 2, [128, 512], F32, ls)
        ost = cx.rot_sb("ost", 2, [128, 512], BF16, ls)
        for h in range(2):
            for c in range(NCH):
                tsl = slice(c * 512, (c + 1) * 512)
                os_t, os_k = osum.next()
                for br in range(3):
                    tiles = []
                    if br == 0:
                        q_ap, q_keys = qu[h][:, tsl], ["qu%d" % h]
                        nts = [0] + ([1] if c >= 4 else [])
                        for nt in nts:
                            mk = None
                            if nt == 0 and c <= 4:
                                mk = mcmp[:, c, :]
                            if nt == 1:
                                mk = mcmp[:, c - 4, :]
                            tiles.append(dict(kT=kcmpT[:, nt * 128:(nt + 1) * 128], kkeys=["kcmpT"],
                                              vs=[vcmp[:, nt, :]], vkeys=["vcmp"], mask=mk, mkeys=["mcmp"]))
                    elif br == 1:
                        q_ap, q_keys = qr[h][:, tsl], ["qr%d" % h]
                        for kt in range(4 * c + 4):
                            mk = mcaus[:, kt - 4 * c, :] if kt >= 4 * c else None
                            tiles.append(dict(kT=ksT[:, kt * 128:(kt + 1) * 128], kkeys=["ksT"],
                                              vs=[vs[:, kt, :]], vkeys=["vs"], mask=mk, mkeys=["mcaus"],
                                              sel=(ebig[:, kt * 128:(kt + 1) * 128], selbT[:, tsl],
                                                   ["ebig", "selbT"])))
                    else:
                        q_ap, q_keys = qr[h][:, tsl], ["qr%d" % h]
                        for kt in range(max(0, 4 * c - 4), 4 * c + 4):
                            if kt >= 4 * c:
                                mk, mkk = mcaus[:, kt - 4 * c, :], ["mcaus"]
                            else:
                                mk, mkk = mfar[:, kt - (4 * c - 4), :], ["mfar"]
                            tiles.append(dict(kT=kwT[:, kt * 128:(kt + 1) * 128], kkeys=["kwT"],
                                              vs=[vw[:, kt, :]], vkeys=["vw"], mask=mk, mkeys=mkk))
                    (io, idn), _ = sets.next()
                    attn_tiles(cx, q_ap, q_keys, tiles, [(PB[io][:], PK[io])], (PB[idn][:], PK[idn]),
                               zrot, erot, ones[:])
                    g_t, g_k = gb.next()
                    cx.load(g_t[:], GO[3 * h + br:3 * h + br + 1, tsl].partition_broadcast(128), (), [g_k])
                    d_t, d_k = dnc.next()
                    cx.ts(d_t[:], PB[idn][:], 1e-30, ALU.max, [PK[idn]], [d_k])
                    cx.recip(d_t[:], d_t[:], [d_k], [d_k])
                    cx.tt(d_t[:], d_t[:], g_t[:], ALU.mult, [d_k, g_k], [d_k])
                    if br == 0:
                        cx.tt(os_t[:], PB[io][:], d_t[:], ALU.mult, [PK[io], d_k], [os_k])
                    else:
                        t_t, t_k = tmp.next()
                        cx.tt(t_t[:], PB[io][:], d_t[:], ALU.mult, [PK[io], d_k], [t_k])
                        cx.tt(os_t[:], os_t[:], t_t[:], ALU.add, [os_k, t_k], [os_k])
                o_t, o_k = ost.next()
                cx.tt(o_t[:], os_t[:], gas[h][:, tsl], ALU.mult, [os_k, "gas%d" % h], [o_k])
                cx.store(OUT[h * 128:(h + 1) * 128, tsl], o_t[:], [o_k], [("OUT", h, c)], q="pool")
        cx.p.emit()


def phase_sb_attn(cx, FO, TO, CD, OUT):
    with ExitStack() as ls:
        sb = lambda n, s, d: cx.sb(n, s, d, ls)
        PB, PK = cx.PB, cx.PBK
        TOv = TO.rearrange("(k p) c -> p k c", p=128)
        q = [sb("q%d" % h, [128, S], BF16) for h in range(2)]
        k = [sb("k%d" % h, [128, S], BF16) for h in range(2)]
        gbs = [sb("gbs%d" % h, [128, S], BF16) for h in range(2)]
        v = [sb("v%d" % h, [128, NT, 128], BF16) for h in range(2)]
        for h in range(2):
            cx.load(q[h][:], FO[12 + h], (), ["q%d" % h])
            cx.load(k[h][:], FO[14 + h], (), ["k%d" % h])
            cx.load(gbs[h][:], FO[16 + h], (), ["gbs%d" % h])
            cx.load(v[h][:], TOv[:, :, 256 + 128 * h:384 + 128 * h], (), ["v%d" % h])
        mstr = sb("mstr", [128, 4, 512], BF16)
        negtri = sb("negtri", [128, 128], BF16)
        cx.load(mstr[:], CD["mstrict"], (), ["mstr"], q="pool")
        cx.load(negtri[:], CD["negtri"], (), ["negtri"], q="pool")
        onec = sb("onec", [128, 1], BF16)
        cx.memset(onec[:], 1.0, ["onec"])
        negrow = sb("negrow", [1, 128], F32)
        cx.memset(negrow[:], -1.0, ["negrow"])
        zA = cx.banks([0, 1], "zA")
        zB = cx.banks([2, 3], "zB")
        oac = cx.banks([4, 5], "oac")
        csb = cx.banks([6, 7], "cs")
        Et = cx.rot_sb("Et", 2, [128, 512], F32, ls)
        spt = cx.rot_sb("spt", 3, [128, 512], BF16, ls)
        at = cx.rot_sb("at", 3, [128, 512], BF16, ls)
        Racc = cx.rot_sb("Racc", 2, [1, 512], F32, ls)
        ost = cx.rot_sb("ost", 2, [128, 512], BF16, ls)
        for h in range(2):
            for c in range(NCH):
                tsl = slice(c * 512, (c + 1) * 512)
                o_t, o_k = oac.next()
                kts = list(range(4 * c + 3, -1, -1))
                R_t, R_k = None, None
                for i, kt in enumerate(kts):
                    ksl = slice(kt * 128, (kt + 1) * 128)
                    diag = kt >= 4 * c
                    a_t, a_k = zA.next()
                    cx.mm(a_t[:], k[h][:, ksl], q[h][:, tsl], True, True, ["k%d" % h, "q%d" % h], [a_k])
                    E_t, E_k = Et.next()
                    cx.actf(E_t[:], a_t[:], ACT.Exp, [a_k], [E_k])
                    s_t, s_k = spt.next()
                    cx.actf(s_t[:], E_t[:], ACT.Ln, [E_k], [s_k], bias=1.0)
                    if diag:
                        cx.tt(s_t[:], s_t[:], mstr[:, kt - 4 * c, :], ALU.mult, [s_k, "mstr"], [s_k])
                    b_t, b_k = zB.next()
                    cx.mm(b_t[:], k[h][:, ksl], q[h][:, tsl], True, False, ["k%d" % h, "q%d" % h], [b_k])
                    cx.mm(b_t[:], negtri[:], s_t[:], False, R_t is None, ["negtri", s_k], [b_k])
                    if R_t is not None:
                        cx.mm(b_t[:], negrow[:], R_t[:], False, True, ["negrow", R_k], [b_k])
                    p_t, p_k = at.next()
                    cx.actf(p_t[:], b_t[:], ACT.Exp, [b_k], [p_k])
                    if diag:
                        cx.tt(p_t[:], p_t[:], mstr[:, kt - 4 * c, :], ALU.mult, [p_k, "mstr"], [p_k])
                    cx.mm(o_t[:], v[h][:, kt, :], p_t[:], i == 0, kt == 0, ["v%d" % h, p_k], [o_k])
                    if kt > 0:
                        c_t, c_k = csb.next()
                        cx.mm(c_t[0:1, :], onec[:], s_t[:], True, True, ["onec", s_k], [c_k])
                        Rn_t, Rn_k = Racc.next()
                        if R_t is None:
                            cx.copy(Rn_t[:], c_t[0:1, :], [c_k], [Rn_k])
                        else:
                            cx.tt(Rn_t[:], c_t[0:1, :], R_t[:], ALU.add, [c_k, R_k], [Rn_k])
                        R_t, R_k = Rn_t, Rn_k
                s_o, s_ok = ost.next()
                cx.tt(s_o[:], o_t[:], gbs[h][:, tsl], ALU.mult, [o_k, "gbs%d" % h], [s_ok])
                cx.store(OUT[(2 + h) * 128:(3 + h) * 128, tsl], s_o[:], [s_ok], [("OUT", 2 + h, c)], q="pool")
        cx.p.emit()


def phase_outproj_ln(cx, oT, xrows, wout, lng, lnb, XO, ntok):
    with ExitStack() as ls:
        sb = lambda n, s, d: cx.sb(n, s, d, ls)
        PB, PK = cx.PB, cx.PBK
        Wo = sb("Wo", [128, 16, D], BF16)
        wv = wout.rearrange("(k p) c -> p k c", p=128)
        for k in range(16):
            cx.load(Wo[:, k, :], wv[:, k, :], (), ["Wo%d" % k], q="pool")
        oTs = sb("oTs", [128, 16, ntok], BF16)
        ov = oT.rearrange("(k p) t -> p k t", p=128)
        for k4 in range(4):
            cx.load(oTs[:, 4 * k4:4 * k4 + 4, :], ov[:, 4 * k4:4 * k4 + 4, :], (), ["oTs%d" % k4], q="sp")
        gbc = sb("gbc", [128, D], F32)
        bbc = sb("bbc", [128, D], F32)
        cx.load(gbc[:], lng.partition_broadcast(128), (), ["gbc"], q="sp")
        cx.load(bbc[:], lnb.partition_broadcast(128), (), ["bbc"], q="sp")
        xs = cx.rot_sb("xs", 2, [128, D], F32, ls)
        zs = cx.rot_sb("zs", 2, [128, D], F32, ls)
        st6 = cx.rot_sb("st6", 2, [128, 4, 6], F32, ls)
        mv = cx.rot_sb("mv", 2, [128, 2], F32, ls)
        sml = cx.rot_sb("sml", 6, [128, 1], F32, ls)
        banks = cx.banks(list(range(8)), "y")
        for tt in range(ntok // 128):
            x_t, x_k = xs.next()
            cx.load(x_t[:], xrows[tt * 128:(tt + 1) * 128, :], (), [x_k], q="sp")
            z_t, z_k = zs.next()
            s_t, s_k = st6.next()
            for cg in range(4):
                b_t, b_k = banks.next()
                for k in range(16):
                    cx.mm(b_t[:], oTs[:, k, tt * 128:(tt + 1) * 128], Wo[:, k, cg * 512:(cg + 1) * 512],
                          k == 0, k == 15, ["oTs%d" % (k // 4), "Wo%d" % k], [b_k])
                zc = z_t[:, cg * 512:(cg + 1) * 512]
                cx.stt(zc, x_t[:, cg * 512:(cg + 1) * 512], ALPHA, b_t[:], ALU.mult, ALU.add, [x_k, b_k],
                       [z_k + "_%d" % cg])
                cx.p.add("dve", lambda e, o=s_t[:, cg, :], i=zc: e.bn_stats(out=o, in_=i), [z_k + "_%d" % cg],
                         [s_k + "_%d" % cg])
            m_t, m_k = mv.next()
            cx.p.add("dve", lambda e, o=m_t[:], i=s_t[:]: e.bn_aggr(out=o, in_=i),
                     [s_k + "_%d" % cg for cg in range(4)], [m_k])
            sd, sd_k = sml.next()
            cx.ts(sd[:], m_t[:, 1:2], 1e-5, ALU.add, [m_k], [sd_k])
            cx.actf(sd[:], sd[:], ACT.Sqrt, [sd_k], [sd_k])
            rs, rs_k = sml.next()
            cx.recip(rs[:], sd[:], [sd_k], [rs_k])
            nb, nb_k = sml.next()
            cx.stt(nb[:], m_t[:, 0:1], -1.0, rs[:], ALU.mult, ALU.mult, [m_k, rs_k], [nb_k])
            zk = [z_k + "_%d" % cg for cg in range(4)]
            cx.ts(z_t[:], z_t[:], rs[:], ALU.mult, zk + [rs_k, nb_k], zk, s2=nb[:], op1=ALU.add)
            cx.tt(z_t[:], z_t[:], gbc[:], ALU.mult, zk + ["gbc"], zk)
            cx.tt(z_t[:], z_t[:], bbc[:], ALU.add, zk + ["bbc"], zk)
            cx.store(XO[tt * 128:(tt + 1) * 128, :], z_t[:], zk, [("XO", tt)], q="sp")
        cx.p.emit()


def build_B(ntok=1024):
    nc = bass.Bass("TRN2", target_bir_lowering=False)
    with ExitStack() as st:
        cx = Ctx(nc, st)
        oT = cx.dram("oT", [D, ntok], BF16, "ExternalInput")
        xr = cx.dram("xr", [ntok, D], F32, "ExternalInput")
        wout = cx.dram("wout", [D, D], F32, "ExternalInput")
        lng = cx.dram("lng", [1, D], F32, "ExternalInput")
        lnb = cx.dram("lnb", [1, D], F32, "ExternalInput")
        XO = cx.dram("XO", [ntok, D], F32, "ExternalOutput")
        phase_outproj_ln(cx, oT, xr, wout, lng, lnb, XO, ntok)
    return nc


SPEC1 = ([("rope_scale", i, None) for i in range(4)] + [("rope", 4 + i, None) for i in range(4)]
         + [("silu", 8 + i, None) for i in range(4)])
NFO1 = 12


def l1_weights(w_in, g):
    cols = []
    for base in (0, 2048):
        for h in (2 * g, 2 * g + 1):
            for c in range(2):
                o = base + (2 * h + c) * 128
                cols.append(w_in[:, o:o + 128])
    for h in (2 * g, 2 * g + 1):
        for j in range(2):
            o = 6144 + h * 256 + j * 128
            cols.append(w_in[:, o:o + 128])
    wF = np.ascontiguousarray(np.concatenate(cols, 1))
    wT = np.ascontiguousarray(w_in[:, 4096 + 2 * g * 256: 4096 + (2 * g + 2) * 256])
    return wF, wT


def phase_diff_attn(cx, FO, TO, CD, lq1, lk1, lq2, lk2, gng, OUT):
    with ExitStack() as ls:
        sb = lambda n, s, d: cx.sb(n, s, d, ls)
        PB, PK = cx.PB, cx.PBK
        TOv = TO.rearrange("(k p) c -> p k c", p=128)
        q = [sb("q%d" % i, [128, S], BF16) for i in range(4)]
        k = [sb("k%d" % i, [128, S], BF16) for i in range(4)]
        gt = [sb("gt%d" % i, [128, S], BF16) for i in range(4)]
        v = [sb("v%d" % h, [128, NT, 256], BF16) for h in range(2)]
        for i in range(4):
            cx.load(q[i][:], FO[i], (), ["q%d" % i])
            cx.load(k[i][:], FO[4 + i], (), ["k%d" % i])
            cx.load(gt[i][:], FO[8 + i], (), ["gt%d" % i])
        for h in range(2):
            cx.load(v[h][:], TOv[:, :, 256 * h:256 * (h + 1)], (), ["v%d" % h])
        mcaus = sb("mcaus", [128, 4, 512], BF16)
        cx.load(mcaus[:], CD["mcaus"], (), ["mcaus"], q="pool")
        ones = sb("ones", [128, 128], BF16)
        cx.memset(ones[:], 1.0, ["ones"])
        lam = sb("lam", [128, 1], F32)
        ex = [sb("ex%d" % i, [128, 1], F32) for i in range(2)]
        for i, (a, b_) in enumerate(((lq1, lk1), (lq2, lk2))):
            la = sb("la%d" % i, [128, 128], F32)
            lb = sb("lb%d" % i, [128, 128], F32)
            cx.load(la[:], a.partition_broadcast(128), (), ["la%d" % i])
            cx.load(lb[:], b_.partition_broadcast(128), (), ["lb%d" % i])
            cx.tt(la[:], la[:], lb[:], ALU.mult, ["la%d" % i, "lb%d" % i], ["la%d" % i])
            cx.p.add("dve", lambda e, o=ex[i][:], i_=la[:]: e.reduce_sum(out=o, in_=i_, axis=AX.X),
                     ["la%d" % i], ["ex%d" % i])
            cx.actf(ex[i][:], ex[i][:], ACT.Exp, ["ex%d" % i], ["ex%d" % i])
        cx.ts(lam[:], ex[0][:], ex[1][:], ALU.subtract, ["ex0", "ex1"], ["lam"], s2=-LAMBDA_INIT, op1=ALU.subtract)
        cx.ts(lam[:], lam[:], -1.0, ALU.mult, ["lam"], ["lam"])
        gsc = sb("gsc", [128, 4], F32)
        cx.load(gsc[:], gng, (), ["gsc"])
        cx.ts(gsc[:], gsc[:], 1.0 - LAMBDA_INIT, ALU.mult, ["gsc"], ["gsc"])
        zrot = cx.banks([0, 1], "z")
        sets = Rot("sets", [(2, 3, 4), (5, 6, 7)])
        erot = cx.rot_sb("e", 4, [128, 512], BF16, ls)
        rd = cx.rot_sb("rd", 2, [128, 512], F32, ls)
        oA = cx.rot_sb("oA", 2, [128, 512], F32, ls)
        oB = cx.rot_sb("oB", 2, [128, 512], F32, ls)
        tm = cx.rot_sb("tm", 2, [128, 512], F32, ls)
        sq = cx.rot_sb("sq", 2, [128, 512], BF16, ls)
        ost = cx.rot_sb("ost", 2, [128, 512], BF16, ls)
        for h in range(2):
            for c in range(NCH):
                tsl = slice(c * 512, (c + 1) * 512)
                oh = [oA.next(), oB.next()]
                for cc in range(2):
                    qi = 2 * h + cc
                    tiles = []
                    for kt in range(4 * c + 4):
                        mk = mcaus[:, kt - 4 * c, :] if kt >= 4 * c else None
                        tiles.append(dict(kT=k[qi][:, kt * 128:(kt + 1) * 128], kkeys=["k%d" % qi],
                                          vs=[v[h][:, kt, 0:128], v[h][:, kt, 128:256]], vkeys=["v%d" % h],
                                          mask=mk, mkeys=["mcaus"]))
                    (ia, ib, idn), _ = sets.next()
                    attn_tiles(cx, q[qi][:, tsl], ["q%d" % qi], tiles, [(PB[ia][:], PK[ia]), (PB[ib][:], PK[ib])],
                               (PB[idn][:], PK[idn]), zrot, erot, ones[:])
                    r_t, r_k = rd.next()
                    cx.recip(r_t[:], PB[idn][:], [PK[idn]], [r_k])
                    for j, ib_ in enumerate((ia, ib)):
                        o_t, o_k = oh[j]
                        if cc == 0:
                            cx.tt(o_t[:], PB[ib_][:], r_t[:], ALU.mult, [PK[ib_], r_k], [o_k])
                        else:
                            t_t, t_k = tm.next()
                            cx.tt(t_t[:], PB[ib_][:], r_t[:], ALU.mult, [PK[ib_], r_k], [t_k])
                            cx.stt(o_t[:], t_t[:], lam[:], o_t[:], ALU.mult, ALU.add, [t_k, "lam", o_k], [o_k])
                z_t, z_k = zrot.next()
                for j in range(2):
                    s_t, s_k = sq.next()
                    cx.tt(s_t[:], oh[j][0][:], oh[j][0][:], ALU.mult, [oh[j][1]], [s_k])
                    cx.mm(z_t[:], ones[:], s_t[:], j == 0, j == 1, ["ones", s_k], [z_k])
                r_t, r_k = rd.next()
                cx.ts(r_t[:], z_t[:], 1.0 / 256.0, ALU.mult, [z_k], [r_k], s2=1e-5, op1=ALU.add)
                cx.actf(r_t[:], r_t[:], ACT.Sqrt, [r_k], [r_k])
                cx.recip(r_t[:], r_t[:], [r_k], [r_k])
                for j in range(2):
                    o_t, o_k = oh[j]
                    cx.stt(o_t[:], o_t[:], gsc[:, 2 * h + j:2 * h + j + 1], r_t[:], ALU.mult, ALU.mult,
                           [o_k, "gsc", r_k], [o_k])
                    s_o, s_ok = ost.next()
                    cx.tt(s_o[:], o_t[:], gt[2 * h + j][:, tsl], ALU.mult, [o_k, "gt%d" % (2 * h + j)], [s_ok])
                    cx.store(OUT[(2 * h + j) * 128:(2 * h + j + 1) * 128, tsl], s_o[:], [s_ok],
                             [("OUT", 2 * h + j, c)], q="pool")
        cx.p.emit()


def build_C(level=9):
    nc = bass.Bass("TRN2", target_bir_lowering=False)
    with ExitStack() as st:
        cx = Ctx(nc, st)
        C = consts()
        xT = cx.dram("xT", [D, S], F32, "ExternalInput")
        wF = cx.dram("wF", [D, 12 * 128], F32, "ExternalInput")
        wT = cx.dram("wT", [D, 512], F32, "ExternalInput")
        lq1 = cx.dram("lq1", [1, 128], F32, "ExternalInput")
        lk1 = cx.dram("lk1", [1, 128], F32, "ExternalInput")
        lq2 = cx.dram("lq2", [1, 128], F32, "ExternalInput")
        lk2 = cx.dram("lk2", [1, 128], F32, "ExternalInput")
        gng = cx.dram("gng", [128, 4], F32, "ExternalInput")
        CD = {n: cx.dram("c_" + n, list(C[n].shape), F32, "ExternalInput")
              for n in ("ropec", "ropes", "pswap", "mcaus")}
        dbg = "ExternalOutput" if level < 9 else None
        FO = cx.dram("FO", [NFO1, 128, S], BF16, dbg)
        TO = cx.dram("TO", [S, 512], BF16, dbg)
        OUT = cx.dram("OUT", [512, S], BF16, "ExternalOutput")
        phase_proj(cx, xT, wF, wT, None, CD["ropec"], CD["ropes"], CD["pswap"], FO, TO, None, SPEC1, 0)
        phase_diff_attn(cx, FO, TO, CD, lq1, lk1, lq2, lk2, gng, OUT)
    return nc


def inputs_C(x1T_b, od, g):
    C = consts()
    wF, wT = l1_weights(od["w_in"], g)
    gsl = od["gn_g"][2 * g * 256:(2 * g + 2) * 256]
    m = {"xT": x1T_b, "wF": wF, "wT": wT,
         "lq1": od["lq1"][None, :], "lk1": od["lk1"][None, :], "lq2": od["lq2"][None, :], "lk2": od["lk2"][None, :],
         "gng": np.ascontiguousarray(gsl.reshape(4, 128).T)}
    for n in ("ropec", "ropes", "pswap", "mcaus"):
        m["c_" + n] = C[n]
    return m


_NC = {}


def _get(name, fn):
    if name not in _NC:
        _NC[name] = fn()
    return _NC[name]


def _gather_oT(res, b, r, order):
    blocks = [None] * 16
    for g in range(4):
        O = np.asarray(res[4 * b + g]["OUT"])
        for i in range(4):
            blocks[order(g, i)] = O[i * 128:(i + 1) * 128, r * 1024:(r + 1) * 1024]
    return np.ascontiguousarray(np.concatenate(blocks, 0))


def kernel(x, ev_w_in, ev_pe_k, ev_pe_v, ev_w1_k, ev_w2_k, ev_w1_v, ev_w2_v, ev_w_out, ev_ln_g, ev_ln_b,
           od_w_in, od_lq1, od_lk1, od_lq2, od_lk2, od_gn_g, od_w_out, od_ln_g, od_ln_b):
    x = np.asarray(x, dtype=np.float32)
    ev = dict(w_in=np.asarray(ev_w_in)[0], pe_k=np.asarray(ev_pe_k)[0], pe_v=np.asarray(ev_pe_v)[0],
              w1_k=np.asarray(ev_w1_k)[0], w2_k=np.asarray(ev_w2_k)[0], w1_v=np.asarray(ev_w1_v)[0],
              w2_v=np.asarray(ev_w2_v)[0])
    od = dict(w_in=np.asarray(od_w_in)[0], lq1=np.asarray(od_lq1)[0], lk1=np.asarray(od_lk1)[0],
              lq2=np.asarray(od_lq2)[0], lk2=np.asarray(od_lk2)[0], gn_g=np.asarray(od_gn_g)[0])
    cores = list(range(8))
    ncA = _get("A", build_A)
    resA = run_bass_kernel_spmd(ncA, [inputs_A(x, ev, c) for c in cores], core_ids=cores).results
    ordA = lambda g, i: (2 * g + i) if i < 2 else (8 + 2 * g + (i - 2))
    ncB = _get("B", build_B)
    inB = []
    for c in cores:
        b, r = c // 4, c % 4
        inB.append({"oT": _gather_oT(resA, b, r, ordA), "xr": np.ascontiguousarray(x[b, r * 1024:(r + 1) * 1024]),
                    "wout": np.asarray(ev_w_out)[0], "lng": np.asarray(ev_ln_g), "lnb": np.asarray(ev_ln_b)})
    resB = run_bass_kernel_spmd(ncB, inB, core_ids=cores).results
    x1 = np.stack([np.concatenate([np.asarray(resB[4 * b + r]["XO"]) for r in range(4)], 0) for b in range(2)], 0)
    ncC = _get("C", build_C)
    x1T = [np.ascontiguousarray(x1[b].T) for b in range(2)]
    resC = run_bass_kernel_spmd(ncC, [inputs_C(x1T[c // 4], od, c % 4) for c in cores], core_ids=cores).results
    ordC = lambda g, i: 4 * g + i
    inD = []
    for c in cores:
        b, r = c // 4, c % 4
        inD.append({"oT": _gather_oT(resC, b, r, ordC), "xr": np.ascontiguousarray(x1[b, r * 1024:(r + 1) * 1024]),
                    "wout": np.asarray(od_w_out)[0], "lng": np.asarray(od_ln_g), "lnb": np.asarray(od_ln_b)})
    resD = run_bass_kernel_spmd(ncB, inD, core_ids=cores).results
    out = np.stack([np.concatenate([np.asarray(resD[4 * b + r]["XO"]) for r in range(4)], 0) for b in range(2)], 0)
    return out.astype(np.float32)
```

```python
import math
from contextlib import ExitStack
import numpy as np
import concourse.bass as bass
import concourse.mybir as mybir
from concourse.bass_utils import run_bass_kernel_spmd

ACT = mybir.ActivationFunctionType
ALU = mybir.AluOpType
AX = mybir.AxisListType
F32 = mybir.dt.float32
BF16 = mybir.dt.bfloat16

D = 2048
S = 4096
NCH = 8
NT = 32
SCALE = 128 ** -0.5
NEG = -32768.0
ALPHA = 4.0 ** 0.25
LAMBDA_INIT = 0.8 - 0.6 * math.exp(-0.3)


class _Op:
    __slots__ = ("eng", "fn", "deps", "isdma", "sem", "val", "sig", "waits", "prewait", "ccsem")


class Prog:
    COMPUTE = ("pe", "act", "dve", "pool")
    QUEUES = ("sp", "pool")
    NDMASEM = 8

    def __init__(self, nc, st):
        self.nc = nc
        self.ops = []
        self.last_w = {}
        self.readers = {}
        self.sems = {e: st.enter_context(nc.semaphore("p_" + e)) for e in self.COMPUTE}
        self.dsems = {q: [st.enter_context(nc.semaphore("d_%s%d" % (q, i))) for i in range(self.NDMASEM)]
                      for q in self.QUEUES}
        self.cnt = {e: 0 for e in self.COMPUTE}
        self.dcnt = {q: 0 for q in self.QUEUES}
        self.nphase = 0

    def add(self, eng, fn, reads=(), writes=(), dma=False):
        op = _Op()
        op.eng = eng
        op.fn = fn
        op.isdma = dma
        op.sig = False
        op.ccsem = None
        deps = []
        for t in reads:
            w = self.last_w.get(t)
            if w is not None:
                deps.append(w)
        for t in writes:
            w = self.last_w.get(t)
            if w is not None:
                deps.append(w)
            last = {}
            for r in self.readers.get(t, ()):
                if r.isdma:
                    deps.append(r)
                else:
                    last[r.eng] = r
            deps.extend(last.values())
        op.deps = deps
        for t in reads:
            self.readers.setdefault(t, []).append(op)
        for t in writes:
            self.last_w[t] = op
            self.readers[t] = []
        self.ops.append(op)
        return op

    def dma(self, q, out, in_, reads=(), writes=()):
        return self.add(q, lambda e, o=out, i=in_: e.dma_start(out=o, in_=i), reads, writes, dma=True)

    def collective(self, st, fn, reads=(), writes=()):
        op = self.add("pool", fn, reads, writes, dma=True)
        self.ncc = getattr(self, "ncc", 0) + 1
        op.ccsem = st.enter_context(self.nc.semaphore("cc%d" % self.ncc))
        return op

    def emit(self):
        nc = self.nc
        ops = self.ops
        for op in ops:
            for d in op.deps:
                if d.isdma:
                    continue
                if d.eng == "pe" and op.eng == "pe" and not op.isdma:
                    continue
                d.sig = True
        for op in ops:
            op.prewait = None
            if op.ccsem is not None:
                op.sem = op.ccsem
                op.val = 1
            elif op.isdma:
                i = self.dcnt[op.eng]
                self.dcnt[op.eng] += 1
                s = self.dsems[op.eng][i % self.NDMASEM]
                n = i // self.NDMASEM
                op.sem = s
                op.val = 16 * (n + 1)
                if n > 0:
                    op.prewait = (s, 16 * n)
            elif op.sig:
                self.cnt[op.eng] += 1
                op.sem = self.sems[op.eng]
                op.val = self.cnt[op.eng]
            else:
                op.sem = None
                op.val = 0
        known = {e: {} for e in ("pe", "act", "dve", "pool", "sp")}
        clocks = {}
        for op in ops:
            kn = known[op.eng]
            waits = {}

            def need(sem, val, clk):
                if kn.get(sem, 0) >= val:
                    return
                if waits.get(sem, 0) < val:
                    waits[sem] = val
                kn[sem] = val
                if clk is not None:
                    for s2, v2 in clk.items():
                        if kn.get(s2, 0) < v2:
                            kn[s2] = v2

            if op.prewait is not None:
                need(op.prewait[0], op.prewait[1], None)
            for d in op.deps:
                if d.sem is None:
                    continue
                if (not d.isdma) and d.eng == "pe" and op.eng == "pe" and not op.isdma:
                    continue
                need(d.sem, d.val, clocks.get(id(d)))
            op.waits = list(waits.items())
            if op.sem is not None:
                clk = dict(kn)
                clk[op.sem] = op.val
                clocks[id(op)] = clk
        fw = []
        for q in self.QUEUES:
            for i in range(self.NDMASEM):
                tot = (self.dcnt[q] - i + self.NDMASEM - 1) // self.NDMASEM
                if tot > 0:
                    fw.append((self.dsems[q][i], 16 * tot))
        for op in ops:
            if op.ccsem is not None:
                fw.append((op.ccsem, 1))
        engmap = {"pe": "tensor", "act": "scalar", "dve": "vector", "pool": "gpsimd", "sp": "sync"}
        with nc.named_scope("phase%d" % self.nphase), nc.Block("ph%d" % self.nphase) as block:
            for ename, bname in engmap.items():
                mine = [o for o in ops if o.eng == ename]
                if not mine and ename != "sp":
                    continue

                def body(eng, mine=mine, ename=ename):
                    for o in mine:
                        for s, v in o.waits:
                            eng.wait_ge(s, v)
                        ins = o.fn(eng)
                        if o.ccsem is not None:
                            ins.then_inc(o.sem)
                        elif o.sem is not None:
                            ins.then_inc(o.sem, 16 if o.isdma else 1)
                    if ename == "sp":
                        for s, v in fw:
                            eng.wait_ge(s, v)

                getattr(block, bname)(body)
        self.nphase += 1
        self.ops = []
        self.last_w = {}
        self.readers = {}


class Rot:
    def __init__(self, name, tensors, keys=None):
        self.name = name
        self.t = tensors
        self.k = keys or ["%s%d" % (name, i) for i in range(len(tensors))]
        self.i = 0

    def next(self):
        k = self.i % len(self.t)
        self.i += 1
        return self.t[k], self.k[k]


class Ctx:
    def __init__(self, nc, st):
        self.nc = nc
        self.st = st
        self.p = Prog(nc, st)
        self.PB = [self.ps("PB%d" % i) for i in range(8)]
        self.PBK = ["PB%d" % i for i in range(8)]

    def sb(self, name, shape, dt, st=None):
        self._uid = getattr(self, "_uid", 0) + 1
        return (st or self.st).enter_context(self.nc.sbuf_tensor("%s_u%d" % (name, self._uid), list(shape), dt))

    def banks(self, idx, name):
        return Rot(name, [self.PB[i] for i in idx], [self.PBK[i] for i in idx])

    def ps(self, name, shape=(128, 512), dt=F32):
        return self.st.enter_context(self.nc.psum_tensor(name, list(shape), dt))

    def rot_sb(self, name, n, shape, dt, st=None):
        return Rot(name, [self.sb("%s_%d" % (name, i), shape, dt, st) for i in range(n)])

    def dram(self, name, shape, dt, kind=None):
        if kind is None:
            return self.nc.dram_tensor(name, list(shape), dt).ap()
        return self.nc.dram_tensor(name, list(shape), dt, kind=kind).ap()

    def mm(self, out, lhsT, rhs, start, stop, reads, writes):
        self.p.add("pe", lambda e, o=out, l=lhsT, r=rhs, s=start, t=stop: e.matmul(o, l, r, start=s, stop=t),
                   reads, writes)

    def actf(self, out, in_, func, reads, writes, bias=None, scale=None, accum_out=None):
        kw = {}
        if bias is not None:
            kw["bias"] = bias
        if scale is not None:
            kw["scale"] = scale
        if accum_out is not None:
            kw["accum_out"] = accum_out
        self.p.add("act", lambda e, o=out, i=in_, f=func, kw=kw: e.activation(out=o, in_=i, func=f, **kw),
                   reads, writes)

    def tt(self, out, in0, in1, op, reads, writes, eng="dve"):
        self.p.add(eng, lambda e, o=out, a=in0, b=in1, op=op: e.tensor_tensor(out=o, in0=a, in1=b, op=op),
                   reads, writes)

    def ts(self, out, in0, s1, op0, reads, writes, s2=None, op1=None, eng="dve"):
        if op1 is None:
            self.p.add(eng, lambda e, o=out, a=in0, s1=s1, op0=op0: e.tensor_scalar(
                out=o, in0=a, scalar1=s1, scalar2=None, op0=op0), reads, writes)
        else:
            self.p.add(eng, lambda e, o=out, a=in0, s1=s1, s2=s2, op0=op0, op1=op1: e.tensor_scalar(
                out=o, in0=a, scalar1=s1, scalar2=s2, op0=op0, op1=op1), reads, writes)

    def stt(self, out, in0, scalar, in1, op0, op1, reads, writes):
        self.p.add("dve", lambda e, o=out, a=in0, s=scalar, b=in1, op0=op0, op1=op1: e.scalar_tensor_tensor(
            out=o, in0=a, scalar=s, in1=b, op0=op0, op1=op1), reads, writes)

    def copy(self, out, in_, reads, writes, eng="dve"):
        self.p.add(eng, lambda e, o=out, i=in_: e.tensor_copy(out=o, in_=i), reads, writes)

    def recip(self, out, in_, reads, writes):
        self.p.add("dve", lambda e, o=out, i=in_: e.reciprocal(out=o, in_=i), reads, writes)

    def rcp(self, out, in_, reads, writes):
        self.p.add("dve", lambda e, o=out, i=in_: e.reciprocal(out=o, in_=i), reads, writes)

    def memset(self, ap, val, writes, eng="dve"):
        self.p.add(eng, lambda e, a=ap, v=val: e.memset(a, v), (), writes)

    def load(self, out, in_, reads, writes, q="sp"):
        return self.p.dma(q, out, in_, reads, writes)

    def store(self, out, in_, reads, writes, q="sp"):
        return self.p.dma(q, out, in_, reads, writes)


def _consts():
    c = {}
    pos = np.arange(S, dtype=np.float32)
    inv = (np.float32(10000.0) ** (-np.arange(0, 128, 2, dtype=np.float32) / np.float32(128))).astype(np.float32)
    ang = (pos[:, None] * inv[None, :]).astype(np.float32)
    cs, sn = np.cos(ang).T, np.sin(ang).T
    c["ropec"] = np.ascontiguousarray(np.concatenate([cs, cs], 0), dtype=np.float32)
    c["ropes"] = np.ascontiguousarray(np.concatenate([-sn, sn], 0), dtype=np.float32)
    m = np.arange(128)
    psw = np.zeros((128, 128), np.float32)
    psw[(m + 64) % 128, m] = 1.0
    c["pswap"] = psw
    c["ident"] = np.eye(128, dtype=np.float32)
    c["ident32k"] = np.eye(128, dtype=np.float32) * 32768.0
    sl = np.arange(128)[:, None]
    tl = np.arange(512)[None, :]
    c["mcaus"] = np.stack([(128 * i + sl <= tl) for i in range(4)], 1).astype(np.float32)
    c["mstrict"] = np.stack([(128 * i + sl < tl) for i in range(4)], 1).astype(np.float32)
    c["mfar"] = np.stack([(128 * i + sl > tl) for i in range(4)], 1).astype(np.float32)
    c["mcmp"] = np.stack([(tl >= -512 * k + 16 * sl + 31) for k in range(5)], 1).astype(np.float32)
    for nm in ("caus", "far", "cmp"):
        c["b" + nm] = ((c["m" + nm] - 1.0) * 32768.0).astype(np.float32)
    j = np.arange(64)[:, None]
    s_ = np.arange(S)[None, :]
    c["ebig"] = (s_ // 64 == j).astype(np.float32)
    jt = np.arange(128)[:, None]
    c["negtri"] = -(jt >= np.arange(128)[None, :]).astype(np.float32)
    tlc = np.arange(128)[:, None]
    jj = np.arange(512)[None, :]
    c["mfull"] = np.where((jj - 256) <= np.floor((tlc - 31) / 16.0), 0.0, NEG).astype(np.float32)
    jj = np.arange(128)[None, :]
    hi = (tlc >= 64).astype(np.int64)
    rel = jj - 64
    forced1 = rel == hi
    forced2 = rel == hi - 1
    invalid = rel > hi
    c["tkval"] = (~(forced1 | forced2 | invalid)).astype(np.float32)
    c["tkbias"] = (forced1 * 2.0e4 + forced2 * 1.0e4 + invalid * (-1.0e4)).astype(np.float32)
    return c


_CONST = None


def consts():
    global _CONST
    if _CONST is None:
        _CONST = _consts()
    return _CONST


def phase_proj(cx, xT, wF, wT, wG, ropec, ropes, pswap, FO, TO, GO, spec, nG, xsrc=None):
    with ExitStack() as ls:
        NF = len(spec)
        NTC = TO.shape[1]
        psw = cx.sb("psw", [128, 128], BF16, ls)
        cx.load(psw[:], pswap, (), ["pswap"], q="pool")
        Wf = cx.sb("Wf", [128, 16, NF * 128], BF16, ls)
        Wt = cx.sb("Wt", [128, 16, NTC], BF16, ls)
        wFv = wF.rearrange("(k p) c -> p k c", p=128)
        wTv = wT.rearrange("(k p) c -> p k c", p=128)
        def wloads(lo, hi):
            for fi in range(lo, hi):
                cx.load(Wf[:, :, fi * 128:(fi + 1) * 128], wFv[:, :, fi * 128:(fi + 1) * 128], (), ["Wf%d" % fi],
                        q="pool")
        wloads(0, 1)
        if nG:
            Wg = cx.sb("Wg", [128, 16, 8], BF16, ls)
            wGv = wG.rearrange("(k p) c -> p k c", p=128)
        xb = cx.rot_sb("xb", 2, [128, 16, 512], BF16, ls)
        xTv = xT.rearrange("(k p) t -> p k t", p=128) if xsrc is None else None
        pj = cx.banks([0, 1, 2, 3], "pj")
        pr = cx.banks([4, 5], "pr")
        pt = cx.banks([6, 7], "pt")
        ost = cx.rot_sb("ost", 6, [128, 512], BF16, ls)
        qtmp = cx.rot_sb("qtmp", 2, [128, 512], BF16, ls)
        rt1 = cx.rot_sb("rt1", 2, [128, 512], F32, ls)
        rt2 = cx.rot_sb("rt2", 2, [128, 512], F32, ls)
        rc = cx.rot_sb("rc", 2, [128, 512], F32, ls)
        rs = cx.rot_sb("rs", 2, [128, 512], F32, ls)
        gst = cx.rot_sb("gst", 2, [8, 512], F32, ls)
        has_rope = any(s[0].startswith("rope") for s in spec)
        for c in range(NCH):
            tsl = slice(c * 512, (c + 1) * 512)
            x_t, x_k = xb.next()
            for k4 in range(4):
                if xsrc is None:
                    cx.load(x_t[:, 4 * k4:4 * k4 + 4, :], xTv[:, 4 * k4:4 * k4 + 4, tsl], (),
                            [x_k + "_%d" % k4], q="pool")
                else:
                    for jj, ((c0, c1), sap) in enumerate(xsrc(c, k4)):
                        cx.load(x_t[:, 4 * k4:4 * k4 + 4, c0:c1], sap, (), [x_k + "_%d_%d" % (k4, jj)], q="sp")
            if xsrc is None:
                xk = [[x_k + "_%d" % (k // 4)] for k in range(16)]
            else:
                xk = [[x_k + "_%d_%d" % (k // 4, jj) for jj in range(2)] for k in range(16)]
            if c == 0:
                wloads(1, NF)
                cx.load(Wt[:, :, :], wTv[:, :, :], (), ["Wt"], q="pool")
                if nG:
                    cx.load(Wg[:, :, 0:nG], wGv, (), ["Wg"], q="pool")
            if has_rope:
                rc_t, rc_k = rc.next()
                rs_t, rs_k = rs.next()
                cx.load(rc_t[:], ropec[:, tsl], (), [rc_k], q="sp")
                cx.load(rs_t[:], ropes[:, tsl], (), [rs_k], q="sp")

            def rope(src_bf, src_k, out_idx):
                r_t, r_k = pr.next()
                cx.mm(r_t[:], psw[:], src_bf[:], True, True, [src_k, "pswap"], [r_k])
                a_t, a_k = rt1.next()
                b_t, b_k = rt2.next()
                cx.tt(a_t[:], src_bf[:], rc_t[:], ALU.mult, [src_k, rc_k], [a_k])
                cx.tt(b_t[:], r_t[:], rs_t[:], ALU.mult, [r_k, rs_k], [b_k])
                o_t, o_k = ost.next()
                cx.tt(o_t[:], a_t[:], b_t[:], ALU.add, [a_k, b_k], [o_k])
                cx.store(FO[out_idx, :, tsl], o_t[:], [o_k], [("FO", out_idx, c)], q="sp")

            for fi, (kind, oi, oi2) in enumerate(spec):
                b_t, b_k = pj.next()
                for k in range(16):
                    cx.mm(b_t[:], Wf[:, k, fi * 128:(fi + 1) * 128], x_t[:, k, :], k == 0, k == 15,
                          ["Wf%d" % fi] + xk[k], [b_k])
                if kind in ("copy", "scale", "silu"):
                    o_t, o_k = ost.next()
                    if kind == "copy":
                        cx.copy(o_t[:], b_t[:], [b_k], [o_k])
                    elif kind == "scale":
                        cx.actf(o_t[:], b_t[:], ACT.Copy, [b_k], [o_k], scale=SCALE)
                    else:
                        cx.actf(o_t[:], b_t[:], ACT.Silu, [b_k], [o_k])
                    cx.store(FO[oi, :, tsl], o_t[:], [o_k], [("FO", oi, c)], q="sp")
                elif kind == "rope":
                    q_t, q_k = qtmp.next()
                    cx.copy(q_t[:], b_t[:], [b_k], [q_k])
                    rope(q_t, q_k, oi)
                elif kind == "rope_scale_both":
                    o_t, o_k = ost.next()
                    cx.actf(o_t[:], b_t[:], ACT.Copy, [b_k], [o_k], scale=SCALE)
                    cx.store(FO[oi, :, tsl], o_t[:], [o_k], [("FO", oi, c)], q="sp")
                    rope(o_t, o_k, oi2)
                elif kind == "rope_scale":
                    q_t, q_k = qtmp.next()
                    cx.actf(q_t[:], b_t[:], ACT.Copy, [b_k], [q_k], scale=SCALE)
                    rope(q_t, q_k, oi)
            for tt in range(4):
                b_t, b_k = pt.next()
                for k in range(16):
                    cx.mm(b_t[:, 0:NTC], x_t[:, k, tt * 128:(tt + 1) * 128], Wt[:, k, :], k == 0, k == 15,
                          ["Wt"] + xk[k], [b_k])
                o_t, o_k = ost.next()
                cx.copy(o_t[:, 0:NTC], b_t[:, 0:NTC], [b_k], [o_k])
                tok = c * 4 + tt
                cx.store(TO[tok * 128:(tok + 1) * 128, :], o_t[:, 0:NTC], [o_k], [("TO", tok)], q="sp")
            if nG:
                b_t, b_k = pt.next()
                for k in range(16):
                    cx.mm(b_t[0:nG, :], Wg[:, k, 0:nG], x_t[:, k, :], k == 0, k == 15, ["Wg"] + xk[k], [b_k])
                g_t, g_k = gst.next()
                cx.actf(g_t[0:nG, :], b_t[0:nG, :], ACT.Sigmoid, [b_k], [g_k])
                cx.store(GO[0:nG, tsl], g_t[0:nG, :], [g_k], [("GO", c)], q="sp")
        cx.p.emit()


EV_OFF = {}
_o = 0
for _n, _sz in zip(("qa", "kc", "vc", "ks", "vs", "kw", "vw", "ga", "gate_a", "qb", "kb", "vb", "gate_b"),
                   (1024, 256, 256, 256, 256, 256, 256, 24, 1024, 1024, 1024, 1024, 1024)):
    EV_OFF[_n] = _o
    _o += _sz

SPEC0 = ([("rope_scale_both", 0, 4), ("rope_scale_both", 1, 5), ("scale", 2, None), ("scale", 3, None),
          ("copy", 6, None), ("copy", 7, None), ("rope", 8, None), ("rope", 9, None),
          ("silu", 10, None), ("silu", 11, None), ("scale", 12, None), ("scale", 13, None),
          ("copy", 14, None), ("copy", 15, None), ("silu", 16, None), ("silu", 17, None)])
NFO0 = 18


def l0_weights(w_in, g):
    hk = g // 2
    own = [2 * g, 2 * g + 1]
    oth = [h for h in range(4 * hk, 4 * hk + 4) if h not in own]
    hc = lambda name, h: w_in[:, EV_OFF[name] + h * 128: EV_OFF[name] + (h + 1) * 128]
    cols = [hc("qa", own[0]), hc("qa", own[1]), hc("qa", oth[0]), hc("qa", oth[1]),
            hc("kc", hk), hc("vc", hk), hc("ks", hk), hc("kw", hk),
            hc("gate_a", own[0]), hc("gate_a", own[1]),
            hc("qb", own[0]), hc("qb", own[1]), hc("kb", own[0]), hc("kb", own[1]),
            hc("gate_b", own[0]), hc("gate_b", own[1])]
    wF = np.ascontiguousarray(np.concatenate(cols, 1))
    wT = np.ascontiguousarray(np.concatenate([hc("vs", hk), hc("vw", hk), hc("vb", own[0]), hc("vb", own[1])], 1))
    g0 = EV_OFF["ga"]
    wG = np.ascontiguousarray(w_in[:, g0 + 3 * own[0]: g0 + 3 * own[0] + 6])
    return wF, wT, wG


def phase_nsa_prep(cx, FO, w1k, w1v, w2k, w2v, pekT, pevT, CD, KCMP, VCMP, SELB):
    with ExitStack() as ls:
        sb = lambda n, s, d: cx.sb(n, s, d, ls)
        W1 = [sb("W1k", [128, 32, 128], BF16), sb("W1v", [128, 32, 128], BF16)]
        W2 = [sb("W2k", [128, 128], BF16), sb("W2v", [128, 128], BF16)]
        PE_ = [sb("pek", [128, 32], BF16), sb("pev", [128, 32], BF16)]
        src = [sb("kcs", [128, 256, 16], BF16), sb("vcs", [128, 256, 16], BF16)]
        for i, (w1, w2, pe) in enumerate(((w1k, w2k, pekT), (w1v, w2v, pevT))):
            cx.load(W1[i][:], w1.rearrange("(l d) f -> d l f", d=128), (), ["W1_%d" % i], q="pool")
            cx.load(W2[i][:], w2, (), ["W2_%d" % i], q="pool")
            cx.load(PE_[i][:], pe, (), ["PE_%d" % i], q="pool")
            cx.load(src[i][:], FO[6 + i].rearrange("p (n r) -> p n r", r=16), (), ["src%d" % i], q="sp")
        kcmpT = sb("kcmpT", [128, 256], BF16)
        vcmp = sb("vcmp", [128, 2, 128], BF16)
        cx.memset(kcmpT[:, 255:256], 0.0, ["kcmpT"])
        cx.memset(vcmp[:], 0.0, ["vcmp"])
        sact = [sb("sk", [128, 256], BF16), sb("sv", [128, 256], BF16)]
        bias = [sb("bk", [128, 1], F32), sb("bv", [128, 1], F32)]
        PB, PK = cx.PB, cx.PBK
        for i in range(2):
            for l in range(32):
                cx.mm(PB[0][:, 0:1], W1[i][:, l, :], PE_[i][:, l:l + 1], l == 0, l == 31,
                      ["W1_%d" % i, "PE_%d" % i], [PK[0]])
            cx.copy(bias[i][:], PB[0][:, 0:1], [PK[0]], ["bias%d" % i])
            for l in range(32):
                n0, r = (0, l) if l < 16 else (1, l - 16)
                cx.mm(PB[1][:, 0:255], W1[i][:, l, :], src[i][:, n0:n0 + 255, r], l == 0, l == 31,
                      ["W1_%d" % i, "src%d" % i], [PK[1]])
            cx.actf(sact[i][:, 0:255], PB[1][:, 0:255], ACT.Silu, [PK[1], "bias%d" % i], ["sact%d" % i],
                    bias=bias[i][:])
        cx.mm(PB[2][:, 0:255], W2[0][:], sact[0][:, 0:255], True, True, ["W2_0", "sact0"], [PK[2]])
        cx.copy(kcmpT[:, 0:255], PB[2][:, 0:255], [PK[2]], ["kcmpT"])
        for i, M in ((0, 128), (1, 127)):
            cx.mm(PB[3][0:M, i * 128:(i + 1) * 128], sact[1][:, i * 128:i * 128 + M], W2[1][:], True, True,
                  ["W2_1", "sact1"], [PK[3]])
            cx.copy(vcmp[0:M, i, :], PB[3][0:M, i * 128:(i + 1) * 128], [PK[3]], ["vcmp"])
        cx.store(KCMP, kcmpT[:], ["kcmpT"], ["KCMP"], q="sp")
        cx.store(VCMP, vcmp[:], ["vcmp"], ["VCMP"], q="sp")
        qu = [sb("qu%d" % g, [128, S], BF16) for g in range(4)]
        for g in range(4):
            cx.load(qu[g][:], FO[g], (), ["qu%d" % g], q="sp")
        mfull = sb("mfull", [128, 512], F32)
        tkval = sb("tkval", [128, 128], F32)
        tkbias = sb("tkbias", [128, 128], F32)
        id32 = sb("id32", [128, 128], BF16)
        cx.load(mfull[:], CD["mfull"], (), ["mfull"], q="sp")
        cx.load(tkval[:], CD["tkval"], (), ["tkval"], q="sp")
        cx.load(tkbias[:], CD["tkbias"], (), ["tkbias"], q="sp")
        cx.load(id32[:], CD["ident32k"], (), ["id32"], q="pool")
        selbT = sb("selbT", [64, S], BF16)
        pairs = Rot("pp", [(0, 1), (2, 3)])
        pT = cx.banks([4, 5], "pT")
        Psum = cx.rot_sb("Psum", 2, [128, 64, 4], F32, ls)
        for t_ in Psum.t:
            cx.memset(t_[:], 0.0, [Psum.k[Psum.t.index(t_)]])
        sm = cx.rot_sb("sm", 8, [128, 255], F32, ls)
        ee = cx.rot_sb("ee", 8, [128, 255], F32, ls)
        sml = cx.rot_sb("sml", 40, [128, 1], F32, ls)
        imp = cx.rot_sb("imp", 2, [128, 64], F32, ls)
        sc = cx.rot_sb("sc", 2, [128, 64], F32, ls)
        sc2 = cx.rot_sb("sc2", 2, [128, 64], F32, ls)
        m8 = cx.rot_sb("m8", 4, [128, 8], F32, ls)
        selb = cx.rot_sb("selb", 2, [128, 64], BF16, ls)
        def tile_steps(tt):
            L = {}
            steps = []

            def s_mm():
                (ia, ib), _ = pairs.next()
                L["b"] = (ia, ib)
                for g in range(4):
                    bi = ia if g < 2 else ib
                    off = (g % 2) * 256
                    cx.mm(PB[bi][:, off:off + 255], qu[g][:, tt * 128:(tt + 1) * 128], kcmpT[:, 0:255], True, True,
                          ["qu%d" % g, "kcmpT"], [PK[bi]])
                L["P"] = Psum.next()
            steps.append(s_mm)

            def bank(g):
                ia, ib = L["b"]
                bi = ia if g < 2 else ib
                return PB[bi], PK[bi], (g % 2) * 256

            def a1(g):
                b_t, b_k, off = bank(g)
                s_t, s_k = sm.next()
                L["s", g] = (s_t, s_k)
                cx.tt(s_t[:], b_t[:, off:off + 255], mfull[:, 256 - 8 * tt: 256 - 8 * tt + 255], ALU.add,
                      [b_k, "mfull"], [s_k])

            def a2(g):
                s_t, s_k = L["s", g]
                mx, mx_k = sml.next()
                L["mx", g] = (mx, mx_k)
                cx.p.add("dve", lambda e, o=mx[:], i=s_t[:]: e.reduce_max(out=o, in_=i, axis=AX.X), [s_k], [mx_k])

            def a3(g):
                mx, mx_k = L["mx", g]
                nm, nm_k = sml.next()
                L["nm", g] = (nm, nm_k)
                cx.ts(nm[:], mx[:], -1000.0, ALU.max, [mx_k], [nm_k], s2=-1.0, op1=ALU.mult)

            def a4(g):
                s_t, s_k = L["s", g]
                nm, nm_k = L["nm", g]
                e_t, e_k = ee.next()
                dn, dn_k = sml.next()
                L["e", g] = (e_t, e_k)
                L["dn", g] = (dn, dn_k)
                cx.actf(e_t[:], s_t[:], ACT.Exp, [s_k, nm_k], [e_k, dn_k], bias=nm[:], accum_out=dn[:])

            def a5(g):
                dn, dn_k = L["dn", g]
                cx.ts(dn[:], dn[:], 1e-30, ALU.max, [dn_k], [dn_k])

            def a6(g):
                dn, dn_k = L["dn", g]
                rd, rd_k = sml.next()
                L["rd", g] = (rd, rd_k)
                cx.recip(rd[:], dn[:], [dn_k], [rd_k])

            def a7(g):
                P_t, P_k = L["P"]
                P2 = P_t[:].rearrange("p a b -> p (a b)")
                e_t, e_k = L["e", g]
                rd, rd_k = L["rd", g]
                if g == 0:
                    cx.ts(P2[:, 0:255], e_t[:], rd[:], ALU.mult, [e_k, rd_k], [P_k])
                else:
                    cx.stt(P2[:, 0:255], e_t[:], rd[:], P2[:, 0:255], ALU.mult, ALU.add, [e_k, rd_k, P_k], [P_k])

            for fn in (a1, a2, a3, a4, a5, a6, a7):
                for g in range(4):
                    steps.append(lambda fn=fn, g=g: fn(g))

            def t1():
                P_t, P_k = L["P"]
                i_t, i_k = imp.next()
                L["i"] = (i_t, i_k)
                cx.p.add("dve", lambda e, o=i_t[:], i=P_t[:]: e.tensor_reduce(out=o, in_=i, axis=AX.X, op=ALU.add),
                         [P_k], [i_k])

            def t2():
                P_t, P_k = L["P"]
                i_t, i_k = L["i"]
                cx.tt(i_t[:, 1:64], i_t[:, 1:64], P_t[:, 0:63, 3], ALU.add, [i_k, P_k], [i_k])

            def t3():
                i_t, i_k = L["i"]
                c_t, c_k = sc.next()
                L["c"] = (c_t, c_k)
                cx.tt(c_t[:], i_t[:], tkval[:, 64 - 2 * tt:128 - 2 * tt], ALU.mult, [i_k, "tkval"], [c_k])

            def t4():
                c_t, c_k = L["c"]
                cx.tt(c_t[:], c_t[:], tkbias[:, 64 - 2 * tt:128 - 2 * tt], ALU.add, [c_k, "tkbias"], [c_k])

            def t5():
                c_t, c_k = L["c"]
                cx.memset(c_t[:, 0:1], 3.0e4, [c_k])

            def t6():
                c_t, c_k = L["c"]
                m1, m1_k = m8.next()
                L["m1"] = (m1, m1_k)
                cx.p.add("dve", lambda e, o=m1[:], i=c_t[:]: e.max(out=o, in_=i), [c_k], [m1_k])

            def t7():
                c_t, c_k = L["c"]
                m1, m1_k = L["m1"]
                d_t, d_k = sc2.next()
                L["d"] = (d_t, d_k)
                cx.p.add("dve", lambda e, o=d_t[:], r=m1[:], v=c_t[:]: e.match_replace(
                    out=o, in_to_replace=r, in_values=v, imm_value=-1.0e9), [c_k, m1_k], [d_k])

            def t8():
                d_t, d_k = L["d"]
                m2, m2_k = m8.next()
                L["m2"] = (m2, m2_k)
                cx.p.add("dve", lambda e, o=m2[:], i=d_t[:]: e.max(out=o, in_=i), [d_k], [m2_k])

            def t9():
                c_t, c_k = L["c"]
                m2, m2_k = L["m2"]
                sb_t, sb_k = selb.next()
                L["sb"] = (sb_t, sb_k)
                cx.ts(sb_t[:], c_t[:], m2[:, 7:8], ALU.is_ge, [c_k, m2_k], [sb_k], s2=1.0, op1=ALU.subtract)

            def t10():
                sb_t, sb_k = L["sb"]
                t_t, t_k = pT.next()
                cx.mm(t_t[0:64, 0:128], sb_t[:], id32[:], True, True, [sb_k, "id32"], [t_k])
                cx.copy(selbT[:, tt * 128:(tt + 1) * 128], t_t[0:64, 0:128], [t_k], ["selbT"])

            steps.extend([t1, t2, t3, t4, t5, t6, t7, t8, t9, t10])
            return steps

        for tt in range(0, NT, 2):
            sa, sb_ = tile_steps(tt), tile_steps(tt + 1)
            for x, y in zip(sa, sb_):
                x()
                y()
        cx.store(SELB, selbT[:], ["selbT"], ["SELB"], q="sp")
        cx.p.emit()


CONST_A = ("ropec", "ropes", "pswap", "ident32k", "bcaus", "mstrict", "bfar", "bcmp", "ebig", "negtri",
           "mfull", "tkval", "tkbias", "ident")


def build_A(level=9):
    nc = bass.Bass("TRN2", target_bir_lowering=False)
    with ExitStack() as st:
        cx = Ctx(nc, st)
        C = consts()
        xT = cx.dram("xT", [D, S], F32, "ExternalInput")
        wF = cx.dram("wF", [D, 16 * 128], F32, "ExternalInput")
        wT = cx.dram("wT", [D, 512], F32, "ExternalInput")
        wG = cx.dram("wG", [D, 6], F32, "ExternalInput")
        w1k = cx.dram("w1k", [4096, 128], F32, "ExternalInput")
        w1v = cx.dram("w1v", [4096, 128], F32, "ExternalInput")
        w2k = cx.dram("w2k", [128, 128], F32, "ExternalInput")
        w2v = cx.dram("w2v", [128, 128], F32, "ExternalInput")
        pekT = cx.dram("pekT", [128, 32], F32, "ExternalInput")
        pevT = cx.dram("pevT", [128, 32], F32, "ExternalInput")
        CD = {n: cx.dram("c_" + n, list(C[n].shape), F32, "ExternalInput") for n in CONST_A}
        dbg = "ExternalOutput" if level < 9 else None
        FO = cx.dram("FO", [NFO0, 128, S], BF16, dbg)
        TO = cx.dram("TO", [S, 512], BF16, dbg)
        GO = cx.dram("GO", [8, S], F32, dbg)
        KCMP = cx.dram("KCMP", [128, 256], BF16, dbg)
        VCMP = cx.dram("VCMP", [128, 2, 128], BF16, dbg)
        SELB = cx.dram("SELB", [64, S], BF16, dbg)
        OUT = cx.dram("OUT", [4, 512, 1024], BF16, "ExternalOutput")
        phase_proj(cx, xT, wF, wT, wG, CD["ropec"], CD["ropes"], CD["pswap"], FO, TO, GO, SPEC0, 6)
        if level >= 2:
            phase_nsa_prep(cx, FO, w1k, w1v, w2k, w2v, pekT, pevT, CD, KCMP, VCMP, SELB)
        if level >= 3:
            phase_nsa_attn(cx, FO, TO, GO, CD, KCMP, VCMP, SELB, OUT)
        if level >= 4:
            phase_sb_attn(cx, FO, TO, CD, OUT)
    return nc


def inputs_A(x, ev, c):
    b, g = c // 4, c % 4
    C = consts()
    wF, wT, wG = l0_weights(ev["w_in"], g)
    m = {"xT": np.ascontiguousarray(x[b].T), "wF": wF, "wT": wT, "wG": wG,
         "w1k": ev["w1_k"], "w1v": ev["w1_v"], "w2k": ev["w2_k"], "w2v": ev["w2_v"],
         "pekT": np.ascontiguousarray(ev["pe_k"].T), "pevT": np.ascontiguousarray(ev["pe_v"].T)}
    for n in CONST_A:
        m["c_" + n] = C[n]
    return m


def run_attn_stream(cx, groups, zrot, erot, esrot, ident, onesb, hlrot, LA=1, defer=2):
    flat = []
    for gi, g in enumerate(groups):
        g["gid"] = gi
        n = len(g["tiles"])
        for i, t in enumerate(g["tiles"]):
            flat.append((g, i, n, t))
    zs = {}

    def emit_z(idx):
        g, i, n, t = flat[idx]
        z_t, z_k = zrot.next()
        sel = t.get("sel")
        mb = t.get("maskb")
        extra = (sel is not None) + (mb is not None)
        cx.mm(z_t[:], t["kT"], g["q_ap"], True, extra == 0, list(t["kkeys"]) + list(g["q_keys"]), [z_k])
        if sel is not None:
            extra -= 1
            cx.mm(z_t[:], sel[0], sel[1], False, extra == 0, list(sel[2]), [z_k])
        if mb is not None:
            cx.mm(z_t[:], ident, mb, False, True, ["ident"] + list(t["mkeys"]), [z_k])
        zs[idx] = (z_t, z_k)

    pending = []

    def run_chain(f):
        r = f()
        while r is not None:
            r = r[0]() if isinstance(r, tuple) else r()

    def flush(resource, exclude):
        keep = []
        todo = []
        for p in pending:
            (todo if (resource in p[2] and p[3] != exclude) else keep).append(p)
        pending[:] = keep
        for p in todo:
            run_chain(p[1])

    def make_chain(g):
        ep = g["epilogue"]

        def stage_c():
            r = ep()
            if callable(r):
                r = r()
            return r
        return stage_c

    for idx in range(min(LA, len(flat))):
        emit_z(idx)
    for idx in range(len(flat)):
        if idx + LA < len(flat):
            emit_z(idx + LA)
        g, i, n, t = flat[idx]
        if i == 0:
            flush(g["oaccs"][0][1], g["gid"])
            flush(g["den"][1], g["gid"])
        z_t, z_k = zs.pop(idx)
        e_t, e_k = erot.next()
        cx.actf(e_t[:], z_t[:], ACT.Exp, [z_k], [e_k])
        for (o_ap, o_k), v in zip(g["oaccs"], t["vs"]):
            cx.mm(o_ap, v, e_t[:], i == 0, i == n - 1, [e_k] + list(t["vkeys"]), [o_k])
        cx.mm(g["den"][0], onesb, e_t[:], i == 0, i == n - 1, [e_k, "onesb"], [g["den"][1]])
        for p in pending:
            p[0] -= 1
        ready = [p for p in pending if p[0] <= 0]
        pending[:] = [p for p in pending if p[0] > 0]
        for p in ready:
            r = p[1]()
            if isinstance(r, tuple):
                pending.append([r[1], r[0], p[2], p[3]])
            elif callable(r):
                pending.append([defer, r, p[2], p[3]])
        if i == n - 1:
            pending.append([defer, make_chain(g), {g["den"][1], g["oaccs"][0][1]}, g["gid"]])
    while pending:
        p = pending.pop(0)
        run_chain(p[1])


def chunked_loads(cx, items, q="sp"):
    for c in range(NCH):
        for t_, d_, name, kind in items:
            if kind == "F":
                cx.load(t_[:, c * 512:(c + 1) * 512], d_[:, c * 512:(c + 1) * 512], (), ["%s_%d" % (name, c)], q=q)
            else:
                cx.load(t_[:, 4 * c:4 * c + 4, :], d_[:, 4 * c:4 * c + 4, :], (), ["%s_%d" % (name, c)], q=q)


def phase_nsa_attn(cx, FO, TO, GO, CD, KCMP, VCMP, SELB, OUT):
    with ExitStack() as ls:
        sb = lambda n, s, d: cx.sb(n, s, d, ls)
        PB, PK = cx.PB, cx.PBK
        qu = [sb("qu%d" % h, [128, S], BF16) for h in range(2)]
        qr = [sb("qr%d" % h, [128, S], BF16) for h in range(2)]
        gas = [sb("gas%d" % h, [128, S], BF16) for h in range(2)]
        kcmpT = sb("kcmpT", [128, 256], BF16)
        vcmp = sb("vcmp", [128, 2, 128], BF16)
        selbT = sb("selbT", [64, S], BF16)
        ebig = sb("ebig", [64, S], BF16)
        cx.load(kcmpT[:], KCMP, (), ["kcmpT"])
        cx.load(vcmp[:], VCMP, (), ["vcmp"])
        mcaus = sb("mcaus", [128, 4, 512], BF16)
        mfar = sb("mfar", [128, 4, 512], BF16)
        mcmp = sb("mcmp", [128, 5, 512], BF16)
        identb = sb("identb", [128, 128], BF16)
        cx.load(identb[:], CD["ident"], (), ["ident"], q="pool")
        cx.load(mcmp[:], CD["bcmp"], (), ["mcmp"], q="pool")
        cx.load(mcaus[:], CD["bcaus"], (), ["mcaus"], q="pool")
        cx.load(ebig[:], CD["ebig"], (), ["ebig"], q="pool")
        cx.load(mfar[:], CD["bfar"], (), ["mfar"], q="pool")
        ksT = sb("ksT", [128, S], BF16)
        kwT = sb("kwT", [128, S], BF16)
        vs = sb("vs", [128, NT, 128], BF16)
        vw = sb("vw", [128, NT, 128], BF16)
        TOv = TO.rearrange("(k p) c -> p k c", p=128)
        chunked_loads(cx, [(qu[0], FO[0], "qu0", "F"), (qr[0], FO[4], "qr0", "F"), (selbT, SELB, "selbT", "F"),
                           (ksT, FO[8], "ksT", "F"), (vs, TOv[:, :, 0:128], "vs", "T"),
                           (kwT, FO[9], "kwT", "F"), (vw, TOv[:, :, 128:256], "vw", "T"),
                           (qu[1], FO[1], "qu1", "F"), (qr[1], FO[5], "qr1", "F"),
                           (gas[0], FO[10], "gas0", "F"), (gas[1], FO[11], "gas1", "F")])
        zrot = cx.banks([0, 1, 2], "z")
        sets = Rot("sets", [(5, 3), (6, 4), (7, 3), (5, 4), (6, 3), (7, 4)])
        erot = cx.rot_sb("e", 5, [128, 512], BF16, ls)
        esrot = cx.rot_sb("es", 4, [128, 512], F32, ls)
        hlrot = cx.rot_sb("hl", 6, [128, 512], BF16, ls)
        onesb = sb("onesb", [128, 128], BF16)
        cx.memset(onesb[:], 1.0, ["onesb"])
        gb = cx.rot_sb("gb", 4, [128, 512], F32, ls)
        dnc = cx.rot_sb("dnc", 3, [128, 512], F32, ls)
        tmp = cx.rot_sb("tmp", 2, [128, 512], F32, ls)
        osum = cx.rot_sb("osum", 2, [128, 512], F32, ls)
        ost = cx.rot_sb("ost", 2, [128, 512], BF16, ls)
        groups = []
        state = {}
        for h in range(2):
            for c in range(NCH):
                tsl = slice(c * 512, (c + 1) * 512)
                for br in range(3):
                    tiles = []
                    if br == 0:
                        q_ap, q_keys = qu[h][:, tsl], ["qu%d_%d" % (h, c)]
                        nts = [0] + ([1] if c >= 4 else [])
                        for nt in nts:
                            mk = None
                            if nt == 0 and c <= 4:
                                mk = mcmp[:, c, :]
                            if nt == 1:
                                mk = mcmp[:, c - 4, :]
                            tiles.append(dict(kT=kcmpT[:, nt * 128:(nt + 1) * 128], kkeys=["kcmpT"],
                                              vs=[vcmp[:, nt, :]], vkeys=["vcmp"], maskb=mk, mkeys=["mcmp"]))
                    elif br == 1:
                        q_ap, q_keys = qr[h][:, tsl], ["qr%d_%d" % (h, c)]
                        for kt in range(4 * c + 4):
                            mk = mcaus[:, kt - 4 * c, :] if kt >= 4 * c else None
                            tiles.append(dict(kT=ksT[:, kt * 128:(kt + 1) * 128], kkeys=["ksT_%d" % (kt // 4)],
                                              vs=[vs[:, kt, :]], vkeys=["vs_%d" % (kt // 4)], maskb=mk, mkeys=["mcaus"],
                                              sel=(ebig[:, kt * 128:(kt + 1) * 128], selbT[:, tsl],
                                                   ["ebig", "selbT_%d" % c])))
                    else:
                        q_ap, q_keys = qr[h][:, tsl], ["qr%d_%d" % (h, c)]
                        for kt in range(max(0, 4 * c - 4), 4 * c + 4):
                            if kt >= 4 * c:
                                mk, mkk = mcaus[:, kt - 4 * c, :], ["mcaus"]
                            else:
                                mk, mkk = mfar[:, kt - (4 * c - 4), :], ["mfar"]
                            tiles.append(dict(kT=kwT[:, kt * 128:(kt + 1) * 128], kkeys=["kwT_%d" % (kt // 4)],
                                              vs=[vw[:, kt, :]], vkeys=["vw_%d" % (kt // 4)], maskb=mk, mkeys=mkk))
                    (io, idn), _ = sets.next()

                    def epi(h=h, c=c, br=br, io=io, idn=idn, tsl=tsl):
                        def deferred():
                            if br == 0:
                                state["os"] = osum.next()
                            os_t, os_k = state["os"]
                            g_t, g_k = gb.next()
                            cx.load(g_t[:], GO[3 * h + br:3 * h + br + 1, tsl].partition_broadcast(128), (), [g_k])
                            d_t, d_k = dnc.next()
                            cx.ts(d_t[:], PB[idn][:], 1e-30, ALU.max, [PK[idn]], [d_k])
                            cx.actf(d_t[:], d_t[:], ACT.Ln, [d_k], [d_k])
                            cx.actf(d_t[:], d_t[:], ACT.Exp, [d_k], [d_k], scale=-1.0)
                            cx.tt(d_t[:], d_t[:], g_t[:], ALU.mult, [d_k, g_k], [d_k])
                            if br == 0:
                                cx.tt(os_t[:], PB[io][:], d_t[:], ALU.mult, [PK[io], d_k], [os_k])
                            else:
                                t_t, t_k = tmp.next()
                                cx.tt(t_t[:], PB[io][:], d_t[:], ALU.mult, [PK[io], d_k], [t_k])
                                cx.tt(os_t[:], os_t[:], t_t[:], ALU.add, [os_k, t_k], [os_k])
                            if br == 2:
                                o_t, o_k = ost.next()
                                cx.tt(o_t[:], os_t[:], gas[h][:, tsl], ALU.mult, [os_k, "gas%d_%d" % (h, c)], [o_k])
                                cx.store(OUT[c // 2, h * 128:(h + 1) * 128, (c % 2) * 512:(c % 2) * 512 + 512],
                                         o_t[:], [o_k], [("OUT", h, c)], q="sp")
                        return deferred

                    groups.append(dict(q_ap=q_ap, q_keys=q_keys, tiles=tiles, oaccs=[(PB[io][:], PK[io])],
                                       den=(PB[idn][:], PK[idn]), epilogue=epi))
        run_attn_stream(cx, groups, zrot, erot, esrot, identb[:], onesb[:], hlrot, LA=2, defer=2)
        cx.p.emit()


def phase_sb_attn(cx, FO, TO, CD, OUT, on_quarter=None, prefetch=None):
    with ExitStack() as ls:
        sb = lambda n, s, d: cx.sb(n, s, d, ls)
        PB, PK = cx.PB, cx.PBK
        TOv = TO.rearrange("(k p) c -> p k c", p=128)
        q = [sb("q%d" % h, [128, S], BF16) for h in range(2)]
        k = [sb("k%d" % h, [128, S], BF16) for h in range(2)]
        gbs = [sb("gbs%d" % h, [128, S], BF16) for h in range(2)]
        v = [sb("v%d" % h, [128, NT, 128], BF16) for h in range(2)]
        mstr = sb("mstr", [128, 4, 512], BF16)
        negtri = sb("negtri", [128, 128], BF16)
        cx.load(mstr[:], CD["mstrict"], (), ["mstr"], q="pool")
        cx.load(negtri[:], CD["negtri"], (), ["negtri"], q="pool")
        items = []
        for h in range(2):
            items += [(q[h], FO[12 + h], "q%d" % h, "F"), (k[h], FO[14 + h], "k%d" % h, "F"),
                      (v[h], TOv[:, :, 256 + 128 * h:384 + 128 * h], "v%d" % h, "T"),
                      (gbs[h], FO[16 + h], "gbs%d" % h, "F")]
        chunked_loads(cx, items)
        negones = sb("negones", [128, 128], BF16)
        cx.memset(negones[:], -1.0, ["negones"])
        zA = cx.banks([0, 1, 2, 3], "zA")
        oac = cx.banks([4, 5], "oac")
        csb = cx.banks([6, 7], "cs")
        Et = cx.rot_sb("Et", 2, [128, 512], F32, ls)
        spt = cx.rot_sb("spt", 4, [128, 512], BF16, ls)
        at = cx.rot_sb("at", 4, [128, 512], BF16, ls)
        Racc = cx.rot_sb("Racc", 3, [128, 512], F32, ls)
        zsb = cx.rot_sb("zsb", 2, [128, 512], F32, ls)
        ost = cx.rot_sb("ost", 2, [128, 512], BF16, ls)
        flat = []
        for c in range(NCH):
            for h in range(2):
                kts = list(range(4 * c + 3, -1, -1))
                for i, kt in enumerate(kts):
                    flat.append(dict(h=h, c=c, kt=kt, first=(i == 0), last=(kt == 0)))
        if prefetch is not None:
            prefetch()
        N = len(flat)
        st_ = [dict() for _ in range(N)]
        cur = {"R": None, "o": None}

        def stage1(s):
            t = flat[s]
            h, c, kt = t["h"], t["c"], t["kt"]
            tsl = slice(c * 512, (c + 1) * 512)
            ksl = slice(kt * 128, (kt + 1) * 128)
            a_t, a_k = zA.next()
            cx.mm(a_t[:], k[h][:, ksl], q[h][:, tsl], True, True, ["k%d_%d" % (h, kt // 4), "q%d_%d" % (h, c)], [a_k])
            E_t, E_k = Et.next()
            cx.actf(E_t[:], a_t[:], ACT.Exp, [a_k], [E_k])
            s_t, s_k = spt.next()
            cx.actf(s_t[:], E_t[:], ACT.Ln, [E_k], [s_k], bias=1.0)
            if kt >= 4 * c:
                cx.tt(s_t[:], s_t[:], mstr[:, kt - 4 * c, :], ALU.mult, [s_k, "mstr"], [s_k])
            st_[s]["sp"] = (s_t, s_k)
            st_[s]["zA"] = (a_t, a_k)

        def stage2(s):
            t = flat[s]
            h, c, kt = t["h"], t["c"], t["kt"]
            tsl = slice(c * 512, (c + 1) * 512)
            ksl = slice(kt * 128, (kt + 1) * 128)
            s_t, s_k = st_[s]["sp"]
            if t["first"]:
                cur["Rb"] = csb.next()
                cur["R"] = None
            Rb_t, Rb_k = cur["Rb"]
            R = cur["R"]
            b_t, b_k = st_[s]["zA"]
            cx.mm(b_t[:], negtri[:], s_t[:], False, True, ["negtri", s_k], [b_k])
            p_t, p_k = at.next()
            if R is None:
                cx.actf(p_t[:], b_t[:], ACT.Exp, [b_k], [p_k])
            else:
                zs_t, zs_k = zsb.next()
                cx.tt(zs_t[:], b_t[:], R[0][:], ALU.add, [b_k, R[1]], [zs_k])
                cx.actf(p_t[:], zs_t[:], ACT.Exp, [zs_k], [p_k])
            if kt >= 4 * c:
                cx.tt(p_t[:], p_t[:], mstr[:, kt - 4 * c, :], ALU.mult, [p_k, "mstr"], [p_k])
            st_[s]["a"] = (p_t, p_k)
            if not t["last"]:
                cx.mm(Rb_t[:], negones[:], s_t[:], t["first"], True, ["negones", s_k], [Rb_k])
                Rn_t, Rn_k = Racc.next()
                cx.copy(Rn_t[:], Rb_t[:], [Rb_k], [Rn_k])
                cur["R"] = (Rn_t, Rn_k)

        def stage3(s):
            t = flat[s]
            h, c, kt = t["h"], t["c"], t["kt"]
            tsl = slice(c * 512, (c + 1) * 512)
            if t["first"]:
                cur["o"] = oac.next()
            o_t, o_k = cur["o"]
            p_t, p_k = st_[s]["a"]
            cx.mm(o_t[:], v[h][:, kt, :], p_t[:], t["first"], t["last"], ["v%d_%d" % (h, kt // 4), p_k], [o_k])
            if t["last"]:
                s_o, s_ok = ost.next()
                cx.tt(s_o[:], o_t[:], gbs[h][:, tsl], ALU.mult, [o_k, "gbs%d_%d" % (h, c)], [s_ok])
                cx.store(OUT[c // 2, (2 + h) * 128:(3 + h) * 128, (c % 2) * 512:(c % 2) * 512 + 512], s_o[:], [s_ok],
                         [("OUT", 2 + h, c)], q="pool")
                if on_quarter is not None and h == 1 and c % 2 == 1:
                    on_quarter(c // 2, [("OUT", 2 + hh, cc_) for hh in range(2) for cc_ in (c - 1, c)])

        for s in range(N + 2):
            if s < N:
                stage1(s)
            if 0 <= s - 1 < N:
                stage2(s - 1)
            if 0 <= s - 2 < N:
                stage3(s - 2)
        cx.p.emit()


def phase_outproj_ln(cx, oT, xrows, wout, lng, lnb, XO, ntok):
    with ExitStack() as ls:
        sb = lambda n, s, d: cx.sb(n, s, d, ls)
        PB, PK = cx.PB, cx.PBK
        Wo = sb("Wo", [128, 16, D], BF16)
        wv = wout.rearrange("(k p) c -> p k c", p=128)
        for k in range(16):
            cx.load(Wo[:, k, :], wv[:, k, :], (), ["Wo%d" % k], q="pool")
        oTs = sb("oTs", [128, 16, ntok], BF16)
        ov = oT.rearrange("(k p) t -> p k t", p=128)
        for k4 in range(4):
            cx.load(oTs[:, 4 * k4:4 * k4 + 4, :], ov[:, 4 * k4:4 * k4 + 4, :], (), ["oTs%d" % k4], q="sp")
        gbc = sb("gbc", [128, D], F32)
        bbc = sb("bbc", [128, D], F32)
        cx.load(gbc[:], lng.partition_broadcast(128), (), ["gbc"], q="sp")
        cx.load(bbc[:], lnb.partition_broadcast(128), (), ["bbc"], q="sp")
        xs = cx.rot_sb("xs", 2, [128, D], F32, ls)
        zs = cx.rot_sb("zs", 2, [128, D], F32, ls)
        st6 = cx.rot_sb("st6", 2, [128, 4, 6], F32, ls)
        mv = cx.rot_sb("mv", 2, [128, 2], F32, ls)
        sml = cx.rot_sb("sml", 6, [128, 1], F32, ls)
        banks = cx.banks(list(range(8)), "y")
        for tt in range(ntok // 128):
            x_t, x_k = xs.next()
            cx.load(x_t[:], xrows[tt * 128:(tt + 1) * 128, :], (), [x_k], q="sp")
            z_t, z_k = zs.next()
            s_t, s_k = st6.next()
            for cg in range(4):
                b_t, b_k = banks.next()
                for k in range(16):
                    cx.mm(b_t[:], oTs[:, k, tt * 128:(tt + 1) * 128], Wo[:, k, cg * 512:(cg + 1) * 512],
                          k == 0, k == 15, ["oTs%d" % (k // 4), "Wo%d" % k], [b_k])
                zc = z_t[:, cg * 512:(cg + 1) * 512]
                cx.stt(zc, x_t[:, cg * 512:(cg + 1) * 512], ALPHA, b_t[:], ALU.mult, ALU.add, [x_k, b_k],
                       [z_k + "_%d" % cg])
                cx.p.add("dve", lambda e, o=s_t[:, cg, :], i=zc: e.bn_stats(out=o, in_=i), [z_k + "_%d" % cg],
                         [s_k + "_%d" % cg])
            m_t, m_k = mv.next()
            cx.p.add("dve", lambda e, o=m_t[:], i=s_t[:]: e.bn_aggr(out=o, in_=i),
                     [s_k + "_%d" % cg for cg in range(4)], [m_k])
            sd, sd_k = sml.next()
            cx.ts(sd[:], m_t[:, 1:2], 1e-5, ALU.add, [m_k], [sd_k])
            cx.actf(sd[:], sd[:], ACT.Sqrt, [sd_k], [sd_k])
            rs, rs_k = sml.next()
            cx.recip(rs[:], sd[:], [sd_k], [rs_k])
            nb, nb_k = sml.next()
            cx.stt(nb[:], m_t[:, 0:1], -1.0, rs[:], ALU.mult, ALU.mult, [m_k, rs_k], [nb_k])
            zk = [z_k + "_%d" % cg for cg in range(4)]
            cx.ts(z_t[:], z_t[:], rs[:], ALU.mult, zk + [rs_k, nb_k], zk, s2=nb[:], op1=ALU.add)
            cx.tt(z_t[:], z_t[:], gbc[:], ALU.mult, zk + ["gbc"], zk)
            cx.tt(z_t[:], z_t[:], bbc[:], ALU.add, zk + ["bbc"], zk)
            cx.store(XO[tt * 128:(tt + 1) * 128, :], z_t[:], zk, [("XO", tt)], q="sp")
        cx.p.emit()


def build_B(ntok=1024):
    nc = bass.Bass("TRN2", target_bir_lowering=False)
    with ExitStack() as st:
        cx = Ctx(nc, st)
        oT = cx.dram("oT", [D, ntok], BF16, "ExternalInput")
        xr = cx.dram("xr", [ntok, D], F32, "ExternalInput")
        wout = cx.dram("wout", [D, D], F32, "ExternalInput")
        lng = cx.dram("lng", [1, D], F32, "ExternalInput")
        lnb = cx.dram("lnb", [1, D], F32, "ExternalInput")
        XO = cx.dram("XO", [ntok, D], F32, "ExternalOutput")
        phase_outproj_ln(cx, oT, xr, wout, lng, lnb, XO, ntok)
    return nc


SPEC1 = ([("rope_scale", i, None) for i in range(4)] + [("rope", 4 + i, None) for i in range(4)]
         + [("silu", 8 + i, None) for i in range(4)])
NFO1 = 12


def l1_weights(w_in, g):
    cols = []
    for base in (0, 2048):
        for h in (2 * g, 2 * g + 1):
            for c in range(2):
                o = base + (2 * h + c) * 128
                cols.append(w_in[:, o:o + 128])
    for h in (2 * g, 2 * g + 1):
        for j in range(2):
            o = 6144 + h * 256 + j * 128
            cols.append(w_in[:, o:o + 128])
    wF = np.ascontiguousarray(np.concatenate(cols, 1))
    wT = np.ascontiguousarray(w_in[:, 4096 + 2 * g * 256: 4096 + (2 * g + 2) * 256])
    return wF, wT


def phase_diff_attn(cx, FO, TO, CD, lq1, lk1, lq2, lk2, gng, OUT, on_quarter=None, prefetch=None):
    with ExitStack() as ls:
        sb = lambda n, s, d: cx.sb(n, s, d, ls)
        PB, PK = cx.PB, cx.PBK
        TOv = TO.rearrange("(k p) c -> p k c", p=128)
        q = [sb("q%d" % i, [128, S], BF16) for i in range(4)]
        k = [sb("k%d" % i, [128, S], BF16) for i in range(4)]
        gt = [sb("gt%d" % i, [128, S], BF16) for i in range(4)]
        v = [sb("v%d" % h, [128, NT, 256], BF16) for h in range(2)]
        mcaus = sb("mcaus", [128, 4, 512], BF16)
        cx.load(mcaus[:], CD["bcaus"], (), ["mcaus"], q="pool")
        identb = sb("identb", [128, 128], BF16)
        cx.load(identb[:], CD["ident"], (), ["ident"], q="pool")
        items = []
        for h in range(2):
            for cc in range(2):
                i = 2 * h + cc
                items += [(q[i], FO[i], "q%d" % i, "F"), (k[i], FO[4 + i], "k%d" % i, "F")]
            items.append((v[h], TOv[:, :, 256 * h:256 * (h + 1)], "v%d" % h, "T"))
        for i in range(4):
            items.append((gt[i], FO[8 + i], "gt%d" % i, "F"))
        chunked_loads(cx, items)
        ones = sb("ones", [128, 128], BF16)
        cx.memset(ones[:], 1.0, ["ones", "onesb"])
        lam = sb("lam", [128, 1], F32)
        ex = [sb("ex%d" % i, [128, 1], F32) for i in range(2)]
        for i, (a, b_) in enumerate(((lq1, lk1), (lq2, lk2))):
            la = sb("la%d" % i, [128, 128], F32)
            lb = sb("lb%d" % i, [128, 128], F32)
            cx.load(la[:], a.partition_broadcast(128), (), ["la%d" % i])
            cx.load(lb[:], b_.partition_broadcast(128), (), ["lb%d" % i])
            cx.tt(la[:], la[:], lb[:], ALU.mult, ["la%d" % i, "lb%d" % i], ["la%d" % i])
            cx.p.add("dve", lambda e, o=ex[i][:], i_=la[:]: e.reduce_sum(out=o, in_=i_, axis=AX.X),
                     ["la%d" % i], ["ex%d" % i])
            cx.actf(ex[i][:], ex[i][:], ACT.Exp, ["ex%d" % i], ["ex%d" % i])
        cx.ts(lam[:], ex[0][:], ex[1][:], ALU.subtract, ["ex0", "ex1"], ["lam"], s2=-LAMBDA_INIT, op1=ALU.subtract)
        cx.ts(lam[:], lam[:], -1.0, ALU.mult, ["lam"], ["lam"])
        gsc = sb("gsc", [128, 4], F32)
        cx.load(gsc[:], gng, (), ["gsc"])
        cx.ts(gsc[:], gsc[:], 1.0 - LAMBDA_INIT, ALU.mult, ["gsc"], ["gsc"])
        zrot = cx.banks([0, 1], "z")
        sets = Rot("sets", [(4, 5, 2), (6, 7, 3)])
        erot = cx.rot_sb("e", 4, [128, 512], BF16, ls)
        esrot = cx.rot_sb("es", 3, [128, 512], F32, ls)
        hlrot = cx.rot_sb("hl", 6, [128, 512], BF16, ls)
        rd = cx.rot_sb("rd", 3, [128, 512], F32, ls)
        oA = cx.rot_sb("oA", 2, [128, 512], F32, ls)
        oB = cx.rot_sb("oB", 2, [128, 512], F32, ls)
        tm = cx.rot_sb("tm", 2, [128, 512], F32, ls)
        sq = cx.rot_sb("sq", 2, [128, 512], BF16, ls)
        ost = cx.rot_sb("ost", 2, [128, 512], BF16, ls)
        groups = []
        state = {}
        if prefetch is not None:
            prefetch()
        for c in range(NCH):
            for h in range(2):
                tsl = slice(c * 512, (c + 1) * 512)
                for cc in range(2):
                    qi = 2 * h + cc
                    tiles = []
                    for kt in range(4 * c + 4):
                        mk = mcaus[:, kt - 4 * c, :] if kt >= 4 * c else None
                        tiles.append(dict(kT=k[qi][:, kt * 128:(kt + 1) * 128], kkeys=["k%d_%d" % (qi, kt // 4)],
                                          vs=[v[h][:, kt, 0:128], v[h][:, kt, 128:256]],
                                          vkeys=["v%d_%d" % (h, kt // 4)],
                                          maskb=mk, mkeys=["mcaus"]))
                    (ia, ib, idn), _ = sets.next()

                    def epi(h=h, c=c, cc=cc, ia=ia, ib=ib, idn=idn, tsl=tsl):
                        def part1():
                            if cc == 0:
                                state["oh"] = [oA.next(), oB.next()]
                            oh = state["oh"]
                            r_t, r_k = rd.next()
                            cx.actf(r_t[:], PB[idn][:], ACT.Ln, [PK[idn]], [r_k])
                            cx.actf(r_t[:], r_t[:], ACT.Exp, [r_k], [r_k], scale=-1.0)
                            for j, ib_ in enumerate((ia, ib)):
                                o_t, o_k = oh[j]
                                if cc == 0:
                                    cx.tt(o_t[:], PB[ib_][:], r_t[:], ALU.mult, [PK[ib_], r_k], [o_k])
                                else:
                                    t_t, t_k = tm.next()
                                    cx.tt(t_t[:], PB[ib_][:], r_t[:], ALU.mult, [PK[ib_], r_k], [t_k])
                                    cx.stt(o_t[:], t_t[:], lam[:], o_t[:], ALU.mult, ALU.add, [t_k, "lam", o_k],
                                           [o_k])
                            return oh
                        if cc == 0:
                            def d0():
                                part1()
                                return None
                            return d0

                        def d1():
                            oh = part1()
                            return (lambda: part2(oh, h, c, tsl, idn), 8)
                        return d1

                    def part2(oh, h, c, tsl, idn):
                        sqs = []
                        for j in range(2):
                            s_t, s_k = sq.next()
                            cx.tt(s_t[:], oh[j][0][:], oh[j][0][:], ALU.mult, [oh[j][1]], [s_k])
                            sqs.append((s_t, s_k))
                        if True:
                            z_t, z_k = PB[idn], PK[idn]
                            for j in range(2):
                                cx.mm(z_t[:], ones[:], sqs[j][0][:], j == 0, j == 1, ["ones", sqs[j][1]], [z_k])
                            r2, r2_k = rd.next()
                            cx.ts(r2[:], z_t[:], 1.0 / 256.0, ALU.mult, [z_k], [r2_k], s2=1e-5, op1=ALU.add)
                            cx.actf(r2[:], r2[:], ACT.Ln, [r2_k], [r2_k])
                            cx.actf(r2[:], r2[:], ACT.Exp, [r2_k], [r2_k], scale=-0.5)
                            for j in range(2):
                                o_t, o_k = oh[j]
                                cx.stt(o_t[:], o_t[:], gsc[:, 2 * h + j:2 * h + j + 1], r2[:], ALU.mult, ALU.mult,
                                       [o_k, "gsc", r2_k], [o_k])
                                s_o, s_ok = ost.next()
                                cx.tt(s_o[:], o_t[:], gt[2 * h + j][:, tsl], ALU.mult, [o_k, "gt%d_%d" % (2 * h + j, c)],
                                      [s_ok])
                                cx.store(OUT[c // 2, (2 * h + j) * 128:(2 * h + j + 1) * 128,
                                             (c % 2) * 512:(c % 2) * 512 + 512],
                                         s_o[:], [s_ok], [("OUT", 2 * h + j, c)], q="sp")
                        if on_quarter is not None and h == 1 and c % 2 == 1:
                            on_quarter(c // 2, [("OUT", rb, cc_) for rb in range(4) for cc_ in (c - 1, c)])

                    groups.append(dict(q_ap=q[qi][:, tsl], q_keys=["q%d_%d" % (qi, c)], tiles=tiles,
                                       oaccs=[(PB[ia][:], PK[ia]), (PB[ib][:], PK[ib])],
                                       den=(PB[idn][:], PK[idn]), epilogue=epi))
        run_attn_stream(cx, groups, zrot, erot, esrot, identb[:], ones[:], hlrot, LA=1, defer=2)
        cx.p.emit()


def build_C(level=9):
    nc = bass.Bass("TRN2", target_bir_lowering=False)
    with ExitStack() as st:
        cx = Ctx(nc, st)
        C = consts()
        xT = cx.dram("xT", [D, S], F32, "ExternalInput")
        wF = cx.dram("wF", [D, 12 * 128], F32, "ExternalInput")
        wT = cx.dram("wT", [D, 512], F32, "ExternalInput")
        lq1 = cx.dram("lq1", [1, 128], F32, "ExternalInput")
        lk1 = cx.dram("lk1", [1, 128], F32, "ExternalInput")
        lq2 = cx.dram("lq2", [1, 128], F32, "ExternalInput")
        lk2 = cx.dram("lk2", [1, 128], F32, "ExternalInput")
        gng = cx.dram("gng", [128, 4], F32, "ExternalInput")
        CD = {n: cx.dram("c_" + n, list(C[n].shape), F32, "ExternalInput")
              for n in ("ropec", "ropes", "pswap", "bcaus", "ident")}
        dbg = "ExternalOutput" if level < 9 else None
        FO = cx.dram("FO", [NFO1, 128, S], BF16, dbg)
        TO = cx.dram("TO", [S, 512], BF16, dbg)
        OUT = cx.dram("OUT", [4, 512, 1024], BF16, "ExternalOutput")
        phase_proj(cx, xT, wF, wT, None, CD["ropec"], CD["ropes"], CD["pswap"], FO, TO, None, SPEC1, 0)
        phase_diff_attn(cx, FO, TO, CD, lq1, lk1, lq2, lk2, gng, OUT)
    return nc


def inputs_C(x1T_b, od, g):
    C = consts()
    wF, wT = l1_weights(od["w_in"], g)
    gsl = od["gn_g"][2 * g * 256:(2 * g + 2) * 256]
    m = {"xT": x1T_b, "wF": wF, "wT": wT,
         "lq1": od["lq1"][None, :], "lk1": od["lk1"][None, :], "lq2": od["lq2"][None, :], "lk2": od["lk2"][None, :],
         "gng": np.ascontiguousarray(gsl.reshape(4, 128).T)}
    for n in ("ropec", "ropes", "pswap", "bcaus", "ident"):
        m["c_" + n] = C[n]
    return m


_NC = {}


def _get(name, fn):
    if name not in _NC:
        _NC[name] = fn()
    return _NC[name]


def _gather_oT(res, b, r, order):
    blocks = [None] * 16
    for g in range(4):
        O = np.asarray(res[4 * b + g]["OUT"])
        for i in range(4):
            blocks[order(g, i)] = O[i * 128:(i + 1) * 128, r * 1024:(r + 1) * 1024]
    return np.ascontiguousarray(np.concatenate(blocks, 0))


def kernel(x, ev_w_in, ev_pe_k, ev_pe_v, ev_w1_k, ev_w2_k, ev_w1_v, ev_w2_v, ev_w_out, ev_ln_g, ev_ln_b,
           od_w_in, od_lq1, od_lk1, od_lq2, od_lk2, od_gn_g, od_w_out, od_ln_g, od_ln_b):
    x = np.asarray(x, dtype=np.float32)
    ev = dict(w_in=np.asarray(ev_w_in)[0], pe_k=np.asarray(ev_pe_k)[0], pe_v=np.asarray(ev_pe_v)[0],
              w1_k=np.asarray(ev_w1_k)[0], w2_k=np.asarray(ev_w2_k)[0], w1_v=np.asarray(ev_w1_v)[0],
              w2_v=np.asarray(ev_w2_v)[0])
    od = dict(w_in=np.asarray(od_w_in)[0], lq1=np.asarray(od_lq1)[0], lk1=np.asarray(od_lk1)[0],
              lq2=np.asarray(od_lq2)[0], lk2=np.asarray(od_lk2)[0], gn_g=np.asarray(od_gn_g)[0])
    cores = list(range(8))
    ncA = _get("A", build_A)
    resA = run_bass_kernel_spmd(ncA, [inputs_A(x, ev, c) for c in cores], core_ids=cores).results
    ordA = lambda g, i: (2 * g + i) if i < 2 else (8 + 2 * g + (i - 2))
    ncB = _get("B", build_B)
    inB = []
    for c in cores:
        b, r = c // 4, c % 4
        inB.append({"oT": _gather_oT(resA, b, r, ordA), "xr": np.ascontiguousarray(x[b, r * 1024:(r + 1) * 1024]),
                    "wout": np.asarray(ev_w_out)[0], "lng": np.asarray(ev_ln_g), "lnb": np.asarray(ev_ln_b)})
    resB = run_bass_kernel_spmd(ncB, inB, core_ids=cores).results
    x1 = np.stack([np.concatenate([np.asarray(resB[4 * b + r]["XO"]) for r in range(4)], 0) for b in range(2)], 0)
    ncC = _get("C", build_C)
    x1T = [np.ascontiguousarray(x1[b].T) for b in range(2)]
    resC = run_bass_kernel_spmd(ncC, [inputs_C(x1T[c // 4], od, c % 4) for c in cores], core_ids=cores).results
    ordC = lambda g, i: 4 * g + i
    inD = []
    for c in cores:
        b, r = c // 4, c % 4
        inD.append({"oT": _gather_oT(resC, b, r, ordC), "xr": np.ascontiguousarray(x1[b, r * 1024:(r + 1) * 1024]),
                    "wout": np.asarray(od_w_out)[0], "lng": np.asarray(od_ln_g), "lnb": np.asarray(od_ln_b)})
    resD = run_bass_kernel_spmd(ncB, inD, core_ids=cores).results
    out = np.stack([np.concatenate([np.asarray(resD[4 * b + r]["XO"]) for r in range(4)], 0) for b in range(2)], 0)
    return out.astype(np.float32)


I32 = mybir.dt.int32


def load_wo(cx, Wo, wout):
    wv = wout.rearrange("(k p) c -> p k c", p=128)
    for cg in range(4):
        for kh in range(2):
            cx.load(Wo[:, 8 * kh:8 * kh + 8, cg * 512:(cg + 1) * 512], wv[:, 8 * kh:8 * kh + 8, cg * 512:(cg + 1) * 512],
                    (), ["Wo%d" % cg], q="pool")


def phase_outproj_ln2(cx, G, roff, xrows, wout, lng, lnb, XO, identD=None, X1T=None, ntok=1024, Wo_pre=None,
                      on_piece=None):
    with ExitStack() as ls:
        sb = lambda n, s, d: cx.sb(n, s, d, ls)
        ri = sb("ri", [1, 1], I32)
        cx.load(ri[:], roff, (), ["ri"], q="pool")
        oTs = sb("oTs", [128, 16, ntok], BF16)
        Gv = G.rearrange("j (k p) t -> p (j k) t", p=128)

        def dyn_load(k4):
            def fn(e):
                with e.register("roff%d_%d" % (cx.p.nphase, k4)) as rr:
                    e.reg_load(rr, ri[0:1, 0:1])
                    v = e.snap(rr)
                    return e.dma_start(out=oTs[:, 4 * k4:4 * k4 + 4, :], in_=Gv[:, bass.ds(v + 4 * k4, 4), :])
            return fn

        import os
        if Wo_pre is None:
            Wo = sb("Wo", [128, 16, D], BF16)
            load_wo(cx, Wo, wout)
        else:
            Wo = Wo_pre
        for k4 in range(4):
            if os.environ.get("MK_STATIC"):
                cx.load(oTs[:, 4 * k4:4 * k4 + 4, :], Gv[:, 4 * k4:4 * k4 + 4, :], ["G"], ["oTs%d" % k4], q="pool")
            else:
                cx.p.add("pool", dyn_load(k4), ["ri", "G0", "G1", "G2", "G3"], ["oTs%d" % k4], dma=True)
        gbc = sb("gbc", [128, D], F32)
        bbc = sb("bbc", [128, D], F32)
        cx.load(gbc[:], lng.partition_broadcast(128), (), ["gbc"], q="sp")
        cx.load(bbc[:], lnb.partition_broadcast(128), (), ["bbc"], q="sp")
        if X1T is not None:
            idb = sb("idb", [128, 128], BF16)
            cx.load(idb[:], identD, (), ["idb"], q="pool")
            x1Ts = sb("x1Ts", [128, 16, ntok], BF16)
            zb = cx.rot_sb("zb", 2, [128, D], BF16, ls)
        xs = cx.rot_sb("xs", 2, [128, D], F32, ls)
        zs = cx.rot_sb("zs", 2, [128, D], F32, ls)
        st6 = cx.rot_sb("st6", 2, [128, 4, 6], F32, ls)
        mv = cx.rot_sb("mv", 2, [128, 2], F32, ls)
        sml = cx.rot_sb("sml", 6, [128, 1], F32, ls)
        banks = cx.banks(list(range(8)), "y")
        for tt in range(ntok // 128):
            x_t, x_k = xs.next()
            cx.load(x_t[:], xrows[tt * 128:(tt + 1) * 128, :], ["X1R"], [x_k], q="sp")
            z_t, z_k = zs.next()
            s_t, s_k = st6.next()
            for cg in range(4):
                b_t, b_k = banks.next()
                for k in range(16):
                    cx.mm(b_t[:], oTs[:, k, tt * 128:(tt + 1) * 128], Wo[:, k, cg * 512:(cg + 1) * 512],
                          k == 0, k == 15, ["oTs%d" % (k // 4), "Wo%d" % cg], [b_k])
                zc = z_t[:, cg * 512:(cg + 1) * 512]
                cx.stt(zc, x_t[:, cg * 512:(cg + 1) * 512], ALPHA, b_t[:], ALU.mult, ALU.add, [x_k, b_k],
                       [z_k + "_%d" % cg])
                cx.p.add("dve", lambda e, o=s_t[:, cg, :], i=zc: e.bn_stats(out=o, in_=i), [z_k + "_%d" % cg],
                         [s_k + "_%d" % cg])
            m_t, m_k = mv.next()
            cx.p.add("dve", lambda e, o=m_t[:], i=s_t[:]: e.bn_aggr(out=o, in_=i),
                     [s_k + "_%d" % cg for cg in range(4)], [m_k])
            sd, sd_k = sml.next()
            cx.ts(sd[:], m_t[:, 1:2], 1e-5, ALU.add, [m_k], [sd_k])
            cx.actf(sd[:], sd[:], ACT.Sqrt, [sd_k], [sd_k])
            rs, rs_k = sml.next()
            cx.recip(rs[:], sd[:], [sd_k], [rs_k])
            nb, nb_k = sml.next()
            cx.stt(nb[:], m_t[:, 0:1], -1.0, rs[:], ALU.mult, ALU.mult, [m_k, rs_k], [nb_k])
            zk = [z_k + "_%d" % cg for cg in range(4)]
            cx.ts(z_t[:], z_t[:], rs[:], ALU.mult, zk + [rs_k, nb_k], zk, s2=nb[:], op1=ALU.add)
            cx.tt(z_t[:], z_t[:], gbc[:], ALU.mult, zk + ["gbc"], zk)
            cx.tt(z_t[:], z_t[:], bbc[:], ALU.add, zk + ["bbc"], zk)
            cx.store(XO[tt * 128:(tt + 1) * 128, :], z_t[:], zk, [("XO", tt)], q="sp")
            if X1T is not None:
                zb_t, zb_k = zb.next()
                cx.actf(zb_t[:], z_t[:], ACT.Copy, zk, [zb_k])
                for k4 in range(4):
                    b_t, b_k = banks.next()
                    for j in range(4):
                        kk = 4 * k4 + j
                        cx.mm(b_t[:, j * 128:(j + 1) * 128], zb_t[:, kk * 128:(kk + 1) * 128], idb[:], True, True,
                              [zb_k, "idb"], [b_k])
                    cx.actf(x1Ts[:, 4 * k4:4 * k4 + 4, tt * 128:(tt + 1) * 128],
                            b_t[:].rearrange("p (j t) -> p j t", j=4), ACT.Copy, [b_k], ["x1Ts%d_%d" % (tt, k4)])
                X1Tv = X1T.rearrange("j (k p) t -> p j k t", p=128)
                cx.store(X1Tv[:, tt // 2, :, (tt % 2) * 128:(tt % 2) * 128 + 128], x1Ts[:, :, tt * 128:(tt + 1) * 128],
                         ["x1Ts%d_%d" % (tt, k4) for k4 in range(4)], [("X1T", tt)], q="sp")
                if on_piece is not None and tt % 2 == 1:
                    on_piece(tt // 2, [("X1T", tt - 1), ("X1T", tt)])
        cx.p.emit()


def allgather(cx, src, dst, name):
    for j in range(4):
        cx.p.collective(cx.st, lambda e, s=src[j], d=dst[j]: e.collective_compute(
            "AllGather", ALU.bypass, replica_groups=[[0, 1, 2, 3], [4, 5, 6, 7]], ins=[s], outs=[d]),
            (), ["%s%d" % (name, j)])


def allgather_j(cx, src, dst, name, j, reads):
    cx.p.collective(cx.st, lambda e, s=src[j], d=dst[j]: e.collective_compute(
        "AllGather", ALU.bypass, replica_groups=[[0, 1, 2, 3], [4, 5, 6, 7]], ins=[s], outs=[d]),
        list(reads), ["%s%d" % (name, j)])


CONST_F = CONST_A


def build_fused():
    nc = bass.Bass("TRN2", target_bir_lowering=False)
    with ExitStack() as st:
        cx = Ctx(nc, st)
        C = consts()
        ein = lambda n, s, d=F32: cx.dram(n, s, d, "ExternalInput")
        xT = ein("xT", [D, S])
        xr = ein("xr", [1024, D])
        roff = ein("roff", [1, 1], I32)
        wF = ein("wF", [D, 16 * 128])
        wT = ein("wT", [D, 512])
        wG = ein("wG", [D, 6])
        w1k = ein("w1k", [4096, 128])
        w1v = ein("w1v", [4096, 128])
        w2k = ein("w2k", [128, 128])
        w2v = ein("w2v", [128, 128])
        pekT = ein("pekT", [128, 32])
        pevT = ein("pevT", [128, 32])
        wo0 = ein("wo0", [D, D])
        lng0 = ein("lng0", [1, D])
        lnb0 = ein("lnb0", [1, D])
        wF1 = ein("wF1", [D, 12 * 128])
        wT1 = ein("wT1", [D, 512])
        lq1 = ein("lq1", [1, 128])
        lk1 = ein("lk1", [1, 128])
        lq2 = ein("lq2", [1, 128])
        lk2 = ein("lk2", [1, 128])
        gng = ein("gng", [128, 4])
        wo1 = ein("wo1", [D, D])
        lng1 = ein("lng1", [1, D])
        lnb1 = ein("lnb1", [1, D])
        CD = {n: ein("c_" + n, list(C[n].shape)) for n in CONST_F}
        FO = cx.dram("FO", [NFO0, 128, S], BF16)
        TO = cx.dram("TO", [S, 512], BF16)
        GO = cx.dram("GO", [8, S], F32)
        KCMP = cx.dram("KCMP", [128, 256], BF16)
        VCMP = cx.dram("VCMP", [128, 2, 128], BF16)
        SELB = cx.dram("SELB", [64, S], BF16)
        OUT0 = cx.dram("OUT0", [4, 512, 1024], BF16)
        G0 = cx.dram("G0", [4, 4 * 512, 1024], BF16)
        X1R = cx.dram("X1R", [1024, D], F32)
        X1T = cx.dram("X1T", [4, D, 256], BF16)
        X1G = cx.dram("X1G", [4, 4 * D, 256], BF16)
        FO1 = cx.dram("FO1", [NFO1, 128, S], BF16)
        TO1 = cx.dram("TO1", [S, 512], BF16)
        OUT1 = cx.dram("OUT1", [4, 512, 1024], BF16)
        G1 = cx.dram("G1", [4, 4 * 512, 1024], BF16)
        XO = cx.dram("XO", [1024, D], F32, "ExternalOutput")
        phase_proj(cx, xT, wF, wT, wG, CD["ropec"], CD["ropes"], CD["pswap"], FO, TO, GO, SPEC0, 6)
        phase_nsa_prep(cx, FO, w1k, w1v, w2k, w2v, pekT, pevT, CD, KCMP, VCMP, SELB)
        phase_nsa_attn(cx, FO, TO, GO, CD, KCMP, VCMP, SELB, OUT0)
        with ExitStack() as span:
            Wo0 = cx.sb("Wo0", [128, 16, D], BF16, span)
            phase_sb_attn(cx, FO, TO, CD, OUT0,
                          on_quarter=lambda j, reads: allgather_j(cx, OUT0, G0, "G", j, reads),
                          prefetch=lambda: load_wo(cx, Wo0, wo0))
            phase_outproj_ln2(cx, G0, roff, xr, wo0, lng0, lnb0, X1R, CD["ident"], X1T, Wo_pre=Wo0,
                              on_piece=lambda j, reads: allgather_j(cx, X1T, X1G, "X1G", j, reads))
        X1Gv = X1G.rearrange("j (r k p) t -> p j r k t", r=4, p=128)
        xsrc = lambda c, k4: [((jj * 256, jj * 256 + 256), X1Gv[:, (c % 2) * 2 + jj, c // 2, 4 * k4:4 * k4 + 4, :])
                              for jj in range(2)]
        phase_proj(cx, None, wF1, wT1, None, CD["ropec"], CD["ropes"], CD["pswap"], FO1, TO1, None, SPEC1, 0,
                   xsrc=xsrc)
        phase_diff_attn(cx, FO1, TO1, CD, lq1, lk1, lq2, lk2, gng, OUT1,
                        on_quarter=lambda j, reads: allgather_j(cx, OUT1, G1, "G", j, reads))
        phase_outproj_ln2(cx, G1, roff, X1R, wo1, lng1, lnb1, XO)
    return nc


def _perm_wout0(w):
    rows = []
    for g in range(4):
        for i in range(4):
            blk = (2 * g + i) if i < 2 else (8 + 2 * g + (i - 2))
            rows.append(w[blk * 128:(blk + 1) * 128])
    return np.ascontiguousarray(np.concatenate(rows, 0))


def kernel(x, ev_w_in, ev_pe_k, ev_pe_v, ev_w1_k, ev_w2_k, ev_w1_v, ev_w2_v, ev_w_out, ev_ln_g, ev_ln_b,
           od_w_in, od_lq1, od_lk1, od_lq2, od_lk2, od_gn_g, od_w_out, od_ln_g, od_ln_b):
    x = np.asarray(x, dtype=np.float32)
    ev = dict(w_in=np.asarray(ev_w_in)[0], pe_k=np.asarray(ev_pe_k)[0], pe_v=np.asarray(ev_pe_v)[0],
              w1_k=np.asarray(ev_w1_k)[0], w2_k=np.asarray(ev_w2_k)[0], w1_v=np.asarray(ev_w1_v)[0],
              w2_v=np.asarray(ev_w2_v)[0])
    od = dict(w_in=np.asarray(od_w_in)[0], lq1=np.asarray(od_lq1)[0], lk1=np.asarray(od_lk1)[0],
              lq2=np.asarray(od_lq2)[0], lk2=np.asarray(od_lk2)[0], gn_g=np.asarray(od_gn_g)[0])
    C = consts()
    wo0 = _perm_wout0(np.asarray(ev_w_out)[0])
    xTs = [np.ascontiguousarray(x[b].T) for b in range(2)]
    cores = list(range(8))
    maps = []
    for c in cores:
        b, g = c // 4, c % 4
        m = inputs_A(x, ev, c)
        m["xT"] = xTs[b]
        m["xr"] = np.ascontiguousarray(x[b, g * 1024:(g + 1) * 1024])
        m["roff"] = np.array([[g * 16]], np.int32)
        m["wo0"] = wo0
        m["lng0"] = np.asarray(ev_ln_g, np.float32).reshape(1, D)
        m["lnb0"] = np.asarray(ev_ln_b, np.float32).reshape(1, D)
        mc = inputs_C(None, od, g)
        m["wF1"], m["wT1"] = mc["wF"], mc["wT"]
        for n in ("lq1", "lk1", "lq2", "lk2", "gng"):
            m[n] = mc[n]
        m["wo1"] = np.asarray(od_w_out)[0]
        m["lng1"] = np.asarray(od_ln_g, np.float32).reshape(1, D)
        m["lnb1"] = np.asarray(od_ln_b, np.float32).reshape(1, D)
        maps.append(m)
    nc = _get("F", build_fused)
    res = run_bass_kernel_spmd(nc, maps, core_ids=cores).results
    out = np.stack([np.concatenate([np.asarray(res[4 * b + r]["XO"]) for r in range(4)], 0) for b in range(2)], 0)
    return out.astype(np.float32)
```

```python
import math
from contextlib import ExitStack
import numpy as np
import concourse.bass as bass
import concourse.mybir as mybir
from concourse.bass_utils import run_bass_kernel_spmd

ACT = mybir.ActivationFunctionType
ALU = mybir.AluOpType
AX = mybir.AxisListType
F32 = mybir.dt.float32
BF16 = mybir.dt.bfloat16

D = 2048
S = 4096
NCH = 8
NT = 32
SCALE = 128 ** -0.5
NEG = -32768.0
ALPHA = 4.0 ** 0.25
LAMBDA_INIT = 0.8 - 0.6 * math.exp(-0.3)


class _Op:
    __slots__ = ("eng", "fn", "deps", "isdma", "sem", "val", "sig", "waits", "prewait", "ccsem")


class Prog:
    COMPUTE = ("pe", "act", "dve", "pool")
    QUEUES = ("sp", "pool")
    NDMASEM = 8

    def __init__(self, nc, st):
        self.nc = nc
        self.ops = []
        self.last_w = {}
        self.readers = {}
        self.sems = {e: st.enter_context(nc.semaphore("p_" + e)) for e in self.COMPUTE}
        self.dsems = {q: [st.enter_context(nc.semaphore("d_%s%d" % (q, i))) for i in range(self.NDMASEM)]
                      for q in self.QUEUES}
        self.cnt = {e: 0 for e in self.COMPUTE}
        self.dcnt = {q: 0 for q in self.QUEUES}
        self.nphase = 0

    def add(self, eng, fn, reads=(), writes=(), dma=False):
        op = _Op()
        op.eng = eng
        op.fn = fn
        op.isdma = dma
        op.sig = False
        op.ccsem = None
        deps = []
        for t in reads:
            w = self.last_w.get(t)
            if w is not None:
                deps.append(w)
        for t in writes:
            w = self.last_w.get(t)
            if w is not None:
                deps.append(w)
            last = {}
            for r in self.readers.get(t, ()):
                if r.isdma:
                    deps.append(r)
                else:
                    last[r.eng] = r
            deps.extend(last.values())
        op.deps = deps
        for t in reads:
            self.readers.setdefault(t, []).append(op)
        for t in writes:
            self.last_w[t] = op
            self.readers[t] = []
        self.ops.append(op)
        return op

    def dma(self, q, out, in_, reads=(), writes=()):
        return self.add(q, lambda e, o=out, i=in_: e.dma_start(out=o, in_=i), reads, writes, dma=True)

    def collective(self, st, fn, reads=(), writes=()):
        op = self.add("pool", fn, reads, writes, dma=True)
        self.ncc = getattr(self, "ncc", 0) + 1
        op.ccsem = st.enter_context(self.nc.semaphore("cc%d" % self.ncc))
        return op

    def emit(self):
        nc = self.nc
        ops = self.ops
        for op in ops:
            for d in op.deps:
                if d.isdma:
                    continue
                if d.eng == "pe" and op.eng == "pe" and not op.isdma:
                    continue
                d.sig = True
        for op in ops:
            op.prewait = None
            if op.ccsem is not None:
                op.sem = op.ccsem
                op.val = 1
            elif op.isdma:
                i = self.dcnt[op.eng]
                self.dcnt[op.eng] += 1
                s = self.dsems[op.eng][i % self.NDMASEM]
                n = i // self.NDMASEM
                op.sem = s
                op.val = 16 * (n + 1)
                if n > 0:
                    op.prewait = (s, 16 * n)
            elif op.sig:
                self.cnt[op.eng] += 1
                op.sem = self.sems[op.eng]
                op.val = self.cnt[op.eng]
            else:
                op.sem = None
                op.val = 0
        known = {e: {} for e in ("pe", "act", "dve", "pool", "sp")}
        clocks = {}
        for op in ops:
            kn = known[op.eng]
            waits = {}

            def need(sem, val, clk):
                if kn.get(sem, 0) >= val:
                    return
                if waits.get(sem, 0) < val:
                    waits[sem] = val
                kn[sem] = val
                if clk is not None:
                    for s2, v2 in clk.items():
                        if kn.get(s2, 0) < v2:
                            kn[s2] = v2

            if op.prewait is not None:
                need(op.prewait[0], op.prewait[1], None)
            for d in op.deps:
                if d.sem is None:
                    continue
                if (not d.isdma) and d.eng == "pe" and op.eng == "pe" and not op.isdma:
                    continue
                need(d.sem, d.val, clocks.get(id(d)))
            op.waits = list(waits.items())
            if op.sem is not None:
                clk = dict(kn)
                clk[op.sem] = op.val
                clocks[id(op)] = clk
        fw = []
        for q in self.QUEUES:
            for i in range(self.NDMASEM):
                tot = (self.dcnt[q] - i + self.NDMASEM - 1) // self.NDMASEM
                if tot > 0:
                    fw.append((self.dsems[q][i], 16 * tot))
        for op in ops:
            if op.ccsem is not None:
                fw.append((op.ccsem, 1))
        engmap = {"pe": "tensor", "act": "scalar", "dve": "vector", "pool": "gpsimd", "sp": "sync"}
        with nc.named_scope("phase%d" % self.nphase), nc.Block("ph%d" % self.nphase) as block:
            for ename, bname in engmap.items():
                mine = [o for o in ops if o.eng == ename]
                if not mine and ename != "sp":
                    continue

                def body(eng, mine=mine, ename=ename):
                    for o in mine:
                        for s, v in o.waits:
                            eng.wait_ge(s, v)
                        ins = o.fn(eng)
                        if o.ccsem is not None:
                            ins.then_inc(o.sem)
                        elif o.sem is not None:
                            ins.then_inc(o.sem, 16 if o.isdma else 1)
                    if ename == "sp":
                        for s, v in fw:
                            eng.wait_ge(s, v)

                getattr(block, bname)(body)
        self.nphase += 1
        self.ops = []
        self.last_w = {}
        self.readers = {}


class Rot:
    def __init__(self, name, tensors, keys=None):
        self.name = name
        self.t = tensors
        self.k = keys or ["%s%d" % (name, i) for i in range(len(tensors))]
        self.i = 0

    def next(self):
        k = self.i % len(self.t)
        self.i += 1
        return self.t[k], self.k[k]


class Ctx:
    def __init__(self, nc, st):
        self.nc = nc
        self.st = st
        self.p = Prog(nc, st)
        self.PB = [self.ps("PB%d" % i) for i in range(8)]
        self.PBK = ["PB%d" % i for i in range(8)]

    def sb(self, name, shape, dt, st=None):
        self._uid = getattr(self, "_uid", 0) + 1
        return (st or self.st).enter_context(self.nc.sbuf_tensor("%s_u%d" % (name, self._uid), list(shape), dt))

    def banks(self, idx, name):
        return Rot(name, [self.PB[i] for i in idx], [self.PBK[i] for i in idx])

    def ps(self, name, shape=(128, 512), dt=F32):
        return self.st.enter_context(self.nc.psum_tensor(name, list(shape), dt))

    def rot_sb(self, name, n, shape, dt, st=None):
        return Rot(name, [self.sb("%s_%d" % (name, i), shape, dt, st) for i in range(n)])

    def dram(self, name, shape, dt, kind=None):
        if kind is None:
            return self.nc.dram_tensor(name, list(shape), dt).ap()
        return self.nc.dram_tensor(name, list(shape), dt, kind=kind).ap()

    def mm(self, out, lhsT, rhs, start, stop, reads, writes):
        self.p.add("pe", lambda e, o=out, l=lhsT, r=rhs, s=start, t=stop: e.matmul(o, l, r, start=s, stop=t),
                   reads, writes)

    def actf(self, out, in_, func, reads, writes, bias=None, scale=None, accum_out=None):
        kw = {}
        if bias is not None:
            kw["bias"] = bias
        if scale is not None:
            kw["scale"] = scale
        if accum_out is not None:
            kw["accum_out"] = accum_out
        self.p.add("act", lambda e, o=out, i=in_, f=func, kw=kw: e.activation(out=o, in_=i, func=f, **kw),
                   reads, writes)

    def tt(self, out, in0, in1, op, reads, writes, eng="dve"):
        self.p.add(eng, lambda e, o=out, a=in0, b=in1, op=op: e.tensor_tensor(out=o, in0=a, in1=b, op=op),
                   reads, writes)

    def ts(self, out, in0, s1, op0, reads, writes, s2=None, op1=None, eng="dve"):
        if op1 is None:
            self.p.add(eng, lambda e, o=out, a=in0, s1=s1, op0=op0: e.tensor_scalar(
                out=o, in0=a, scalar1=s1, scalar2=None, op0=op0), reads, writes)
        else:
            self.p.add(eng, lambda e, o=out, a=in0, s1=s1, s2=s2, op0=op0, op1=op1: e.tensor_scalar(
                out=o, in0=a, scalar1=s1, scalar2=s2, op0=op0, op1=op1), reads, writes)

    def stt(self, out, in0, scalar, in1, op0, op1, reads, writes):
        self.p.add("dve", lambda e, o=out, a=in0, s=scalar, b=in1, op0=op0, op1=op1: e.scalar_tensor_tensor(
            out=o, in0=a, scalar=s, in1=b, op0=op0, op1=op1), reads, writes)

    def copy(self, out, in_, reads, writes, eng="dve"):
        self.p.add(eng, lambda e, o=out, i=in_: e.tensor_copy(out=o, in_=i), reads, writes)

    def recip(self, out, in_, reads, writes):
        self.p.add("dve", lambda e, o=out, i=in_: e.reciprocal(out=o, in_=i), reads, writes)

    def rcp(self, out, in_, reads, writes):
        self.p.add("dve", lambda e, o=out, i=in_: e.reciprocal(out=o, in_=i), reads, writes)

    def memset(self, ap, val, writes, eng="dve"):
        self.p.add(eng, lambda e, a=ap, v=val: e.memset(a, v), (), writes)

    def load(self, out, in_, reads, writes, q="sp"):
        return self.p.dma(q, out, in_, reads, writes)

    def store(self, out, in_, reads, writes, q="sp"):
        return self.p.dma(q, out, in_, reads, writes)


def _consts():
    c = {}
    pos = np.arange(S, dtype=np.float32)
    inv = (np.float32(10000.0) ** (-np.arange(0, 128, 2, dtype=np.float32) / np.float32(128))).astype(np.float32)
    ang = (pos[:, None] * inv[None, :]).astype(np.float32)
    cs, sn = np.cos(ang).T, np.sin(ang).T
    c["ropec"] = np.ascontiguousarray(np.concatenate([cs, cs], 0), dtype=np.float32)
    c["ropes"] = np.ascontiguousarray(np.concatenate([-sn, sn], 0), dtype=np.float32)
    m = np.arange(128)
    psw = np.zeros((128, 128), np.float32)
    psw[(m + 64) % 128, m] = 1.0
    c["pswap"] = psw
    c["ident"] = np.eye(128, dtype=np.float32)
    c["ident32k"] = np.eye(128, dtype=np.float32) * 32768.0
    sl = np.arange(128)[:, None]
    tl = np.arange(512)[None, :]
    c["mcaus"] = np.stack([(128 * i + sl <= tl) for i in range(4)], 1).astype(np.float32)
    c["mstrict"] = np.stack([(128 * i + sl < tl) for i in range(4)], 1).astype(np.float32)
    c["mfar"] = np.stack([(128 * i + sl > tl) for i in range(4)], 1).astype(np.float32)
    c["mcmp"] = np.stack([(tl >= -512 * k + 16 * sl + 31) for k in range(5)], 1).astype(np.float32)
    for nm in ("caus", "far", "cmp"):
        c["b" + nm] = ((c["m" + nm] - 1.0) * 32768.0).astype(np.float32)
    j = np.arange(64)[:, None]
    s_ = np.arange(S)[None, :]
    c["ebig"] = (s_ // 64 == j).astype(np.float32)
    jt = np.arange(128)[:, None]
    c["negtri"] = -(jt >= np.arange(128)[None, :]).astype(np.float32)
    tlc = np.arange(128)[:, None]
    jj = np.arange(512)[None, :]
    c["mfull"] = np.where((jj - 256) <= np.floor((tlc - 31) / 16.0), 0.0, NEG).astype(np.float32)
    jj = np.arange(128)[None, :]
    hi = (tlc >= 64).astype(np.int64)
    rel = jj - 64
    forced1 = rel == hi
    forced2 = rel == hi - 1
    invalid = rel > hi
    c["tkval"] = (~(forced1 | forced2 | invalid)).astype(np.float32)
    c["tkbias"] = (forced1 * 2.0e4 + forced2 * 1.0e4 + invalid * (-1.0e4)).astype(np.float32)
    return c


_CONST = None


def consts():
    global _CONST
    if _CONST is None:
        _CONST = _consts()
    return _CONST


def phase_proj(cx, xT, wF, wT, wG, ropec, ropes, pswap, FO, TO, GO, spec, nG, xsrc=None):
    with ExitStack() as ls:
        NF = len(spec)
        NTC = TO.shape[1]
        psw = cx.sb("psw", [128, 128], BF16, ls)
        cx.load(psw[:], pswap, (), ["pswap"], q="pool")
        Wf = cx.sb("Wf", [128, 16, NF * 128], BF16, ls)
        Wt = cx.sb("Wt", [128, 16, NTC], BF16, ls)
        wFv = wF.rearrange("(k p) c -> p k c", p=128)
        wTv = wT.rearrange("(k p) c -> p k c", p=128)
        def wloads(lo, hi):
            for fi in range(lo, hi):
                cx.load(Wf[:, :, fi * 128:(fi + 1) * 128], wFv[:, :, fi * 128:(fi + 1) * 128], (), ["Wf%d" % fi],
                        q="pool")
        wloads(0, 1)
        if nG:
            Wg = cx.sb("Wg", [128, 16, 8], BF16, ls)
            wGv = wG.rearrange("(k p) c -> p k c", p=128)
        xb = cx.rot_sb("xb", 2, [128, 16, 512], BF16, ls)
        xTv = xT.rearrange("(k p) t -> p k t", p=128) if xsrc is None else None
        pj = cx.banks([0, 1, 2, 3], "pj")
        pr = cx.banks([4, 5], "pr")
        pt = cx.banks([6, 7], "pt")
        ost = cx.rot_sb("ost", 6, [128, 512], BF16, ls)
        qtmp = cx.rot_sb("qtmp", 2, [128, 512], BF16, ls)
        rt1 = cx.rot_sb("rt1", 2, [128, 512], F32, ls)
        rt2 = cx.rot_sb("rt2", 2, [128, 512], F32, ls)
        rc = cx.rot_sb("rc", 2, [128, 512], F32, ls)
        rs = cx.rot_sb("rs", 2, [128, 512], F32, ls)
        gst = cx.rot_sb("gst", 2, [8, 512], F32, ls)
        has_rope = any(s[0].startswith("rope") for s in spec)
        for c in range(NCH):
            tsl = slice(c * 512, (c + 1) * 512)
            x_t, x_k = xb.next()
            for k4 in range(4):
                if xsrc is None:
                    cx.load(x_t[:, 4 * k4:4 * k4 + 4, :], xTv[:, 4 * k4:4 * k4 + 4, tsl], (),
                            [x_k + "_%d" % k4], q="pool")
                else:
                    for jj, ((c0, c1), sap) in enumerate(xsrc(c, k4)):
                        cx.load(x_t[:, 4 * k4:4 * k4 + 4, c0:c1], sap, (), [x_k + "_%d_%d" % (k4, jj)], q="sp")
            if xsrc is None:
                xk = [[x_k + "_%d" % (k // 4)] for k in range(16)]
            else:
                xk = [[x_k + "_%d_%d" % (k // 4, jj) for jj in range(2)] for k in range(16)]
            if c == 0:
                wloads(1, NF)
                cx.load(Wt[:, :, :], wTv[:, :, :], (), ["Wt"], q="pool")
                if nG:
                    cx.load(Wg[:, :, 0:nG], wGv, (), ["Wg"], q="pool")
            if has_rope:
                rc_t, rc_k = rc.next()
                rs_t, rs_k = rs.next()
                cx.load(rc_t[:], ropec[:, tsl], (), [rc_k], q="sp")
                cx.load(rs_t[:], ropes[:, tsl], (), [rs_k], q="sp")

            def rope(src_bf, src_k, out_idx):
                r_t, r_k = pr.next()
                cx.mm(r_t[:], psw[:], src_bf[:], True, True, [src_k, "pswap"], [r_k])
                a_t, a_k = rt1.next()
                b_t, b_k = rt2.next()
                cx.tt(a_t[:], src_bf[:], rc_t[:], ALU.mult, [src_k, rc_k], [a_k])
                cx.tt(b_t[:], r_t[:], rs_t[:], ALU.mult, [r_k, rs_k], [b_k])
                o_t, o_k = ost.next()
                cx.tt(o_t[:], a_t[:], b_t[:], ALU.add, [a_k, b_k], [o_k])
                cx.store(FO[out_idx, :, tsl], o_t[:], [o_k], [("FO", out_idx, c)], q="sp")

            for fi, (kind, oi, oi2) in enumerate(spec):
                b_t, b_k = pj.next()
                for k in range(16):
                    cx.mm(b_t[:], Wf[:, k, fi * 128:(fi + 1) * 128], x_t[:, k, :], k == 0, k == 15,
                          ["Wf%d" % fi] + xk[k], [b_k])
                if kind in ("copy", "scale", "silu"):
                    o_t, o_k = ost.next()
                    if kind == "copy":
                        cx.copy(o_t[:], b_t[:], [b_k], [o_k])
                    elif kind == "scale":
                        cx.actf(o_t[:], b_t[:], ACT.Copy, [b_k], [o_k], scale=SCALE)
                    else:
                        cx.actf(o_t[:], b_t[:], ACT.Silu, [b_k], [o_k])
                    cx.store(FO[oi, :, tsl], o_t[:], [o_k], [("FO", oi, c)], q="sp")
                elif kind == "rope":
                    q_t, q_k = qtmp.next()
                    cx.copy(q_t[:], b_t[:], [b_k], [q_k])
                    rope(q_t, q_k, oi)
                elif kind == "rope_scale_both":
                    o_t, o_k = ost.next()
                    cx.actf(o_t[:], b_t[:], ACT.Copy, [b_k], [o_k], scale=SCALE)
                    cx.store(FO[oi, :, tsl], o_t[:], [o_k], [("FO", oi, c)], q="sp")
                    rope(o_t, o_k, oi2)
                elif kind == "rope_scale":
                    q_t, q_k = qtmp.next()
                    cx.actf(q_t[:], b_t[:], ACT.Copy, [b_k], [q_k], scale=SCALE)
                    rope(q_t, q_k, oi)
            for tt in range(4):
                b_t, b_k = pt.next()
                for k in range(16):
                    cx.mm(b_t[:, 0:NTC], x_t[:, k, tt * 128:(tt + 1) * 128], Wt[:, k, :], k == 0, k == 15,
                          ["Wt"] + xk[k], [b_k])
                o_t, o_k = ost.next()
                cx.copy(o_t[:, 0:NTC], b_t[:, 0:NTC], [b_k], [o_k])
                tok = c * 4 + tt
                cx.store(TO[tok * 128:(tok + 1) * 128, :], o_t[:, 0:NTC], [o_k], [("TO", tok)], q="sp")
            if nG:
                b_t, b_k = pt.next()
                for k in range(16):
                    cx.mm(b_t[0:nG, :], Wg[:, k, 0:nG], x_t[:, k, :], k == 0, k == 15, ["Wg"] + xk[k], [b_k])
                g_t, g_k = gst.next()
                cx.actf(g_t[0:nG, :], b_t[0:nG, :], ACT.Sigmoid, [b_k], [g_k])
                cx.store(GO[0:nG, tsl], g_t[0:nG, :], [g_k], [("GO", c)], q="sp")
        cx.p.emit()


EV_OFF = {}
_o = 0
for _n, _sz in zip(("qa", "kc", "vc", "ks", "vs", "kw", "vw", "ga", "gate_a", "qb", "kb", "vb", "gate_b"),
                   (1024, 256, 256, 256, 256, 256, 256, 24, 1024, 1024, 1024, 1024, 1024)):
    EV_OFF[_n] = _o
    _o += _sz

SPEC0 = ([("rope_scale_both", 0, 4), ("rope_scale_both", 1, 5), ("scale", 2, None), ("scale", 3, None),
          ("copy", 6, None), ("copy", 7, None), ("rope", 8, None), ("rope", 9, None),
          ("silu", 10, None), ("silu", 11, None), ("scale", 12, None), ("scale", 13, None),
          ("copy", 14, None), ("copy", 15, None), ("silu", 16, None), ("silu", 17, None)])
NFO0 = 18


def l0_weights(w_in, g):
    hk = g // 2
    own = [2 * g, 2 * g + 1]
    oth = [h for h in range(4 * hk, 4 * hk + 4) if h not in own]
    hc = lambda name, h: w_in[:, EV_OFF[name] + h * 128: EV_OFF[name] + (h + 1) * 128]
    cols = [hc("qa", own[0]), hc("qa", own[1]), hc("qa", oth[0]), hc("qa", oth[1]),
            hc("kc", hk), hc("vc", hk), hc("ks", hk), hc("kw", hk),
            hc("gate_a", own[0]), hc("gate_a", own[1]),
            hc("qb", own[0]), hc("qb", own[1]), hc("kb", own[0]), hc("kb", own[1]),
            hc("gate_b", own[0]), hc("gate_b", own[1])]
    wF = np.ascontiguousarray(np.concatenate(cols, 1))
    wT = np.ascontiguousarray(np.concatenate([hc("vs", hk), hc("vw", hk), hc("vb", own[0]), hc("vb", own[1])], 1))
    g0 = EV_OFF["ga"]
    wG = np.ascontiguousarray(w_in[:, g0 + 3 * own[0]: g0 + 3 * own[0] + 6])
    return wF, wT, wG


def phase_nsa_prep(cx, FO, w1k, w1v, w2k, w2v, pekT, pevT, CD, KCMP, VCMP, SELB):
    with ExitStack() as ls:
        sb = lambda n, s, d: cx.sb(n, s, d, ls)
        W1 = [sb("W1k", [128, 32, 128], BF16), sb("W1v", [128, 32, 128], BF16)]
        W2 = [sb("W2k", [128, 128], BF16), sb("W2v", [128, 128], BF16)]
        PE_ = [sb("pek", [128, 32], BF16), sb("pev", [128, 32], BF16)]
        src = [sb("kcs", [128, 256, 16], BF16), sb("vcs", [128, 256, 16], BF16)]
        for i, (w1, w2, pe) in enumerate(((w1k, w2k, pekT), (w1v, w2v, pevT))):
            cx.load(W1[i][:], w1.rearrange("(l d) f -> d l f", d=128), (), ["W1_%d" % i], q="pool")
            cx.load(W2[i][:], w2, (), ["W2_%d" % i], q="pool")
            cx.load(PE_[i][:], pe, (), ["PE_%d" % i], q="pool")
            cx.load(src[i][:], FO[6 + i].rearrange("p (n r) -> p n r", r=16), (), ["src%d" % i], q="sp")
        kcmpT = sb("kcmpT", [128, 256], BF16)
        vcmp = sb("vcmp", [128, 2, 128], BF16)
        cx.memset(kcmpT[:, 255:256], 0.0, ["kcmpT"])
        cx.memset(vcmp[:], 0.0, ["vcmp"])
        sact = [sb("sk", [128, 256], BF16), sb("sv", [128, 256], BF16)]
        bias = [sb("bk", [128, 1], F32), sb("bv", [128, 1], F32)]
        PB, PK = cx.PB, cx.PBK
        for i in range(2):
            for l in range(32):
                cx.mm(PB[0][:, 0:1], W1[i][:, l, :], PE_[i][:, l:l + 1], l == 0, l == 31,
                      ["W1_%d" % i, "PE_%d" % i], [PK[0]])
            cx.copy(bias[i][:], PB[0][:, 0:1], [PK[0]], ["bias%d" % i])
            for l in range(32):
                n0, r = (0, l) if l < 16 else (1, l - 16)
                cx.mm(PB[1][:, 0:255], W1[i][:, l, :], src[i][:, n0:n0 + 255, r], l == 0, l == 31,
                      ["W1_%d" % i, "src%d" % i], [PK[1]])
            cx.actf(sact[i][:, 0:255], PB[1][:, 0:255], ACT.Silu, [PK[1], "bias%d" % i], ["sact%d" % i],
                    bias=bias[i][:])
        cx.mm(PB[2][:, 0:255], W2[0][:], sact[0][:, 0:255], True, True, ["W2_0", "sact0"], [PK[2]])
        cx.copy(kcmpT[:, 0:255], PB[2][:, 0:255], [PK[2]], ["kcmpT"])
        for i, M in ((0, 128), (1, 127)):
            cx.mm(PB[3][0:M, i * 128:(i + 1) * 128], sact[1][:, i * 128:i * 128 + M], W2[1][:], True, True,
                  ["W2_1", "sact1"], [PK[3]])
            cx.copy(vcmp[0:M, i, :], PB[3][0:M, i * 128:(i + 1) * 128], [PK[3]], ["vcmp"])
        cx.store(KCMP, kcmpT[:], ["kcmpT"], ["KCMP"], q="sp")
        cx.store(VCMP, vcmp[:], ["vcmp"], ["VCMP"], q="sp")
        qu = [sb("qu%d" % g, [128, S], BF16) for g in range(4)]
        for g in range(4):
            cx.load(qu[g][:], FO[g], (), ["qu%d" % g], q="sp")
        mfull = sb("mfull", [128, 512], F32)
        tkval = sb("tkval", [128, 128], F32)
        tkbias = sb("tkbias", [128, 128], F32)
        id32 = sb("id32", [128, 128], BF16)
        cx.load(mfull[:], CD["mfull"], (), ["mfull"], q="sp")
        cx.load(tkval[:], CD["tkval"], (), ["tkval"], q="sp")
        cx.load(tkbias[:], CD["tkbias"], (), ["tkbias"], q="sp")
        cx.load(id32[:], CD["ident32k"], (), ["id32"], q="pool")
        selbT = sb("selbT", [64, S], BF16)
        pairs = Rot("pp", [(0, 1), (2, 3)])
        pT = cx.banks([4, 5], "pT")
        Psum = cx.rot_sb("Psum", 2, [128, 64, 4], F32, ls)
        for t_ in Psum.t:
            cx.memset(t_[:], 0.0, [Psum.k[Psum.t.index(t_)]])
        sm = cx.rot_sb("sm", 8, [128, 255], F32, ls)
        ee = cx.rot_sb("ee", 8, [128, 255], F32, ls)
        sml = cx.rot_sb("sml", 40, [128, 1], F32, ls)
        imp = cx.rot_sb("imp", 2, [128, 64], F32, ls)
        sc = cx.rot_sb("sc", 2, [128, 64], F32, ls)
        sc2 = cx.rot_sb("sc2", 2, [128, 64], F32, ls)
        m8 = cx.rot_sb("m8", 4, [128, 8], F32, ls)
        selb = cx.rot_sb("selb", 2, [128, 64], BF16, ls)
        def tile_steps(tt):
            L = {}
            steps = []

            def s_mm():
                (ia, ib), _ = pairs.next()
                L["b"] = (ia, ib)
                for g in range(4):
                    bi = ia if g < 2 else ib
                    off = (g % 2) * 256
                    cx.mm(PB[bi][:, off:off + 255], qu[g][:, tt * 128:(tt + 1) * 128], kcmpT[:, 0:255], True, True,
                          ["qu%d" % g, "kcmpT"], [PK[bi]])
                L["P"] = Psum.next()
            steps.append(s_mm)

            def bank(g):
                ia, ib = L["b"]
                bi = ia if g < 2 else ib
                return PB[bi], PK[bi], (g % 2) * 256

            def a1(g):
                b_t, b_k, off = bank(g)
                s_t, s_k = sm.next()
                L["s", g] = (s_t, s_k)
                cx.tt(s_t[:], b_t[:, off:off + 255], mfull[:, 256 - 8 * tt: 256 - 8 * tt + 255], ALU.add,
                      [b_k, "mfull"], [s_k])

            def a2(g):
                s_t, s_k = L["s", g]
                mx, mx_k = sml.next()
                L["mx", g] = (mx, mx_k)
                cx.p.add("dve", lambda e, o=mx[:], i=s_t[:]: e.reduce_max(out=o, in_=i, axis=AX.X), [s_k], [mx_k])

            def a3(g):
                mx, mx_k = L["mx", g]
                nm, nm_k = sml.next()
                L["nm", g] = (nm, nm_k)
                cx.ts(nm[:], mx[:], -1000.0, ALU.max, [mx_k], [nm_k], s2=-1.0, op1=ALU.mult)

            def a4(g):
                s_t, s_k = L["s", g]
                nm, nm_k = L["nm", g]
                e_t, e_k = ee.next()
                dn, dn_k = sml.next()
                L["e", g] = (e_t, e_k)
                L["dn", g] = (dn, dn_k)
                cx.actf(e_t[:], s_t[:], ACT.Exp, [s_k, nm_k], [e_k, dn_k], bias=nm[:], accum_out=dn[:])

            def a5(g):
                dn, dn_k = L["dn", g]
                cx.ts(dn[:], dn[:], 1e-30, ALU.max, [dn_k], [dn_k])

            def a6(g):
                dn, dn_k = L["dn", g]
                rd, rd_k = sml.next()
                L["rd", g] = (rd, rd_k)
                cx.recip(rd[:], dn[:], [dn_k], [rd_k])

            def a7(g):
                P_t, P_k = L["P"]
                P2 = P_t[:].rearrange("p a b -> p (a b)")
                e_t, e_k = L["e", g]
                rd, rd_k = L["rd", g]
                if g == 0:
                    cx.ts(P2[:, 0:255], e_t[:], rd[:], ALU.mult, [e_k, rd_k], [P_k])
                else:
                    cx.stt(P2[:, 0:255], e_t[:], rd[:], P2[:, 0:255], ALU.mult, ALU.add, [e_k, rd_k, P_k], [P_k])

            for fn in (a1, a2, a3, a4, a5, a6, a7):
                for g in range(4):
                    steps.append(lambda fn=fn, g=g: fn(g))

            def t1():
                P_t, P_k = L["P"]
                i_t, i_k = imp.next()
                L["i"] = (i_t, i_k)
                cx.p.add("dve", lambda e, o=i_t[:], i=P_t[:]: e.tensor_reduce(out=o, in_=i, axis=AX.X, op=ALU.add),
                         [P_k], [i_k])

            def t2():
                P_t, P_k = L["P"]
                i_t, i_k = L["i"]
                cx.tt(i_t[:, 1:64], i_t[:, 1:64], P_t[:, 0:63, 3], ALU.add, [i_k, P_k], [i_k])

            def t3():
                i_t, i_k = L["i"]
                c_t, c_k = sc.next()
                L["c"] = (c_t, c_k)
                cx.tt(c_t[:], i_t[:], tkval[:, 64 - 2 * tt:128 - 2 * tt], ALU.mult, [i_k, "tkval"], [c_k])

            def t4():
                c_t, c_k = L["c"]
                cx.tt(c_t[:], c_t[:], tkbias[:, 64 - 2 * tt:128 - 2 * tt], ALU.add, [c_k, "tkbias"], [c_k])

            def t5():
                c_t, c_k = L["c"]
                cx.memset(c_t[:, 0:1], 3.0e4, [c_k])

            def t6():
                c_t, c_k = L["c"]
                m1, m1_k = m8.next()
                L["m1"] = (m1, m1_k)
                cx.p.add("dve", lambda e, o=m1[:], i=c_t[:]: e.max(out=o, in_=i), [c_k], [m1_k])

            def t7():
                c_t, c_k = L["c"]
                m1, m1_k = L["m1"]
                d_t, d_k = sc2.next()
                L["d"] = (d_t, d_k)
                cx.p.add("dve", lambda e, o=d_t[:], r=m1[:], v=c_t[:]: e.match_replace(
                    out=o, in_to_replace=r, in_values=v, imm_value=-1.0e9), [c_k, m1_k], [d_k])

            def t8():
                d_t, d_k = L["d"]
                m2, m2_k = m8.next()
                L["m2"] = (m2, m2_k)
                cx.p.add("dve", lambda e, o=m2[:], i=d_t[:]: e.max(out=o, in_=i), [d_k], [m2_k])

            def t9():
                c_t, c_k = L["c"]
                m2, m2_k = L["m2"]
                sb_t, sb_k = selb.next()
                L["sb"] = (sb_t, sb_k)
                cx.ts(sb_t[:], c_t[:], m2[:, 7:8], ALU.is_ge, [c_k, m2_k], [sb_k], s2=1.0, op1=ALU.subtract)

            def t10():
                sb_t, sb_k = L["sb"]
                t_t, t_k = pT.next()
                cx.mm(t_t[0:64, 0:128], sb_t[:], id32[:], True, True, [sb_k, "id32"], [t_k])
                cx.copy(selbT[:, tt * 128:(tt + 1) * 128], t_t[0:64, 0:128], [t_k], ["selbT"])

            steps.extend([t1, t2, t3, t4, t5, t6, t7, t8, t9, t10])
            return steps

        for tt in range(0, NT, 2):
            sa, sb_ = tile_steps(tt), tile_steps(tt + 1)
            for x, y in zip(sa, sb_):
                x()
                y()
        cx.store(SELB, selbT[:], ["selbT"], ["SELB"], q="sp")
        cx.p.emit()


CONST_A = ("ropec", "ropes", "pswap", "ident32k", "bcaus", "mstrict", "bfar", "bcmp", "ebig", "negtri",
           "mfull", "tkval", "tkbias", "ident")


def build_A(level=9):
    nc = bass.Bass("TRN2", target_bir_lowering=False)
    with ExitStack() as st:
        cx = Ctx(nc, st)
        C = consts()
        xT = cx.dram("xT", [D, S], F32, "ExternalInput")
        wF = cx.dram("wF", [D, 16 * 128], F32, "ExternalInput")
        wT = cx.dram("wT", [D, 512], F32, "ExternalInput")
        wG = cx.dram("wG", [D, 6], F32, "ExternalInput")
        w1k = cx.dram("w1k", [4096, 128], F32, "ExternalInput")
        w1v = cx.dram("w1v", [4096, 128], F32, "ExternalInput")
        w2k = cx.dram("w2k", [128, 128], F32, "ExternalInput")
        w2v = cx.dram("w2v", [128, 128], F32, "ExternalInput")
        pekT = cx.dram("pekT", [128, 32], F32, "ExternalInput")
        pevT = cx.dram("pevT", [128, 32], F32, "ExternalInput")
        CD = {n: cx.dram("c_" + n, list(C[n].shape), F32, "ExternalInput") for n in CONST_A}
        dbg = "ExternalOutput" if level < 9 else None
        FO = cx.dram("FO", [NFO0, 128, S], BF16, dbg)
        TO = cx.dram("TO", [S, 512], BF16, dbg)
        GO = cx.dram("GO", [8, S], F32, dbg)
        KCMP = cx.dram("KCMP", [128, 256], BF16, dbg)
        VCMP = cx.dram("VCMP", [128, 2, 128], BF16, dbg)
        SELB = cx.dram("SELB", [64, S], BF16, dbg)
        OUT = cx.dram("OUT", [4, 512, 1024], BF16, "ExternalOutput")
        phase_proj(cx, xT, wF, wT, wG, CD["ropec"], CD["ropes"], CD["pswap"], FO, TO, GO, SPEC0, 6)
        if level >= 2:
            phase_nsa_prep(cx, FO, w1k, w1v, w2k, w2v, pekT, pevT, CD, KCMP, VCMP, SELB)
        if level >= 3:
            phase_nsa_attn(cx, FO, TO, GO, CD, KCMP, VCMP, SELB, OUT)
        if level >= 4:
            phase_sb_attn(cx, FO, TO, CD, OUT)
    return nc


def inputs_A(x, ev, c):
    b, g = c // 4, c % 4
    C = consts()
    wF, wT, wG = l0_weights(ev["w_in"], g)
    m = {"xT": np.ascontiguousarray(x[b].T), "wF": wF, "wT": wT, "wG": wG,
         "w1k": ev["w1_k"], "w1v": ev["w1_v"], "w2k": ev["w2_k"], "w2v": ev["w2_v"],
         "pekT": np.ascontiguousarray(ev["pe_k"].T), "pevT": np.ascontiguousarray(ev["pe_v"].T)}
    for n in CONST_A:
        m["c_" + n] = C[n]
    return m


def run_attn_stream(cx, groups, zrot, erot, esrot, ident, onesb, hlrot, LA=1, defer=2):
    flat = []
    for gi, g in enumerate(groups):
        g["gid"] = gi
        n = len(g["tiles"])
        for i, t in enumerate(g["tiles"]):
            flat.append((g, i, n, t))
    zs = {}

    def emit_z(idx):
        g, i, n, t = flat[idx]
        z_t, z_k = zrot.next()
        sel = t.get("sel")
        mb = t.get("maskb")
        extra = (sel is not None) + (mb is not None)
        cx.mm(z_t[:], t["kT"], g["q_ap"], True, extra == 0, list(t["kkeys"]) + list(g["q_keys"]), [z_k])
        if sel is not None:
            extra -= 1
            cx.mm(z_t[:], sel[0], sel[1], False, extra == 0, list(sel[2]), [z_k])
        if mb is not None:
            cx.mm(z_t[:], ident, mb, False, True, ["ident"] + list(t["mkeys"]), [z_k])
        zs[idx] = (z_t, z_k)

    pending = []

    def run_chain(f):
        r = f()
        while r is not None:
            r = r[0]() if isinstance(r, tuple) else r()

    def flush(resource, exclude):
        keep = []
        todo = []
        for p in pending:
            (todo if (resource in p[2] and p[3] != exclude) else keep).append(p)
        pending[:] = keep
        for p in todo:
            run_chain(p[1])

    def make_chain(g):
        ep = g["epilogue"]

        def stage_c():
            r = ep()
            if callable(r):
                r = r()
            return r
        return stage_c

    for idx in range(min(LA, len(flat))):
        emit_z(idx)
    for idx in range(len(flat)):
        if idx + LA < len(flat):
            emit_z(idx + LA)
        g, i, n, t = flat[idx]
        if i == 0:
            flush(g["oaccs"][0][1], g["gid"])
            flush(g["den"][1], g["gid"])
        z_t, z_k = zs.pop(idx)
        e_t, e_k = erot.next()
        cx.actf(e_t[:], z_t[:], ACT.Exp, [z_k], [e_k])
        for (o_ap, o_k), v in zip(g["oaccs"], t["vs"]):
            cx.mm(o_ap, v, e_t[:], i == 0, i == n - 1, [e_k] + list(t["vkeys"]), [o_k])
        cx.mm(g["den"][0], onesb, e_t[:], i == 0, i == n - 1, [e_k, "onesb"], [g["den"][1]])
        for p in pending:
            p[0] -= 1
        ready = [p for p in pending if p[0] <= 0]
        pending[:] = [p for p in pending if p[0] > 0]
        for p in ready:
            r = p[1]()
            if isinstance(r, tuple):
                pending.append([r[1], r[0], p[2], p[3]])
            elif callable(r):
                pending.append([defer, r, p[2], p[3]])
        if i == n - 1:
            pending.append([defer, make_chain(g), {g["den"][1], g["oaccs"][0][1]}, g["gid"]])
    while pending:
        p = pending.pop(0)
        run_chain(p[1])


def chunked_loads(cx, items, q="sp"):
    for c in range(NCH):
        for t_, d_, name, kind in items:
            if kind == "F":
                cx.load(t_[:, c * 512:(c + 1) * 512], d_[:, c * 512:(c + 1) * 512], (), ["%s_%d" % (name, c)], q=q)
            else:
                cx.load(t_[:, 4 * c:4 * c + 4, :], d_[:, 4 * c:4 * c + 4, :], (), ["%s_%d" % (name, c)], q=q)


def phase_nsa_attn(cx, FO, TO, GO, CD, KCMP, VCMP, SELB, OUT):
    with ExitStack() as ls:
        sb = lambda n, s, d: cx.sb(n, s, d, ls)
        PB, PK = cx.PB, cx.PBK
        qu = [sb("qu%d" % h, [128, S], BF16) for h in range(2)]
        qr = [sb("qr%d" % h, [128, S], BF16) for h in range(2)]
        gas = [sb("gas%d" % h, [128, S], BF16) for h in range(2)]
        kcmpT = sb("kcmpT", [128, 256], BF16)
        vcmp = sb("vcmp", [128, 2, 128], BF16)
        selbT = sb("selbT", [64, S], BF16)
        ebig = sb("ebig", [64, S], BF16)
        cx.load(kcmpT[:], KCMP, (), ["kcmpT"])
        cx.load(vcmp[:], VCMP, (), ["vcmp"])
        mcaus = sb("mcaus", [128, 4, 512], BF16)
        mfar = sb("mfar", [128, 4, 512], BF16)
        mcmp = sb("mcmp", [128, 5, 512], BF16)
        identb = sb("identb", [128, 128], BF16)
        cx.load(identb[:], CD["ident"], (), ["ident"], q="pool")
        cx.load(mcmp[:], CD["bcmp"], (), ["mcmp"], q="pool")
        cx.load(mcaus[:], CD["bcaus"], (), ["mcaus"], q="pool")
        cx.load(ebig[:], CD["ebig"], (), ["ebig"], q="pool")
        cx.load(mfar[:], CD["bfar"], (), ["mfar"], q="pool")
        ksT = sb("ksT", [128, S], BF16)
        kwT = sb("kwT", [128, S], BF16)
        vs = sb("vs", [128, NT, 128], BF16)
        vw = sb("vw", [128, NT, 128], BF16)
        TOv = TO.rearrange("(k p) c -> p k c", p=128)
        chunked_loads(cx, [(qu[0], FO[0], "qu0", "F"), (qr[0], FO[4], "qr0", "F"), (selbT, SELB, "selbT", "F"),
                           (ksT, FO[8], "ksT", "F"), (vs, TOv[:, :, 0:128], "vs", "T"),
                           (kwT, FO[9], "kwT", "F"), (vw, TOv[:, :, 128:256], "vw", "T"),
                           (qu[1], FO[1], "qu1", "F"), (qr[1], FO[5], "qr1", "F"),
                           (gas[0], FO[10], "gas0", "F"), (gas[1], FO[11], "gas1", "F")], q="pool")
        zrot = cx.banks([0, 1, 2], "z")
        sets = Rot("sets", [(5, 3), (6, 4), (7, 3), (5, 4), (6, 3), (7, 4)])
        erot = cx.rot_sb("e", 5, [128, 512], BF16, ls)
        esrot = cx.rot_sb("es", 4, [128, 512], F32, ls)
        hlrot = cx.rot_sb("hl", 6, [128, 512], BF16, ls)
        onesb = sb("onesb", [128, 128], BF16)
        cx.memset(onesb[:], 1.0, ["onesb"])
        gb = cx.rot_sb("gb", 4, [128, 512], F32, ls)
        dnc = cx.rot_sb("dnc", 3, [128, 512], F32, ls)
        tmp = cx.rot_sb("tmp", 2, [128, 512], F32, ls)
        osum = cx.rot_sb("osum", 2, [128, 512], F32, ls)
        ost = cx.rot_sb("ost", 2, [128, 512], BF16, ls)
        groups = []
        state = {}
        for h in range(2):
            for c in range(NCH):
                tsl = slice(c * 512, (c + 1) * 512)
                for br in range(3):
                    tiles = []
                    if br == 0:
                        q_ap, q_keys = qu[h][:, tsl], ["qu%d_%d" % (h, c)]
                        nts = [0] + ([1] if c >= 4 else [])
                        for nt in nts:
                            mk = None
                            if nt == 0 and c <= 4:
                                mk = mcmp[:, c, :]
                            if nt == 1:
                                mk = mcmp[:, c - 4, :]
                            tiles.append(dict(kT=kcmpT[:, nt * 128:(nt + 1) * 128], kkeys=["kcmpT"],
                                              vs=[vcmp[:, nt, :]], vkeys=["vcmp"], maskb=mk, mkeys=["mcmp"]))
                    elif br == 1:
                        q_ap, q_keys = qr[h][:, tsl], ["qr%d_%d" % (h, c)]
                        for kt in range(4 * c + 4):
                            mk = mcaus[:, kt - 4 * c, :] if kt >= 4 * c else None
                            tiles.append(dict(kT=ksT[:, kt * 128:(kt + 1) * 128], kkeys=["ksT_%d" % (kt // 4)],
                                              vs=[vs[:, kt, :]], vkeys=["vs_%d" % (kt // 4)], maskb=mk, mkeys=["mcaus"],
                                              sel=(ebig[:, kt * 128:(kt + 1) * 128], selbT[:, tsl],
                                                   ["ebig", "selbT_%d" % c])))
                    else:
                        q_ap, q_keys = qr[h][:, tsl], ["qr%d_%d" % (h, c)]
                        for kt in range(max(0, 4 * c - 4), 4 * c + 4):
                            if kt >= 4 * c:
                                mk, mkk = mcaus[:, kt - 4 * c, :], ["mcaus"]
                            else:
                                mk, mkk = mfar[:, kt - (4 * c - 4), :], ["mfar"]
                            tiles.append(dict(kT=kwT[:, kt * 128:(kt + 1) * 128], kkeys=["kwT_%d" % (kt // 4)],
                                              vs=[vw[:, kt, :]], vkeys=["vw_%d" % (kt // 4)], maskb=mk, mkeys=mkk))
                    (io, idn), _ = sets.next()

                    def epi(h=h, c=c, br=br, io=io, idn=idn, tsl=tsl):
                        def deferred():
                            if br == 0:
                                state["os"] = osum.next()
                            os_t, os_k = state["os"]
                            g_t, g_k = gb.next()
                            cx.load(g_t[:], GO[3 * h + br:3 * h + br + 1, tsl].partition_broadcast(128), (), [g_k])
                            d_t, d_k = dnc.next()
                            cx.ts(d_t[:], PB[idn][:], 1e-30, ALU.max, [PK[idn]], [d_k])
                            cx.actf(d_t[:], d_t[:], ACT.Ln, [d_k], [d_k])
                            cx.actf(d_t[:], d_t[:], ACT.Exp, [d_k], [d_k], scale=-1.0)
                            cx.tt(d_t[:], d_t[:], g_t[:], ALU.mult, [d_k, g_k], [d_k])
                            if br == 0:
                                cx.tt(os_t[:], PB[io][:], d_t[:], ALU.mult, [PK[io], d_k], [os_k])
                            else:
                                t_t, t_k = tmp.next()
                                cx.tt(t_t[:], PB[io][:], d_t[:], ALU.mult, [PK[io], d_k], [t_k])
                                cx.tt(os_t[:], os_t[:], t_t[:], ALU.add, [os_k, t_k], [os_k])
                            if br == 2:
                                o_t, o_k = ost.next()
                                cx.tt(o_t[:], os_t[:], gas[h][:, tsl], ALU.mult, [os_k, "gas%d_%d" % (h, c)], [o_k])
                                cx.store(OUT[c // 2, h * 128:(h + 1) * 128, (c % 2) * 512:(c % 2) * 512 + 512],
                                         o_t[:], [o_k], [("OUT", h, c)], q="sp")
                        return deferred

                    groups.append(dict(q_ap=q_ap, q_keys=q_keys, tiles=tiles, oaccs=[(PB[io][:], PK[io])],
                                       den=(PB[idn][:], PK[idn]), epilogue=epi))
        run_attn_stream(cx, groups, zrot, erot, esrot, identb[:], onesb[:], hlrot, LA=2, defer=2)
        cx.p.emit()


def phase_sb_attn(cx, FO, TO, CD, OUT, on_quarter=None, prefetch=None):
    with ExitStack() as ls:
        sb = lambda n, s, d: cx.sb(n, s, d, ls)
        PB, PK = cx.PB, cx.PBK
        TOv = TO.rearrange("(k p) c -> p k c", p=128)
        q = [sb("q%d" % h, [128, S], BF16) for h in range(2)]
        k = [sb("k%d" % h, [128, S], BF16) for h in range(2)]
        gbs = [sb("gbs%d" % h, [128, S], BF16) for h in range(2)]
        v = [sb("v%d" % h, [128, NT, 128], BF16) for h in range(2)]
        mstr = sb("mstr", [128, 4, 512], BF16)
        negtri = sb("negtri", [128, 128], BF16)
        cx.load(mstr[:], CD["mstrict"], (), ["mstr"], q="pool")
        cx.load(negtri[:], CD["negtri"], (), ["negtri"], q="pool")
        items = []
        for h in range(2):
            items += [(q[h], FO[12 + h], "q%d" % h, "F"), (k[h], FO[14 + h], "k%d" % h, "F"),
                      (v[h], TOv[:, :, 256 + 128 * h:384 + 128 * h], "v%d" % h, "T"),
                      (gbs[h], FO[16 + h], "gbs%d" % h, "F")]
        chunked_loads(cx, items)
        negones = sb("negones", [128, 128], BF16)
        cx.memset(negones[:], -1.0, ["negones"])
        zA = cx.banks([0, 1, 2, 3], "zA")
        oac = cx.banks([4, 5], "oac")
        csb = cx.banks([6, 7], "cs")
        Et = cx.rot_sb("Et", 2, [128, 512], F32, ls)
        spt = cx.rot_sb("spt", 4, [128, 512], BF16, ls)
        at = cx.rot_sb("at", 4, [128, 512], BF16, ls)
        Racc = cx.rot_sb("Racc", 3, [128, 512], F32, ls)
        zsb = cx.rot_sb("zsb", 2, [128, 512], F32, ls)
        ost = cx.rot_sb("ost", 2, [128, 512], BF16, ls)
        flat = []
        for c in range(NCH):
            for h in range(2):
                kts = list(range(4 * c + 3, -1, -1))
                for i, kt in enumerate(kts):
                    flat.append(dict(h=h, c=c, kt=kt, first=(i == 0), last=(kt == 0)))
        if prefetch is not None:
            prefetch()
        N = len(flat)
        st_ = [dict() for _ in range(N)]
        cur = {"R": None, "o": None}

        def stage1(s):
            t = flat[s]
            h, c, kt = t["h"], t["c"], t["kt"]
            tsl = slice(c * 512, (c + 1) * 512)
            ksl = slice(kt * 128, (kt + 1) * 128)
            a_t, a_k = zA.next()
            cx.mm(a_t[:], k[h][:, ksl], q[h][:, tsl], True, True, ["k%d_%d" % (h, kt // 4), "q%d_%d" % (h, c)], [a_k])
            E_t, E_k = Et.next()
            cx.actf(E_t[:], a_t[:], ACT.Exp, [a_k], [E_k])
            s_t, s_k = spt.next()
            cx.actf(s_t[:], E_t[:], ACT.Ln, [E_k], [s_k], bias=1.0)
            if kt >= 4 * c:
                cx.tt(s_t[:], s_t[:], mstr[:, kt - 4 * c, :], ALU.mult, [s_k, "mstr"], [s_k])
            st_[s]["sp"] = (s_t, s_k)
            st_[s]["zA"] = (a_t, a_k)

        def stage2(s):
            t = flat[s]
            h, c, kt = t["h"], t["c"], t["kt"]
            tsl = slice(c * 512, (c + 1) * 512)
            ksl = slice(kt * 128, (kt + 1) * 128)
            s_t, s_k = st_[s]["sp"]
            if t["first"]:
                cur["Rb"] = csb.next()
                cur["R"] = None
            Rb_t, Rb_k = cur["Rb"]
            R = cur["R"]
            b_t, b_k = st_[s]["zA"]
            cx.mm(b_t[:], negtri[:], s_t[:], False, True, ["negtri", s_k], [b_k])
            p_t, p_k = at.next()
            if R is None:
                cx.actf(p_t[:], b_t[:], ACT.Exp, [b_k], [p_k])
            else:
                zs_t, zs_k = zsb.next()
                cx.tt(zs_t[:], b_t[:], R[0][:], ALU.add, [b_k, R[1]], [zs_k])
                cx.actf(p_t[:], zs_t[:], ACT.Exp, [zs_k], [p_k])
            if kt >= 4 * c:
                cx.tt(p_t[:], p_t[:], mstr[:, kt - 4 * c, :], ALU.mult, [p_k, "mstr"], [p_k])
            st_[s]["a"] = (p_t, p_k)
            if not t["last"]:
                cx.mm(Rb_t[:], negones[:], s_t[:], t["first"], True, ["negones", s_k], [Rb_k])
                Rn_t, Rn_k = Racc.next()
                cx.copy(Rn_t[:], Rb_t[:], [Rb_k], [Rn_k])
                cur["R"] = (Rn_t, Rn_k)

        def stage3(s):
            t = flat[s]
            h, c, kt = t["h"], t["c"], t["kt"]
            tsl = slice(c * 512, (c + 1) * 512)
            if t["first"]:
                cur["o"] = oac.next()
            o_t, o_k = cur["o"]
            p_t, p_k = st_[s]["a"]
            cx.mm(o_t[:], v[h][:, kt, :], p_t[:], t["first"], t["last"], ["v%d_%d" % (h, kt // 4), p_k], [o_k])
            if t["last"]:
                s_o, s_ok = ost.next()
                cx.tt(s_o[:], o_t[:], gbs[h][:, tsl], ALU.mult, [o_k, "gbs%d_%d" % (h, c)], [s_ok])
                cx.store(OUT[c // 2, (2 + h) * 128:(3 + h) * 128, (c % 2) * 512:(c % 2) * 512 + 512], s_o[:], [s_ok],
                         [("OUT", 2 + h, c)], q="pool")
                if on_quarter is not None and h == 1 and c % 2 == 1:
                    on_quarter(c // 2, [("OUT", 2 + hh, cc_) for hh in range(2) for cc_ in (c - 1, c)])

        for s in range(N + 2):
            if s < N:
                stage1(s)
            if 0 <= s - 1 < N:
                stage2(s - 1)
            if 0 <= s - 2 < N:
                stage3(s - 2)
        cx.p.emit()


def phase_outproj_ln(cx, oT, xrows, wout, lng, lnb, XO, ntok):
    with ExitStack() as ls:
        sb = lambda n, s, d: cx.sb(n, s, d, ls)
        PB, PK = cx.PB, cx.PBK
        Wo = sb("Wo", [128, 16, D], BF16)
        wv = wout.rearrange("(k p) c -> p k c", p=128)
        for k in range(16):
            cx.load(Wo[:, k, :], wv[:, k, :], (), ["Wo%d" % k], q="pool")
        oTs = sb("oTs", [128, 16, ntok], BF16)
        ov = oT.rearrange("(k p) t -> p k t", p=128)
        for k4 in range(4):
            cx.load(oTs[:, 4 * k4:4 * k4 + 4, :], ov[:, 4 * k4:4 * k4 + 4, :], (), ["oTs%d" % k4], q="sp")
        gbc = sb("gbc", [128, D], F32)
        bbc = sb("bbc", [128, D], F32)
        cx.load(gbc[:], lng.partition_broadcast(128), (), ["gbc"], q="sp")
        cx.load(bbc[:], lnb.partition_broadcast(128), (), ["bbc"], q="sp")
        xs = cx.rot_sb("xs", 2, [128, D], F32, ls)
        zs = cx.rot_sb("zs", 2, [128, D], F32, ls)
        st6 = cx.rot_sb("st6", 2, [128, 4, 6], F32, ls)
        mv = cx.rot_sb("mv", 2, [128, 2], F32, ls)
        sml = cx.rot_sb("sml", 6, [128, 1], F32, ls)
        banks = cx.banks(list(range(8)), "y")
        for tt in range(ntok // 128):
            x_t, x_k = xs.next()
            cx.load(x_t[:], xrows[tt * 128:(tt + 1) * 128, :], (), [x_k], q="sp")
            z_t, z_k = zs.next()
            s_t, s_k = st6.next()
            for cg in range(4):
                b_t, b_k = banks.next()
                for k in range(16):
                    cx.mm(b_t[:], oTs[:, k, tt * 128:(tt + 1) * 128], Wo[:, k, cg * 512:(cg + 1) * 512],
                          k == 0, k == 15, ["oTs%d" % (k // 4), "Wo%d" % k], [b_k])
                zc = z_t[:, cg * 512:(cg + 1) * 512]
                cx.stt(zc, x_t[:, cg * 512:(cg + 1) * 512], ALPHA, b_t[:], ALU.mult, ALU.add, [x_k, b_k],
                       [z_k + "_%d" % cg])
                cx.p.add("dve", lambda e, o=s_t[:, cg, :], i=zc: e.bn_stats(out=o, in_=i), [z_k + "_%d" % cg],
                         [s_k + "_%d" % cg])
            m_t, m_k = mv.next()
            cx.p.add("dve", lambda e, o=m_t[:], i=s_t[:]: e.bn_aggr(out=o, in_=i),
                     [s_k + "_%d" % cg for cg in range(4)], [m_k])
            sd, sd_k = sml.next()
            cx.ts(sd[:], m_t[:, 1:2], 1e-5, ALU.add, [m_k], [sd_k])
            cx.actf(sd[:], sd[:], ACT.Sqrt, [sd_k], [sd_k])
            rs, rs_k = sml.next()
            cx.recip(rs[:], sd[:], [sd_k], [rs_k])
            nb, nb_k = sml.next()
            cx.stt(nb[:], m_t[:, 0:1], -1.0, rs[:], ALU.mult, ALU.mult, [m_k, rs_k], [nb_k])
            zk = [z_k + "_%d" % cg for cg in range(4)]
            cx.ts(z_t[:], z_t[:], rs[:], ALU.mult, zk + [rs_k, nb_k], zk, s2=nb[:], op1=ALU.add)
            cx.tt(z_t[:], z_t[:], gbc[:], ALU.mult, zk + ["gbc"], zk)
            cx.tt(z_t[:], z_t[:], bbc[:], ALU.add, zk + ["bbc"], zk)
            cx.store(XO[tt * 128:(tt + 1) * 128, :], z_t[:], zk, [("XO", tt)], q="sp")
        cx.p.emit()


def build_B(ntok=1024):
    nc = bass.Bass("TRN2", target_bir_lowering=False)
    with ExitStack() as st:
        cx = Ctx(nc, st)
        oT = cx.dram("oT", [D, ntok], BF16, "ExternalInput")
        xr = cx.dram("xr", [ntok, D], F32, "ExternalInput")
        wout = cx.dram("wout", [D, D], F32, "ExternalInput")
        lng = cx.dram("lng", [1, D], F32, "ExternalInput")
        lnb = cx.dram("lnb", [1, D], F32, "ExternalInput")
        XO = cx.dram("XO", [ntok, D], F32, "ExternalOutput")
        phase_outproj_ln(cx, oT, xr, wout, lng, lnb, XO, ntok)
    return nc


SPEC1 = ([("rope_scale", i, None) for i in range(4)] + [("rope", 4 + i, None) for i in range(4)]
         + [("silu", 8 + i, None) for i in range(4)])
NFO1 = 12


def l1_weights(w_in, g):
    cols = []
    for base in (0, 2048):
        for h in (2 * g, 2 * g + 1):
            for c in range(2):
                o = base + (2 * h + c) * 128
                cols.append(w_in[:, o:o + 128])
    for h in (2 * g, 2 * g + 1):
        for j in range(2):
            o = 6144 + h * 256 + j * 128
            cols.append(w_in[:, o:o + 128])
    wF = np.ascontiguousarray(np.concatenate(cols, 1))
    wT = np.ascontiguousarray(w_in[:, 4096 + 2 * g * 256: 4096 + (2 * g + 2) * 256])
    return wF, wT


def phase_diff_attn(cx, FO, TO, CD, lq1, lk1, lq2, lk2, gng, OUT, on_quarter=None, prefetch=None):
    with ExitStack() as ls:
        sb = lambda n, s, d: cx.sb(n, s, d, ls)
        PB, PK = cx.PB, cx.PBK
        TOv = TO.rearrange("(k p) c -> p k c", p=128)
        q = [sb("q%d" % i, [128, S], BF16) for i in range(4)]
        k = [sb("k%d" % i, [128, S], BF16) for i in range(4)]
        gt = [sb("gt%d" % i, [128, S], BF16) for i in range(4)]
        v = [sb("v%d" % h, [128, NT, 256], BF16) for h in range(2)]
        mcaus = sb("mcaus", [128, 4, 512], BF16)
        cx.load(mcaus[:], CD["bcaus"], (), ["mcaus"], q="pool")
        identb = sb("identb", [128, 128], BF16)
        cx.load(identb[:], CD["ident"], (), ["ident"], q="pool")
        items = []
        for h in range(2):
            for cc in range(2):
                i = 2 * h + cc
                items += [(q[i], FO[i], "q%d" % i, "F"), (k[i], FO[4 + i], "k%d" % i, "F")]
            items.append((v[h], TOv[:, :, 256 * h:256 * (h + 1)], "v%d" % h, "T"))
        for i in range(4):
            items.append((gt[i], FO[8 + i], "gt%d" % i, "F"))
        chunked_loads(cx, items, q="pool")
        ones = sb("ones", [128, 128], BF16)
        cx.memset(ones[:], 1.0, ["ones", "onesb"])
        lam = sb("lam", [128, 1], F32)
        ex = [sb("ex%d" % i, [128, 1], F32) for i in range(2)]
        for i, (a, b_) in enumerate(((lq1, lk1), (lq2, lk2))):
            la = sb("la%d" % i, [128, 128], F32)
            lb = sb("lb%d" % i, [128, 128], F32)
            cx.load(la[:], a.partition_broadcast(128), (), ["la%d" % i])
            cx.load(lb[:], b_.partition_broadcast(128), (), ["lb%d" % i])
            cx.tt(la[:], la[:], lb[:], ALU.mult, ["la%d" % i, "lb%d" % i], ["la%d" % i])
            cx.p.add("dve", lambda e, o=ex[i][:], i_=la[:]: e.reduce_sum(out=o, in_=i_, axis=AX.X),
                     ["la%d" % i], ["ex%d" % i])
            cx.actf(ex[i][:], ex[i][:], ACT.Exp, ["ex%d" % i], ["ex%d" % i])
        cx.ts(lam[:], ex[0][:], ex[1][:], ALU.subtract, ["ex0", "ex1"], ["lam"], s2=-LAMBDA_INIT, op1=ALU.subtract)
        cx.ts(lam[:], lam[:], -1.0, ALU.mult, ["lam"], ["lam"])
        gsc = sb("gsc", [128, 4], F32)
        cx.load(gsc[:], gng, (), ["gsc"])
        cx.ts(gsc[:], gsc[:], 1.0 - LAMBDA_INIT, ALU.mult, ["gsc"], ["gsc"])
        zrot = cx.banks([0, 1], "z")
        sets = Rot("sets", [(4, 5, 2), (6, 7, 3)])
        erot = cx.rot_sb("e", 4, [128, 512], BF16, ls)
        esrot = cx.rot_sb("es", 3, [128, 512], F32, ls)
        hlrot = cx.rot_sb("hl", 6, [128, 512], BF16, ls)
        rd = cx.rot_sb("rd", 3, [128, 512], F32, ls)
        oA = cx.rot_sb("oA", 2, [128, 512], F32, ls)
        oB = cx.rot_sb("oB", 2, [128, 512], F32, ls)
        tm = cx.rot_sb("tm", 2, [128, 512], F32, ls)
        sq = cx.rot_sb("sq", 2, [128, 512], BF16, ls)
        ost = cx.rot_sb("ost", 2, [128, 512], BF16, ls)
        groups = []
        state = {}
        if prefetch is not None:
            prefetch()
        for c in range(NCH):
            for h in range(2):
                tsl = slice(c * 512, (c + 1) * 512)
                for cc in range(2):
                    qi = 2 * h + cc
                    tiles = []
                    for kt in range(4 * c + 4):
                        mk = mcaus[:, kt - 4 * c, :] if kt >= 4 * c else None
                        tiles.append(dict(kT=k[qi][:, kt * 128:(kt + 1) * 128], kkeys=["k%d_%d" % (qi, kt // 4)],
                                          vs=[v[h][:, kt, 0:128], v[h][:, kt, 128:256]],
                                          vkeys=["v%d_%d" % (h, kt // 4)],
                                          maskb=mk, mkeys=["mcaus"]))
                    (ia, ib, idn), _ = sets.next()

                    def epi(h=h, c=c, cc=cc, ia=ia, ib=ib, idn=idn, tsl=tsl):
                        def part1():
                            if cc == 0:
                                state["oh"] = [oA.next(), oB.next()]
                            oh = state["oh"]
                            r_t, r_k = rd.next()
                            cx.actf(r_t[:], PB[idn][:], ACT.Ln, [PK[idn]], [r_k])
                            cx.actf(r_t[:], r_t[:], ACT.Exp, [r_k], [r_k], scale=-1.0)
                            for j, ib_ in enumerate((ia, ib)):
                                o_t, o_k = oh[j]
                                if cc == 0:
                                    cx.tt(o_t[:], PB[ib_][:], r_t[:], ALU.mult, [PK[ib_], r_k], [o_k])
                                else:
                                    t_t, t_k = tm.next()
                                    cx.tt(t_t[:], PB[ib_][:], r_t[:], ALU.mult, [PK[ib_], r_k], [t_k])
                                    cx.stt(o_t[:], t_t[:], lam[:], o_t[:], ALU.mult, ALU.add, [t_k, "lam", o_k],
                                           [o_k])
                            return oh
                        if cc == 0:
                            def d0():
                                part1()
                                return None
                            return d0

                        def d1():
                            oh = part1()
                            return (lambda: part2(oh, h, c, tsl, idn), 8)
                        return d1

                    def part2(oh, h, c, tsl, idn):
                        sqs = []
                        for j in range(2):
                            s_t, s_k = sq.next()
                            cx.tt(s_t[:], oh[j][0][:], oh[j][0][:], ALU.mult, [oh[j][1]], [s_k])
                            sqs.append((s_t, s_k))
                        if True:
                            z_t, z_k = PB[idn], PK[idn]
                            for j in range(2):
                                cx.mm(z_t[:], ones[:], sqs[j][0][:], j == 0, j == 1, ["ones", sqs[j][1]], [z_k])
                            r2, r2_k = rd.next()
                            cx.ts(r2[:], z_t[:], 1.0 / 256.0, ALU.mult, [z_k], [r2_k], s2=1e-5, op1=ALU.add)
                            cx.actf(r2[:], r2[:], ACT.Ln, [r2_k], [r2_k])
                            cx.actf(r2[:], r2[:], ACT.Exp, [r2_k], [r2_k], scale=-0.5)
                            for j in range(2):
                                o_t, o_k = oh[j]
                                cx.stt(o_t[:], o_t[:], gsc[:, 2 * h + j:2 * h + j + 1], r2[:], ALU.mult, ALU.mult,
                                       [o_k, "gsc", r2_k], [o_k])
                                s_o, s_ok = ost.next()
                                cx.tt(s_o[:], o_t[:], gt[2 * h + j][:, tsl], ALU.mult, [o_k, "gt%d_%d" % (2 * h + j, c)],
                                      [s_ok])
                                cx.store(OUT[c // 2, (2 * h + j) * 128:(2 * h + j + 1) * 128,
                                             (c % 2) * 512:(c % 2) * 512 + 512],
                                         s_o[:], [s_ok], [("OUT", 2 * h + j, c)], q="sp")
                        if on_quarter is not None and h == 1 and c % 2 == 1:
                            on_quarter(c // 2, [("OUT", rb, cc_) for rb in range(4) for cc_ in (c - 1, c)])

                    groups.append(dict(q_ap=q[qi][:, tsl], q_keys=["q%d_%d" % (qi, c)], tiles=tiles,
                                       oaccs=[(PB[ia][:], PK[ia]), (PB[ib][:], PK[ib])],
                                       den=(PB[idn][:], PK[idn]), epilogue=epi))
        run_attn_stream(cx, groups, zrot, erot, esrot, identb[:], ones[:], hlrot, LA=1, defer=2)
        cx.p.emit()


def build_C(level=9):
    nc = bass.Bass("TRN2", target_bir_lowering=False)
    with ExitStack() as st:
        cx = Ctx(nc, st)
        C = consts()
        xT = cx.dram("xT", [D, S], F32, "ExternalInput")
        wF = cx.dram("wF", [D, 12 * 128], F32, "ExternalInput")
        wT = cx.dram("wT", [D, 512], F32, "ExternalInput")
        lq1 = cx.dram("lq1", [1, 128], F32, "ExternalInput")
        lk1 = cx.dram("lk1", [1, 128], F32, "ExternalInput")
        lq2 = cx.dram("lq2", [1, 128], F32, "ExternalInput")
        lk2 = cx.dram("lk2", [1, 128], F32, "ExternalInput")
        gng = cx.dram("gng", [128, 4], F32, "ExternalInput")
        CD = {n: cx.dram("c_" + n, list(C[n].shape), F32, "ExternalInput")
              for n in ("ropec", "ropes", "pswap", "bcaus", "ident")}
        dbg = "ExternalOutput" if level < 9 else None
        FO = cx.dram("FO", [NFO1, 128, S], BF16, dbg)
        TO = cx.dram("TO", [S, 512], BF16, dbg)
        OUT = cx.dram("OUT", [4, 512, 1024], BF16, "ExternalOutput")
        phase_proj(cx, xT, wF, wT, None, CD["ropec"], CD["ropes"], CD["pswap"], FO, TO, None, SPEC1, 0)
        phase_diff_attn(cx, FO, TO, CD, lq1, lk1, lq2, lk2, gng, OUT)
    return nc


def inputs_C(x1T_b, od, g):
    C = consts()
    wF, wT = l1_weights(od["w_in"], g)
    gsl = od["gn_g"][2 * g * 256:(2 * g + 2) * 256]
    m = {"xT": x1T_b, "wF": wF, "wT": wT,
         "lq1": od["lq1"][None, :], "lk1": od["lk1"][None, :], "lq2": od["lq2"][None, :], "lk2": od["lk2"][None, :],
         "gng": np.ascontiguousarray(gsl.reshape(4, 128).T)}
    for n in ("ropec", "ropes", "pswap", "bcaus", "ident"):
        m["c_" + n] = C[n]
    return m


_NC = {}


def _get(name, fn):
    if name not in _NC:
        _NC[name] = fn()
    return _NC[name]


def _gather_oT(res, b, r, order):
    blocks = [None] * 16
    for g in range(4):
        O = np.asarray(res[4 * b + g]["OUT"])
        for i in range(4):
            blocks[order(g, i)] = O[i * 128:(i + 1) * 128, r * 1024:(r + 1) * 1024]
    return np.ascontiguousarray(np.concatenate(blocks, 0))


def kernel(x, ev_w_in, ev_pe_k, ev_pe_v, ev_w1_k, ev_w2_k, ev_w1_v, ev_w2_v, ev_w_out, ev_ln_g, ev_ln_b,
           od_w_in, od_lq1, od_lk1, od_lq2, od_lk2, od_gn_g, od_w_out, od_ln_g, od_ln_b):
    x = np.asarray(x, dtype=np.float32)
    ev = dict(w_in=np.asarray(ev_w_in)[0], pe_k=np.asarray(ev_pe_k)[0], pe_v=np.asarray(ev_pe_v)[0],
              w1_k=np.asarray(ev_w1_k)[0], w2_k=np.asarray(ev_w2_k)[0], w1_v=np.asarray(ev_w1_v)[0],
              w2_v=np.asarray(ev_w2_v)[0])
    od = dict(w_in=np.asarray(od_w_in)[0], lq1=np.asarray(od_lq1)[0], lk1=np.asarray(od_lk1)[0],
              lq2=np.asarray(od_lq2)[0], lk2=np.asarray(od_lk2)[0], gn_g=np.asarray(od_gn_g)[0])
    cores = list(range(8))
    ncA = _get("A", build_A)
    resA = run_bass_kernel_spmd(ncA, [inputs_A(x, ev, c) for c in cores], core_ids=cores).results
    ordA = lambda g, i: (2 * g + i) if i < 2 else (8 + 2 * g + (i - 2))
    ncB = _get("B", build_B)
    inB = []
    for c in cores:
        b, r = c // 4, c % 4
        inB.append({"oT": _gather_oT(resA, b, r, ordA), "xr": np.ascontiguousarray(x[b, r * 1024:(r + 1) * 1024]),
                    "wout": np.asarray(ev_w_out)[0], "lng": np.asarray(ev_ln_g), "lnb": np.asarray(ev_ln_b)})
    resB = run_bass_kernel_spmd(ncB, inB, core_ids=cores).results
    x1 = np.stack([np.concatenate([np.asarray(resB[4 * b + r]["XO"]) for r in range(4)], 0) for b in range(2)], 0)
    ncC = _get("C", build_C)
    x1T = [np.ascontiguousarray(x1[b].T) for b in range(2)]
    resC = run_bass_kernel_spmd(ncC, [inputs_C(x1T[c // 4], od, c % 4) for c in cores], core_ids=cores).results
    ordC = lambda g, i: 4 * g + i
    inD = []
    for c in cores:
        b, r = c // 4, c % 4
        inD.append({"oT": _gather_oT(resC, b, r, ordC), "xr": np.ascontiguousarray(x1[b, r * 1024:(r + 1) * 1024]),
                    "wout": np.asarray(od_w_out)[0], "lng": np.asarray(od_ln_g), "lnb": np.asarray(od_ln_b)})
    resD = run_bass_kernel_spmd(ncB, inD, core_ids=cores).results
    out = np.stack([np.concatenate([np.asarray(resD[4 * b + r]["XO"]) for r in range(4)], 0) for b in range(2)], 0)
    return out.astype(np.float32)


I32 = mybir.dt.int32


def load_wo(cx, Wo, wout):
    wv = wout.rearrange("(k p) c -> p k c", p=128)
    for cg in range(4):
        for kh in range(2):
            cx.load(Wo[:, 8 * kh:8 * kh + 8, cg * 512:(cg + 1) * 512], wv[:, 8 * kh:8 * kh + 8, cg * 512:(cg + 1) * 512],
                    (), ["Wo%d" % cg], q="pool")


def phase_outproj_ln2(cx, G, roff, xrows, wout, lng, lnb, XO, identD=None, X1T=None, ntok=1024, Wo_pre=None,
                      on_piece=None):
    with ExitStack() as ls:
        sb = lambda n, s, d: cx.sb(n, s, d, ls)
        ri = sb("ri", [1, 1], I32)
        cx.load(ri[:], roff, (), ["ri"], q="pool")
        oTs = sb("oTs", [128, 16, ntok], BF16)
        Gv = G.rearrange("j (k p) t -> p (j k) t", p=128)

        def dyn_load(k4):
            def fn(e):
                with e.register("roff%d_%d" % (cx.p.nphase, k4)) as rr:
                    e.reg_load(rr, ri[0:1, 0:1])
                    v = e.snap(rr)
                    return e.dma_start(out=oTs[:, 4 * k4:4 * k4 + 4, :], in_=Gv[:, bass.ds(v + 4 * k4, 4), :])
            return fn

        import os
        if Wo_pre is None:
            Wo = sb("Wo", [128, 16, D], BF16)
            load_wo(cx, Wo, wout)
        else:
            Wo = Wo_pre
        for k4 in range(4):
            if os.environ.get("MK_STATIC"):
                cx.load(oTs[:, 4 * k4:4 * k4 + 4, :], Gv[:, 4 * k4:4 * k4 + 4, :], ["G"], ["oTs%d" % k4], q="pool")
            else:
                cx.p.add("pool", dyn_load(k4), ["ri", "G0", "G1", "G2", "G3"], ["oTs%d" % k4], dma=True)
        gbc = sb("gbc", [128, D], F32)
        bbc = sb("bbc", [128, D], F32)
        cx.load(gbc[:], lng.partition_broadcast(128), (), ["gbc"], q="sp")
        cx.load(bbc[:], lnb.partition_broadcast(128), (), ["bbc"], q="sp")
        if X1T is not None:
            idb = sb("idb", [128, 128], BF16)
            cx.load(idb[:], identD, (), ["idb"], q="pool")
            x1Ts = sb("x1Ts", [128, 16, ntok], BF16)
            zb = cx.rot_sb("zb", 2, [128, D], BF16, ls)
        xs = cx.rot_sb("xs", 2, [128, D], F32, ls)
        zs = cx.rot_sb("zs", 2, [128, D], F32, ls)
        st6 = cx.rot_sb("st6", 2, [128, 4, 6], F32, ls)
        mv = cx.rot_sb("mv", 2, [128, 2], F32, ls)
        sml = cx.rot_sb("sml", 6, [128, 1], F32, ls)
        banks = cx.banks(list(range(8)), "y")
        for tt in range(ntok // 128):
            x_t, x_k = xs.next()
            cx.load(x_t[:], xrows[tt * 128:(tt + 1) * 128, :], ["X1R"], [x_k], q="sp")
            z_t, z_k = zs.next()
            s_t, s_k = st6.next()
            for cg in range(4):
                b_t, b_k = banks.next()
                for k in range(16):
                    cx.mm(b_t[:], oTs[:, k, tt * 128:(tt + 1) * 128], Wo[:, k, cg * 512:(cg + 1) * 512],
                          k == 0, k == 15, ["oTs%d" % (k // 4), "Wo%d" % cg], [b_k])
                zc = z_t[:, cg * 512:(cg + 1) * 512]
                cx.stt(zc, x_t[:, cg * 512:(cg + 1) * 512], ALPHA, b_t[:], ALU.mult, ALU.add, [x_k, b_k],
                       [z_k + "_%d" % cg])
                cx.p.add("dve", lambda e, o=s_t[:, cg, :], i=zc: e.bn_stats(out=o, in_=i), [z_k + "_%d" % cg],
                         [s_k + "_%d" % cg])
            m_t, m_k = mv.next()
            cx.p.add("dve", lambda e, o=m_t[:], i=s_t[:]: e.bn_aggr(out=o, in_=i),
                     [s_k + "_%d" % cg for cg in range(4)], [m_k])
            sd, sd_k = sml.next()
            cx.ts(sd[:], m_t[:, 1:2], 1e-5, ALU.add, [m_k], [sd_k])
            cx.actf(sd[:], sd[:], ACT.Sqrt, [sd_k], [sd_k])
            rs, rs_k = sml.next()
            cx.recip(rs[:], sd[:], [sd_k], [rs_k])
            nb, nb_k = sml.next()
            cx.stt(nb[:], m_t[:, 0:1], -1.0, rs[:], ALU.mult, ALU.mult, [m_k, rs_k], [nb_k])
            zk = [z_k + "_%d" % cg for cg in range(4)]
            cx.ts(z_t[:], z_t[:], rs[:], ALU.mult, zk + [rs_k, nb_k], zk, s2=nb[:], op1=ALU.add)
            cx.tt(z_t[:], z_t[:], gbc[:], ALU.mult, zk + ["gbc"], zk)
            cx.tt(z_t[:], z_t[:], bbc[:], ALU.add, zk + ["bbc"], zk)
            cx.store(XO[tt * 128:(tt + 1) * 128, :], z_t[:], zk, [("XO", tt)], q="sp")
            if X1T is not None:
                zb_t, zb_k = zb.next()
                cx.actf(zb_t[:], z_t[:], ACT.Copy, zk, [zb_k])
                for k4 in range(4):
                    b_t, b_k = banks.next()
                    for j in range(4):
                        kk = 4 * k4 + j
                        cx.mm(b_t[:, j * 128:(j + 1) * 128], zb_t[:, kk * 128:(kk + 1) * 128], idb[:], True, True,
                              [zb_k, "idb"], [b_k])
                    cx.actf(x1Ts[:, 4 * k4:4 * k4 + 4, tt * 128:(tt + 1) * 128],
                            b_t[:].rearrange("p (j t) -> p j t", j=4), ACT.Copy, [b_k], ["x1Ts%d_%d" % (tt, k4)])
                X1Tv = X1T.rearrange("j (k p) t -> p j k t", p=128)
                cx.store(X1Tv[:, tt // 2, :, (tt % 2) * 128:(tt % 2) * 128 + 128], x1Ts[:, :, tt * 128:(tt + 1) * 128],
                         ["x1Ts%d_%d" % (tt, k4) for k4 in range(4)], [("X1T", tt)], q="sp")
                if on_piece is not None and tt % 2 == 1:
                    on_piece(tt // 2, [("X1T", tt - 1), ("X1T", tt)])
        cx.p.emit()


def allgather(cx, src, dst, name):
    for j in range(4):
        cx.p.collective(cx.st, lambda e, s=src[j], d=dst[j]: e.collective_compute(
            "AllGather", ALU.bypass, replica_groups=[[0, 1, 2, 3], [4, 5, 6, 7]], ins=[s], outs=[d]),
            (), ["%s%d" % (name, j)])


def allgather_j(cx, src, dst, name, j, reads):
    cx.p.collective(cx.st, lambda e, s=src[j], d=dst[j]: e.collective_compute(
        "AllGather", ALU.bypass, replica_groups=[[0, 1, 2, 3], [4, 5, 6, 7]], ins=[s], outs=[d]),
        list(reads), ["%s%d" % (name, j)])


CONST_F = CONST_A


def build_fused():
    nc = bass.Bass("TRN2", target_bir_lowering=False)
    with ExitStack() as st:
        cx = Ctx(nc, st)
        C = consts()
        ein = lambda n, s, d=F32: cx.dram(n, s, d, "ExternalInput")
        xT = ein("xT", [D, S])
        xr = ein("xr", [1024, D])
        roff = ein("roff", [1, 1], I32)
        wF = ein("wF", [D, 16 * 128])
        wT = ein("wT", [D, 512])
        wG = ein("wG", [D, 6])
        w1k = ein("w1k", [4096, 128])
        w1v = ein("w1v", [4096, 128])
        w2k = ein("w2k", [128, 128])
        w2v = ein("w2v", [128, 128])
        pekT = ein("pekT", [128, 32])
        pevT = ein("pevT", [128, 32])
        wo0 = ein("wo0", [D, D])
        lng0 = ein("lng0", [1, D])
        lnb0 = ein("lnb0", [1, D])
        wF1 = ein("wF1", [D, 12 * 128])
        wT1 = ein("wT1", [D, 512])
        lq1 = ein("lq1", [1, 128])
        lk1 = ein("lk1", [1, 128])
        lq2 = ein("lq2", [1, 128])
        lk2 = ein("lk2", [1, 128])
        gng = ein("gng", [128, 4])
        wo1 = ein("wo1", [D, D])
        lng1 = ein("lng1", [1, D])
        lnb1 = ein("lnb1", [1, D])
        CD = {n: ein("c_" + n, list(C[n].shape)) for n in CONST_F}
        FO = cx.dram("FO", [NFO0, 128, S], BF16)
        TO = cx.dram("TO", [S, 512], BF16)
        GO = cx.dram("GO", [8, S], F32)
        KCMP = cx.dram("KCMP", [128, 256], BF16)
        VCMP = cx.dram("VCMP", [128, 2, 128], BF16)
        SELB = cx.dram("SELB", [64, S], BF16)
        OUT0 = cx.dram("OUT0", [4, 512, 1024], BF16)
        G0 = cx.dram("G0", [4, 4 * 512, 1024], BF16)
        X1R = cx.dram("X1R", [1024, D], F32)
        X1T = cx.dram("X1T", [4, D, 256], BF16)
        X1G = cx.dram("X1G", [4, 4 * D, 256], BF16)
        FO1 = cx.dram("FO1", [NFO1, 128, S], BF16)
        TO1 = cx.dram("TO1", [S, 512], BF16)
        OUT1 = cx.dram("OUT1", [4, 512, 1024], BF16)
        G1 = cx.dram("G1", [4, 4 * 512, 1024], BF16)
        XO = cx.dram("XO", [1024, D], F32, "ExternalOutput")
        phase_proj(cx, xT, wF, wT, wG, CD["ropec"], CD["ropes"], CD["pswap"], FO, TO, GO, SPEC0, 6)
        phase_nsa_prep(cx, FO, w1k, w1v, w2k, w2v, pekT, pevT, CD, KCMP, VCMP, SELB)
        phase_nsa_attn(cx, FO, TO, GO, CD, KCMP, VCMP, SELB, OUT0)
        with ExitStack() as span:
            Wo0 = cx.sb("Wo0", [128, 16, D], BF16, span)
            phase_sb_attn(cx, FO, TO, CD, OUT0,
                          on_quarter=lambda j, reads: allgather_j(cx, OUT0, G0, "G", j, reads),
                          prefetch=lambda: load_wo(cx, Wo0, wo0))
            phase_outproj_ln2(cx, G0, roff, xr, wo0, lng0, lnb0, X1R, CD["ident"], X1T, Wo_pre=Wo0,
                              on_piece=lambda j, reads: allgather_j(cx, X1T, X1G, "X1G", j, reads))
        X1Gv = X1G.rearrange("j (r k p) t -> p j r k t", r=4, p=128)
        xsrc = lambda c, k4: [((jj * 256, jj * 256 + 256), X1Gv[:, (c % 2) * 2 + jj, c // 2, 4 * k4:4 * k4 + 4, :])
                              for jj in range(2)]
        phase_proj(cx, None, wF1, wT1, None, CD["ropec"], CD["ropes"], CD["pswap"], FO1, TO1, None, SPEC1, 0,
                   xsrc=xsrc)
        phase_diff_attn(cx, FO1, TO1, CD, lq1, lk1, lq2, lk2, gng, OUT1,
                        on_quarter=lambda j, reads: allgather_j(cx, OUT1, G1, "G", j, reads))
        phase_outproj_ln2(cx, G1, roff, X1R, wo1, lng1, lnb1, XO)
    return nc


def _perm_wout0(w):
    rows = []
    for g in range(4):
        for i in range(4):
            blk = (2 * g + i) if i < 2 else (8 + 2 * g + (i - 2))
            rows.append(w[blk * 128:(blk + 1) * 128])
    return np.ascontiguousarray(np.concatenate(rows, 0))


def kernel(x, ev_w_in, ev_pe_k, ev_pe_v, ev_w1_k, ev_w2_k, ev_w1_v, ev_w2_v, ev_w_out, ev_ln_g, ev_ln_b,
           od_w_in, od_lq1, od_lk1, od_lq2, od_lk2, od_gn_g, od_w_out, od_ln_g, od_ln_b):
    x = np.asarray(x, dtype=np.float32)
    ev = dict(w_in=np.asarray(ev_w_in)[0], pe_k=np.asarray(ev_pe_k)[0], pe_v=np.asarray(ev_pe_v)[0],
              w1_k=np.asarray(ev_w1_k)[0], w2_k=np.asarray(ev_w2_k)[0], w1_v=np.asarray(ev_w1_v)[0],
              w2_v=np.asarray(ev_w2_v)[0])
    od = dict(w_in=np.asarray(od_w_in)[0], lq1=np.asarray(od_lq1)[0], lk1=np.asarray(od_lk1)[0],
              lq2=np.asarray(od_lq2)[0], lk2=np.asarray(od_lk2)[0], gn_g=np.asarray(od_gn_g)[0])
    C = consts()
    wo0 = _perm_wout0(np.asarray(ev_w_out)[0])
    xTs = [np.ascontiguousarray(x[b].T) for b in range(2)]
    cores = list(range(8))
    maps = []
    for c in cores:
        b, g = c // 4, c % 4
        m = inputs_A(x, ev, c)
        m["xT"] = xTs[b]
        m["xr"] = np.ascontiguousarray(x[b, g * 1024:(g + 1) * 1024])
        m["roff"] = np.array([[g * 16]], np.int32)
        m["wo0"] = wo0
        m["lng0"] = np.asarray(ev_ln_g, np.float32).reshape(1, D)
        m["lnb0"] = np.asarray(ev_ln_b, np.float32).reshape(1, D)
        mc = inputs_C(None, od, g)
        m["wF1"], m["wT1"] = mc["wF"], mc["wT"]
        for n in ("lq1", "lk1", "lq2", "lk2", "gng"):
            m[n] = mc[n]
        m["wo1"] = np.asarray(od_w_out)[0]
        m["lng1"] = np.asarray(od_ln_g, np.float32).reshape(1, D)
        m["lnb1"] = np.asarray(od_ln_b, np.float32).reshape(1, D)
        maps.append(m)
    nc = _get("F", build_fused)
    res = run_bass_kernel_spmd(nc, maps, core_ids=cores).results
    out = np.stack([np.concatenate([np.asarray(res[4 * b + r]["XO"]) for r in range(4)], 0) for b in range(2)], 0)
    return out.astype(np.float32)
```

```python
import math
from contextlib import ExitStack
import numpy as np
import concourse.bass as bass
import concourse.mybir as mybir
from concourse.bass_utils import run_bass_kernel_spmd

ACT = mybir.ActivationFunctionType
ALU = mybir.AluOpType
AX = mybir.AxisListType
F32 = mybir.dt.float32
BF16 = mybir.dt.bfloat16

D = 2048
S = 4096
NCH = 8
NT = 32
SCALE = 128 ** -0.5
NEG = -32768.0
ALPHA = 4.0 ** 0.25
LAMBDA_INIT = 0.8 - 0.6 * math.exp(-0.3)


class _Op:
    __slots__ = ("eng", "fn", "deps", "isdma", "sem", "val", "sig", "waits", "prewait", "ccsem")


class Prog:
    COMPUTE = ("pe", "act", "dve", "pool")
    QUEUES = ("sp", "pool")
    NDMASEM = 8

    def __init__(self, nc, st):
        self.nc = nc
        self.ops = []
        self.last_w = {}
        self.readers = {}
        self.sems = {e: st.enter_context(nc.semaphore("p_" + e)) for e in self.COMPUTE}
        self.dsems = {q: [st.enter_context(nc.semaphore("d_%s%d" % (q, i))) for i in range(self.NDMASEM)]
                      for q in self.QUEUES}
        self.cnt = {e: 0 for e in self.COMPUTE}
        self.dcnt = {q: 0 for q in self.QUEUES}
        self.nphase = 0

    def add(self, eng, fn, reads=(), writes=(), dma=False):
        op = _Op()
        op.eng = eng
        op.fn = fn
        op.isdma = dma
        op.sig = False
        op.ccsem = None
        deps = []
        for t in reads:
            w = self.last_w.get(t)
            if w is not None:
                deps.append(w)
        for t in writes:
            w = self.last_w.get(t)
            if w is not None:
                deps.append(w)
            last = {}
            for r in self.readers.get(t, ()):
                if r.isdma:
                    deps.append(r)
                else:
                    last[r.eng] = r
            deps.extend(last.values())
        op.deps = deps
        for t in reads:
            self.readers.setdefault(t, []).append(op)
        for t in writes:
            self.last_w[t] = op
            self.readers[t] = []
        self.ops.append(op)
        return op

    def dma(self, q, out, in_, reads=(), writes=()):
        return self.add(q, lambda e, o=out, i=in_: e.dma_start(out=o, in_=i), reads, writes, dma=True)

    def collective(self, st, fn, reads=(), writes=()):
        op = self.add("pool", fn, reads, writes, dma=True)
        self.ncc = getattr(self, "ncc", 0) + 1
        op.ccsem = st.enter_context(self.nc.semaphore("cc%d" % self.ncc))
        return op

    def emit(self):
        nc = self.nc
        ops = self.ops
        for op in ops:
            for d in op.deps:
                if d.isdma:
                    continue
                if d.eng == "pe" and op.eng == "pe" and not op.isdma:
                    continue
                d.sig = True
        for op in ops:
            op.prewait = None
            if op.ccsem is not None:
                op.sem = op.ccsem
                op.val = 1
            elif op.isdma:
                i = self.dcnt[op.eng]
                self.dcnt[op.eng] += 1
                s = self.dsems[op.eng][i % self.NDMASEM]
                n = i // self.NDMASEM
                op.sem = s
                op.val = 16 * (n + 1)
                if n > 0:
                    op.prewait = (s, 16 * n)
            elif op.sig:
                self.cnt[op.eng] += 1
                op.sem = self.sems[op.eng]
                op.val = self.cnt[op.eng]
            else:
                op.sem = None
                op.val = 0
        known = {e: {} for e in ("pe", "act", "dve", "pool", "sp")}
        clocks = {}
        for op in ops:
            kn = known[op.eng]
            waits = {}

            def need(sem, val, clk):
                if kn.get(sem, 0) >= val:
                    return
                if waits.get(sem, 0) < val:
                    waits[sem] = val
                kn[sem] = val
                if clk is not None:
                    for s2, v2 in clk.items():
                        if kn.get(s2, 0) < v2:
                            kn[s2] = v2

            if op.prewait is not None:
                need(op.prewait[0], op.prewait[1], None)
            for d in op.deps:
                if d.sem is None:
                    continue
                if (not d.isdma) and d.eng == "pe" and op.eng == "pe" and not op.isdma:
                    continue
                need(d.sem, d.val, clocks.get(id(d)))
            op.waits = list(waits.items())
            if op.sem is not None:
                clk = dict(kn)
                clk[op.sem] = op.val
                clocks[id(op)] = clk
        fw = []
        for q in self.QUEUES:
            for i in range(self.NDMASEM):
                tot = (self.dcnt[q] - i + self.NDMASEM - 1) // self.NDMASEM
                if tot > 0:
                    fw.append((self.dsems[q][i], 16 * tot))
        for op in ops:
            if op.ccsem is not None:
                fw.append((op.ccsem, 1))
        engmap = {"pe": "tensor", "act": "scalar", "dve": "vector", "pool": "gpsimd", "sp": "sync"}
        with nc.named_scope("phase%d" % self.nphase), nc.Block("ph%d" % self.nphase) as block:
            for ename, bname in engmap.items():
                mine = [o for o in ops if o.eng == ename]
                if not mine and ename != "sp":
                    continue

                def body(eng, mine=mine, ename=ename):
                    for o in mine:
                        for s, v in o.waits:
                            eng.wait_ge(s, v)
                        ins = o.fn(eng)
                        if o.ccsem is not None:
                            ins.then_inc(o.sem)
                        elif o.sem is not None:
                            ins.then_inc(o.sem, 16 if o.isdma else 1)
                    if ename == "sp":
                        for s, v in fw:
                            eng.wait_ge(s, v)

                getattr(block, bname)(body)
        self.nphase += 1
        self.ops = []
        self.last_w = {}
        self.readers = {}


class Rot:
    def __init__(self, name, tensors, keys=None):
        self.name = name
        self.t = tensors
        self.k = keys or ["%s%d" % (name, i) for i in range(len(tensors))]
        self.i = 0

    def next(self):
        k = self.i % len(self.t)
        self.i += 1
        return self.t[k], self.k[k]


class Ctx:
    def __init__(self, nc, st):
        self.nc = nc
        self.st = st
        self.p = Prog(nc, st)
        self.PB = [self.ps("PB%d" % i) for i in range(8)]
        self.PBK = ["PB%d" % i for i in range(8)]

    def sb(self, name, shape, dt, st=None):
        self._uid = getattr(self, "_uid", 0) + 1
        return (st or self.st).enter_context(self.nc.sbuf_tensor("%s_u%d" % (name, self._uid), list(shape), dt))

    def banks(self, idx, name):
        return Rot(name, [self.PB[i] for i in idx], [self.PBK[i] for i in idx])

    def ps(self, name, shape=(128, 512), dt=F32):
        return self.st.enter_context(self.nc.psum_tensor(name, list(shape), dt))

    def rot_sb(self, name, n, shape, dt, st=None):
        return Rot(name, [self.sb("%s_%d" % (name, i), shape, dt, st) for i in range(n)])

    def dram(self, name, shape, dt, kind=None):
        if kind is None:
            return self.nc.dram_tensor(name, list(shape), dt).ap()
        return self.nc.dram_tensor(name, list(shape), dt, kind=kind).ap()

    def mm(self, out, lhsT, rhs, start, stop, reads, writes):
        self.p.add("pe", lambda e, o=out, l=lhsT, r=rhs, s=start, t=stop: e.matmul(o, l, r, start=s, stop=t),
                   reads, writes)

    def actf(self, out, in_, func, reads, writes, bias=None, scale=None, accum_out=None):
        kw = {}
        if bias is not None:
            kw["bias"] = bias
        if scale is not None:
            kw["scale"] = scale
        if accum_out is not None:
            kw["accum_out"] = accum_out
        self.p.add("act", lambda e, o=out, i=in_, f=func, kw=kw: e.activation(out=o, in_=i, func=f, **kw),
                   reads, writes)

    def tt(self, out, in0, in1, op, reads, writes, eng="dve"):
        self.p.add(eng, lambda e, o=out, a=in0, b=in1, op=op: e.tensor_tensor(out=o, in0=a, in1=b, op=op),
                   reads, writes)

    def ts(self, out, in0, s1, op0, reads, writes, s2=None, op1=None, eng="dve"):
        if op1 is None:
            self.p.add(eng, lambda e, o=out, a=in0, s1=s1, op0=op0: e.tensor_scalar(
                out=o, in0=a, scalar1=s1, scalar2=None, op0=op0), reads, writes)
        else:
            self.p.add(eng, lambda e, o=out, a=in0, s1=s1, s2=s2, op0=op0, op1=op1: e.tensor_scalar(
                out=o, in0=a, scalar1=s1, scalar2=s2, op0=op0, op1=op1), reads, writes)

    def stt(self, out, in0, scalar, in1, op0, op1, reads, writes):
        self.p.add("dve", lambda e, o=out, a=in0, s=scalar, b=in1, op0=op0, op1=op1: e.scalar_tensor_tensor(
            out=o, in0=a, scalar=s, in1=b, op0=op0, op1=op1), reads, writes)

    def copy(self, out, in_, reads, writes, eng="dve"):
        self.p.add(eng, lambda e, o=out, i=in_: e.tensor_copy(out=o, in_=i), reads, writes)

    def recip(self, out, in_, reads, writes):
        self.p.add("dve", lambda e, o=out, i=in_: e.reciprocal(out=o, in_=i), reads, writes)

    def rcp(self, out, in_, reads, writes):
        self.p.add("dve", lambda e, o=out, i=in_: e.reciprocal(out=o, in_=i), reads, writes)

    def memset(self, ap, val, writes, eng="dve"):
        self.p.add(eng, lambda e, a=ap, v=val: e.memset(a, v), (), writes)

    def load(self, out, in_, reads, writes, q="sp"):
        return self.p.dma(q, out, in_, reads, writes)

    def store(self, out, in_, reads, writes, q="sp"):
        return self.p.dma(q, out, in_, reads, writes)


def _consts():
    c = {}
    pos = np.arange(S, dtype=np.float32)
    inv = (np.float32(10000.0) ** (-np.arange(0, 128, 2, dtype=np.float32) / np.float32(128))).astype(np.float32)
    ang = (pos[:, None] * inv[None, :]).astype(np.float32)
    cs, sn = np.cos(ang).T, np.sin(ang).T
    c["ropec"] = np.ascontiguousarray(np.concatenate([cs, cs], 0), dtype=np.float32)
    c["ropes"] = np.ascontiguousarray(np.concatenate([-sn, sn], 0), dtype=np.float32)
    m = np.arange(128)
    psw = np.zeros((128, 128), np.float32)
    psw[(m + 64) % 128, m] = 1.0
    c["pswap"] = psw
    c["ident"] = np.eye(128, dtype=np.float32)
    c["ident32k"] = np.eye(128, dtype=np.float32) * 32768.0
    sl = np.arange(128)[:, None]
    tl = np.arange(512)[None, :]
    c["mcaus"] = np.stack([(128 * i + sl <= tl) for i in range(4)], 1).astype(np.float32)
    c["mstrict"] = np.stack([(128 * i + sl < tl) for i in range(4)], 1).astype(np.float32)
    c["mfar"] = np.stack([(128 * i + sl > tl) for i in range(4)], 1).astype(np.float32)
    c["mcmp"] = np.stack([(tl >= -512 * k + 16 * sl + 31) for k in range(5)], 1).astype(np.float32)
    for nm in ("caus", "far", "cmp"):
        c["b" + nm] = ((c["m" + nm] - 1.0) * 32768.0).astype(np.float32)
    j = np.arange(64)[:, None]
    s_ = np.arange(S)[None, :]
    c["ebig"] = (s_ // 64 == j).astype(np.float32)
    jt = np.arange(128)[:, None]
    c["negtri"] = -(jt >= np.arange(128)[None, :]).astype(np.float32)
    tlc = np.arange(128)[:, None]
    jj = np.arange(512)[None, :]
    c["mfull"] = np.where((jj - 256) <= np.floor((tlc - 31) / 16.0), 0.0, NEG).astype(np.float32)
    jj = np.arange(128)[None, :]
    hi = (tlc >= 64).astype(np.int64)
    rel = jj - 64
    forced1 = rel == hi
    forced2 = rel == hi - 1
    invalid = rel > hi
    c["tkval"] = (~(forced1 | forced2 | invalid)).astype(np.float32)
    c["tkbias"] = (forced1 * 2.0e4 + forced2 * 1.0e4 + invalid * (-1.0e4)).astype(np.float32)
    return c


_CONST = None


def consts():
    global _CONST
    if _CONST is None:
        _CONST = _consts()
    return _CONST


def phase_proj(cx, xT, wF, wT, wG, ropec, ropes, pswap, FO, TO, GO, spec, nG, xsrc=None):
    with ExitStack() as ls:
        NF = len(spec)
        NTC = TO.shape[1]
        psw = cx.sb("psw", [128, 128], BF16, ls)
        cx.load(psw[:], pswap, (), ["pswap"], q="pool")
        Wf = cx.sb("Wf", [128, 16, NF * 128], BF16, ls)
        Wt = cx.sb("Wt", [128, 16, NTC], BF16, ls)
        wFv = wF.rearrange("(k p) c -> p k c", p=128)
        wTv = wT.rearrange("(k p) c -> p k c", p=128)
        def wloads(lo, hi):
            for fi in range(lo, hi):
                cx.load(Wf[:, :, fi * 128:(fi + 1) * 128], wFv[:, :, fi * 128:(fi + 1) * 128], (), ["Wf%d" % fi],
                        q="pool")
        wloads(0, 1)
        if nG:
            Wg = cx.sb("Wg", [128, 16, 8], BF16, ls)
            wGv = wG.rearrange("(k p) c -> p k c", p=128)
        xb = cx.rot_sb("xb", 2, [128, 16, 512], BF16, ls)
        xTv = xT.rearrange("(k p) t -> p k t", p=128) if xsrc is None else None
        pj = cx.banks([0, 1, 2, 3], "pj")
        pr = cx.banks([4, 5], "pr")
        pt = cx.banks([6, 7], "pt")
        ost = cx.rot_sb("ost", 6, [128, 512], BF16, ls)
        qtmp = cx.rot_sb("qtmp", 2, [128, 512], BF16, ls)
        rt1 = cx.rot_sb("rt1", 2, [128, 512], F32, ls)
        rt2 = cx.rot_sb("rt2", 2, [128, 512], F32, ls)
        rc = cx.rot_sb("rc", 2, [128, 512], F32, ls)
        rs = cx.rot_sb("rs", 2, [128, 512], F32, ls)
        gst = cx.rot_sb("gst", 2, [8, 512], F32, ls)
        has_rope = any(s[0].startswith("rope") for s in spec)
        for c in range(NCH):
            tsl = slice(c * 512, (c + 1) * 512)
            x_t, x_k = xb.next()
            for k4 in range(4):
                if xsrc is None:
                    cx.load(x_t[:, 4 * k4:4 * k4 + 4, :], xTv[:, 4 * k4:4 * k4 + 4, tsl], (),
                            [x_k + "_%d" % k4], q="pool")
                else:
                    for jj, ((c0, c1), sap) in enumerate(xsrc(c, k4)):
                        cx.load(x_t[:, 4 * k4:4 * k4 + 4, c0:c1], sap, (), [x_k + "_%d_%d" % (k4, jj)], q="sp")
            if xsrc is None:
                xk = [[x_k + "_%d" % (k // 4)] for k in range(16)]
            else:
                xk = [[x_k + "_%d_%d" % (k // 4, jj) for jj in range(2)] for k in range(16)]
            if c == 0:
                wloads(1, NF)
                cx.load(Wt[:, :, :], wTv[:, :, :], (), ["Wt"], q="pool")
                if nG:
                    cx.load(Wg[:, :, 0:nG], wGv, (), ["Wg"], q="pool")
            if has_rope:
                rc_t, rc_k = rc.next()
                rs_t, rs_k = rs.next()
                cx.load(rc_t[:], ropec[:, tsl], (), [rc_k], q="sp")
                cx.load(rs_t[:], ropes[:, tsl], (), [rs_k], q="sp")

            def rope(src_bf, src_k, out_idx):
                r_t, r_k = pr.next()
                cx.mm(r_t[:], psw[:], src_bf[:], True, True, [src_k, "pswap"], [r_k])
                a_t, a_k = rt1.next()
                b_t, b_k = rt2.next()
                cx.tt(a_t[:], src_bf[:], rc_t[:], ALU.mult, [src_k, rc_k], [a_k])
                cx.tt(b_t[:], r_t[:], rs_t[:], ALU.mult, [r_k, rs_k], [b_k])
                o_t, o_k = ost.next()
                cx.tt(o_t[:], a_t[:], b_t[:], ALU.add, [a_k, b_k], [o_k])
                cx.store(FO[out_idx, :, tsl], o_t[:], [o_k], [("FO", out_idx, c)], q="sp")

            for fi, (kind, oi, oi2) in enumerate(spec):
                b_t, b_k = pj.next()
                for k in range(16):
                    cx.mm(b_t[:], Wf[:, k, fi * 128:(fi + 1) * 128], x_t[:, k, :], k == 0, k == 15,
                          ["Wf%d" % fi] + xk[k], [b_k])
                if kind in ("copy", "scale", "silu"):
                    o_t, o_k = ost.next()
                    if kind == "copy":
                        cx.copy(o_t[:], b_t[:], [b_k], [o_k])
                    elif kind == "scale":
                        cx.actf(o_t[:], b_t[:], ACT.Copy, [b_k], [o_k], scale=SCALE)
                    else:
                        cx.actf(o_t[:], b_t[:], ACT.Silu, [b_k], [o_k])
                    cx.store(FO[oi, :, tsl], o_t[:], [o_k], [("FO", oi, c)], q="sp")
                elif kind == "rope":
                    q_t, q_k = qtmp.next()
                    cx.copy(q_t[:], b_t[:], [b_k], [q_k])
                    rope(q_t, q_k, oi)
                elif kind == "rope_scale_both":
                    o_t, o_k = ost.next()
                    cx.actf(o_t[:], b_t[:], ACT.Copy, [b_k], [o_k], scale=SCALE)
                    cx.store(FO[oi, :, tsl], o_t[:], [o_k], [("FO", oi, c)], q="sp")
                    rope(o_t, o_k, oi2)
                elif kind == "rope_scale":
                    q_t, q_k = qtmp.next()
                    cx.actf(q_t[:], b_t[:], ACT.Copy, [b_k], [q_k], scale=SCALE)
                    rope(q_t, q_k, oi)
            for tt in range(4):
                b_t, b_k = pt.next()
                for k in range(16):
                    cx.mm(b_t[:, 0:NTC], x_t[:, k, tt * 128:(tt + 1) * 128], Wt[:, k, :], k == 0, k == 15,
                          ["Wt"] + xk[k], [b_k])
                o_t, o_k = ost.next()
                cx.copy(o_t[:, 0:NTC], b_t[:, 0:NTC], [b_k], [o_k])
                tok = c * 4 + tt
                cx.store(TO[tok * 128:(tok + 1) * 128, :], o_t[:, 0:NTC], [o_k], [("TO", tok)], q="sp")
            if nG:
                b_t, b_k = pt.next()
                for k in range(16):
                    cx.mm(b_t[0:nG, :], Wg[:, k, 0:nG], x_t[:, k, :], k == 0, k == 15, ["Wg"] + xk[k], [b_k])
                g_t, g_k = gst.next()
                cx.actf(g_t[0:nG, :], b_t[0:nG, :], ACT.Sigmoid, [b_k], [g_k])
                cx.store(GO[0:nG, tsl], g_t[0:nG, :], [g_k], [("GO", c)], q="sp")
        cx.p.emit()


EV_OFF = {}
_o = 0
for _n, _sz in zip(("qa", "kc", "vc", "ks", "vs", "kw", "vw", "ga", "gate_a", "qb", "kb", "vb", "gate_b"),
                   (1024, 256, 256, 256, 256, 256, 256, 24, 1024, 1024, 1024, 1024, 1024)):
    EV_OFF[_n] = _o
    _o += _sz

SPEC0 = ([("rope_scale_both", 0, 4), ("rope_scale_both", 1, 5), ("scale", 2, None), ("scale", 3, None),
          ("copy", 6, None), ("copy", 7, None), ("rope", 8, None), ("rope", 9, None),
          ("silu", 10, None), ("silu", 11, None), ("scale", 12, None), ("scale", 13, None),
          ("copy", 14, None), ("copy", 15, None), ("silu", 16, None), ("silu", 17, None)])
NFO0 = 18


def l0_weights(w_in, g):
    hk = g // 2
    own = [2 * g, 2 * g + 1]
    oth = [h for h in range(4 * hk, 4 * hk + 4) if h not in own]
    hc = lambda name, h: w_in[:, EV_OFF[name] + h * 128: EV_OFF[name] + (h + 1) * 128]
    cols = [hc("qa", own[0]), hc("qa", own[1]), hc("qa", oth[0]), hc("qa", oth[1]),
            hc("kc", hk), hc("vc", hk), hc("ks", hk), hc("kw", hk),
            hc("gate_a", own[0]), hc("gate_a", own[1]),
            hc("qb", own[0]), hc("qb", own[1]), hc("kb", own[0]), hc("kb", own[1]),
            hc("gate_b", own[0]), hc("gate_b", own[1])]
    wF = np.ascontiguousarray(np.concatenate(cols, 1))
    wT = np.ascontiguousarray(np.concatenate([hc("vs", hk), hc("vw", hk), hc("vb", own[0]), hc("vb", own[1])], 1))
    g0 = EV_OFF["ga"]
    wG = np.ascontiguousarray(w_in[:, g0 + 3 * own[0]: g0 + 3 * own[0] + 6])
    return wF, wT, wG


def phase_nsa_prep(cx, FO, w1k, w1v, w2k, w2v, pekT, pevT, CD, KCMP, VCMP, SELB):
    with ExitStack() as ls:
        sb = lambda n, s, d: cx.sb(n, s, d, ls)
        W1 = [sb("W1k", [128, 32, 128], BF16), sb("W1v", [128, 32, 128], BF16)]
        W2 = [sb("W2k", [128, 128], BF16), sb("W2v", [128, 128], BF16)]
        PE_ = [sb("pek", [128, 32], BF16), sb("pev", [128, 32], BF16)]
        src = [sb("kcs", [128, 256, 16], BF16), sb("vcs", [128, 256, 16], BF16)]
        for i, (w1, w2, pe) in enumerate(((w1k, w2k, pekT), (w1v, w2v, pevT))):
            cx.load(W1[i][:], w1.rearrange("(l d) f -> d l f", d=128), (), ["W1_%d" % i], q="pool")
            cx.load(W2[i][:], w2, (), ["W2_%d" % i], q="pool")
            cx.load(PE_[i][:], pe, (), ["PE_%d" % i], q="pool")
            cx.load(src[i][:], FO[6 + i].rearrange("p (n r) -> p n r", r=16), (), ["src%d" % i], q="sp")
        kcmpT = sb("kcmpT", [128, 256], BF16)
        vcmp = sb("vcmp", [128, 2, 128], BF16)
        cx.memset(kcmpT[:, 255:256], 0.0, ["kcmpT"])
        cx.memset(vcmp[:], 0.0, ["vcmp"])
        sact = [sb("sk", [128, 256], BF16), sb("sv", [128, 256], BF16)]
        bias = [sb("bk", [128, 1], F32), sb("bv", [128, 1], F32)]
        PB, PK = cx.PB, cx.PBK
        for i in range(2):
            for l in range(32):
                cx.mm(PB[0][:, 0:1], W1[i][:, l, :], PE_[i][:, l:l + 1], l == 0, l == 31,
                      ["W1_%d" % i, "PE_%d" % i], [PK[0]])
            cx.copy(bias[i][:], PB[0][:, 0:1], [PK[0]], ["bias%d" % i])
            for l in range(32):
                n0, r = (0, l) if l < 16 else (1, l - 16)
                cx.mm(PB[1][:, 0:255], W1[i][:, l, :], src[i][:, n0:n0 + 255, r], l == 0, l == 31,
                      ["W1_%d" % i, "src%d" % i], [PK[1]])
            cx.actf(sact[i][:, 0:255], PB[1][:, 0:255], ACT.Silu, [PK[1], "bias%d" % i], ["sact%d" % i],
                    bias=bias[i][:])
        cx.mm(PB[2][:, 0:255], W2[0][:], sact[0][:, 0:255], True, True, ["W2_0", "sact0"], [PK[2]])
        cx.copy(kcmpT[:, 0:255], PB[2][:, 0:255], [PK[2]], ["kcmpT"])
        for i, M in ((0, 128), (1, 127)):
            cx.mm(PB[3][0:M, i * 128:(i + 1) * 128], sact[1][:, i * 128:i * 128 + M], W2[1][:], True, True,
                  ["W2_1", "sact1"], [PK[3]])
            cx.copy(vcmp[0:M, i, :], PB[3][0:M, i * 128:(i + 1) * 128], [PK[3]], ["vcmp"])
        cx.store(KCMP, kcmpT[:], ["kcmpT"], ["KCMP"], q="sp")
        cx.store(VCMP, vcmp[:], ["vcmp"], ["VCMP"], q="sp")
        qu = [sb("qu%d" % g, [128, S], BF16) for g in range(4)]
        for g in range(4):
            cx.load(qu[g][:], FO[g], (), ["qu%d" % g], q="sp")
        mfull = sb("mfull", [128, 512], F32)
        tkval = sb("tkval", [128, 128], F32)
        tkbias = sb("tkbias", [128, 128], F32)
        id32 = sb("id32", [128, 128], BF16)
        cx.load(mfull[:], CD["mfull"], (), ["mfull"], q="sp")
        cx.load(tkval[:], CD["tkval"], (), ["tkval"], q="sp")
        cx.load(tkbias[:], CD["tkbias"], (), ["tkbias"], q="sp")
        cx.load(id32[:], CD["ident32k"], (), ["id32"], q="pool")
        selbT = sb("selbT", [64, S], BF16)
        pairs = Rot("pp", [(0, 1), (2, 3)])
        pT = cx.banks([4, 5], "pT")
        Psum = cx.rot_sb("Psum", 2, [128, 64, 4], F32, ls)
        for t_ in Psum.t:
            cx.memset(t_[:], 0.0, [Psum.k[Psum.t.index(t_)]])
        sm = cx.rot_sb("sm", 8, [128, 255], F32, ls)
        ee = cx.rot_sb("ee", 8, [128, 255], F32, ls)
        sml = cx.rot_sb("sml", 40, [128, 1], F32, ls)
        imp = cx.rot_sb("imp", 2, [128, 64], F32, ls)
        sc = cx.rot_sb("sc", 2, [128, 64], F32, ls)
        sc2 = cx.rot_sb("sc2", 2, [128, 64], F32, ls)
        m8 = cx.rot_sb("m8", 4, [128, 8], F32, ls)
        selb = cx.rot_sb("selb", 2, [128, 64], BF16, ls)
        def tile_steps(tt):
            L = {}
            steps = []

            def s_mm():
                (ia, ib), _ = pairs.next()
                L["b"] = (ia, ib)
                for g in range(4):
                    bi = ia if g < 2 else ib
                    off = (g % 2) * 256
                    cx.mm(PB[bi][:, off:off + 255], qu[g][:, tt * 128:(tt + 1) * 128], kcmpT[:, 0:255], True, True,
                          ["qu%d" % g, "kcmpT"], [PK[bi]])
                L["P"] = Psum.next()
            steps.append(s_mm)

            def bank(g):
                ia, ib = L["b"]
                bi = ia if g < 2 else ib
                return PB[bi], PK[bi], (g % 2) * 256

            def a1(g):
                b_t, b_k, off = bank(g)
                s_t, s_k = sm.next()
                L["s", g] = (s_t, s_k)
                cx.tt(s_t[:], b_t[:, off:off + 255], mfull[:, 256 - 8 * tt: 256 - 8 * tt + 255], ALU.add,
                      [b_k, "mfull"], [s_k])

            def a2(g):
                s_t, s_k = L["s", g]
                mx, mx_k = sml.next()
                L["mx", g] = (mx, mx_k)
                cx.p.add("dve", lambda e, o=mx[:], i=s_t[:]: e.reduce_max(out=o, in_=i, axis=AX.X), [s_k], [mx_k])

            def a3(g):
                mx, mx_k = L["mx", g]
                nm, nm_k = sml.next()
                L["nm", g] = (nm, nm_k)
                cx.ts(nm[:], mx[:], -1000.0, ALU.max, [mx_k], [nm_k], s2=-1.0, op1=ALU.mult)

            def a4(g):
                s_t, s_k = L["s", g]
                nm, nm_k = L["nm", g]
                e_t, e_k = ee.next()
                dn, dn_k = sml.next()
                L["e", g] = (e_t, e_k)
                L["dn", g] = (dn, dn_k)
                cx.actf(e_t[:], s_t[:], ACT.Exp, [s_k, nm_k], [e_k, dn_k], bias=nm[:], accum_out=dn[:])

            def a5(g):
                dn, dn_k = L["dn", g]
                cx.ts(dn[:], dn[:], 1e-30, ALU.max, [dn_k], [dn_k])

            def a6(g):
                dn, dn_k = L["dn", g]
                rd, rd_k = sml.next()
                L["rd", g] = (rd, rd_k)
                cx.recip(rd[:], dn[:], [dn_k], [rd_k])

            def a7(g):
                P_t, P_k = L["P"]
                P2 = P_t[:].rearrange("p a b -> p (a b)")
                e_t, e_k = L["e", g]
                rd, rd_k = L["rd", g]
                if g == 0:
                    cx.ts(P2[:, 0:255], e_t[:], rd[:], ALU.mult, [e_k, rd_k], [P_k])
                else:
                    cx.stt(P2[:, 0:255], e_t[:], rd[:], P2[:, 0:255], ALU.mult, ALU.add, [e_k, rd_k, P_k], [P_k])

            for fn in (a1, a2, a3, a4, a5, a6, a7):
                for g in range(4):
                    steps.append(lambda fn=fn, g=g: fn(g))

            def t1():
                P_t, P_k = L["P"]
                i_t, i_k = imp.next()
                L["i"] = (i_t, i_k)
                cx.p.add("dve", lambda e, o=i_t[:], i=P_t[:]: e.tensor_reduce(out=o, in_=i, axis=AX.X, op=ALU.add),
                         [P_k], [i_k])

            def t2():
                P_t, P_k = L["P"]
                i_t, i_k = L["i"]
                cx.tt(i_t[:, 1:64], i_t[:, 1:64], P_t[:, 0:63, 3], ALU.add, [i_k, P_k], [i_k])

            def t3():
                i_t, i_k = L["i"]
                c_t, c_k = sc.next()
                L["c"] = (c_t, c_k)
                cx.tt(c_t[:], i_t[:], tkval[:, 64 - 2 * tt:128 - 2 * tt], ALU.mult, [i_k, "tkval"], [c_k])

            def t4():
                c_t, c_k = L["c"]
                cx.tt(c_t[:], c_t[:], tkbias[:, 64 - 2 * tt:128 - 2 * tt], ALU.add, [c_k, "tkbias"], [c_k])

            def t5():
                c_t, c_k = L["c"]
                cx.memset(c_t[:, 0:1], 3.0e4, [c_k])

            def t6():
                c_t, c_k = L["c"]
                m1, m1_k = m8.next()
                L["m1"] = (m1, m1_k)
                cx.p.add("dve", lambda e, o=m1[:], i=c_t[:]: e.max(out=o, in_=i), [c_k], [m1_k])

            def t7():
                c_t, c_k = L["c"]
                m1, m1_k = L["m1"]
                d_t, d_k = sc2.next()
                L["d"] = (d_t, d_k)
                cx.p.add("dve", lambda e, o=d_t[:], r=m1[:], v=c_t[:]: e.match_replace(
                    out=o, in_to_replace=r, in_values=v, imm_value=-1.0e9), [c_k, m1_k], [d_k])

            def t8():
                d_t, d_k = L["d"]
                m2, m2_k = m8.next()
                L["m2"] = (m2, m2_k)
                cx.p.add("dve", lambda e, o=m2[:], i=d_t[:]: e.max(out=o, in_=i), [d_k], [m2_k])

            def t9():
                c_t, c_k = L["c"]
                m2, m2_k = L["m2"]
                sb_t, sb_k = selb.next()
                L["sb"] = (sb_t, sb_k)
                cx.ts(sb_t[:], c_t[:], m2[:, 7:8], ALU.is_ge, [c_k, m2_k], [sb_k], s2=1.0, op1=ALU.subtract)

            def t10():
                sb_t, sb_k = L["sb"]
                t_t, t_k = pT.next()
                cx.mm(t_t[0:64, 0:128], sb_t[:], id32[:], True, True, [sb_k, "id32"], [t_k])
                cx.copy(selbT[:, tt * 128:(tt + 1) * 128], t_t[0:64, 0:128], [t_k], ["selbT"])

            steps.extend([t1, t2, t3, t4, t5, t6, t7, t8, t9, t10])
            return steps

        for tt in range(0, NT, 2):
            sa, sb_ = tile_steps(tt), tile_steps(tt + 1)
            for x, y in zip(sa, sb_):
                x()
                y()
        cx.store(SELB, selbT[:], ["selbT"], ["SELB"], q="sp")
        cx.p.emit()


CONST_A = ("ropec", "ropes", "pswap", "ident32k", "bcaus", "mstrict", "bfar", "bcmp", "ebig", "negtri",
           "mfull", "tkval", "tkbias", "ident")


def build_A(level=9):
    nc = bass.Bass("TRN2", target_bir_lowering=False)
    with ExitStack() as st:
        cx = Ctx(nc, st)
        C = consts()
        xT = cx.dram("xT", [D, S], F32, "ExternalInput")
        wF = cx.dram("wF", [D, 16 * 128], F32, "ExternalInput")
        wT = cx.dram("wT", [D, 512], F32, "ExternalInput")
        wG = cx.dram("wG", [D, 6], F32, "ExternalInput")
        w1k = cx.dram("w1k", [4096, 128], F32, "ExternalInput")
        w1v = cx.dram("w1v", [4096, 128], F32, "ExternalInput")
        w2k = cx.dram("w2k", [128, 128], F32, "ExternalInput")
        w2v = cx.dram("w2v", [128, 128], F32, "ExternalInput")
        pekT = cx.dram("pekT", [128, 32], F32, "ExternalInput")
        pevT = cx.dram("pevT", [128, 32], F32, "ExternalInput")
        CD = {n: cx.dram("c_" + n, list(C[n].shape), F32, "ExternalInput") for n in CONST_A}
        dbg = "ExternalOutput" if level < 9 else None
        FO = cx.dram("FO", [NFO0, 128, S], BF16, dbg)
        TO = cx.dram("TO", [S, 512], BF16, dbg)
        GO = cx.dram("GO", [8, S], F32, dbg)
        KCMP = cx.dram("KCMP", [128, 256], BF16, dbg)
        VCMP = cx.dram("VCMP", [128, 2, 128], BF16, dbg)
        SELB = cx.dram("SELB", [64, S], BF16, dbg)
        OUT = cx.dram("OUT", [4, 512, 1024], BF16, "ExternalOutput")
        phase_proj(cx, xT, wF, wT, wG, CD["ropec"], CD["ropes"], CD["pswap"], FO, TO, GO, SPEC0, 6)
        if level >= 2:
            phase_nsa_prep(cx, FO, w1k, w1v, w2k, w2v, pekT, pevT, CD, KCMP, VCMP, SELB)
        if level >= 3:
            phase_nsa_attn(cx, FO, TO, GO, CD, KCMP, VCMP, SELB, OUT)
        if level >= 4:
            phase_sb_attn(cx, FO, TO, CD, OUT)
    return nc


def inputs_A(x, ev, c):
    b, g = c // 4, c % 4
    C = consts()
    wF, wT, wG = l0_weights(ev["w_in"], g)
    m = {"xT": np.ascontiguousarray(x[b].T), "wF": wF, "wT": wT, "wG": wG,
         "w1k": ev["w1_k"], "w1v": ev["w1_v"], "w2k": ev["w2_k"], "w2v": ev["w2_v"],
         "pekT": np.ascontiguousarray(ev["pe_k"].T), "pevT": np.ascontiguousarray(ev["pe_v"].T)}
    for n in CONST_A:
        m["c_" + n] = C[n]
    return m


def run_attn_stream(cx, groups, zrot, erot, esrot, ident, onesb, hlrot, LA=1, defer=2):
    flat = []
    for gi, g in enumerate(groups):
        g["gid"] = gi
        n = len(g["tiles"])
        for i, t in enumerate(g["tiles"]):
            flat.append((g, i, n, t))
    zs = {}

    def emit_z(idx):
        g, i, n, t = flat[idx]
        z_t, z_k = zrot.next()
        sel = t.get("sel")
        mb = t.get("maskb")
        extra = (sel is not None) + (mb is not None)
        cx.mm(z_t[:], t["kT"], g["q_ap"], True, extra == 0, list(t["kkeys"]) + list(g["q_keys"]), [z_k])
        if sel is not None:
            extra -= 1
            cx.mm(z_t[:], sel[0], sel[1], False, extra == 0, list(sel[2]), [z_k])
        if mb is not None:
            cx.mm(z_t[:], ident, mb, False, True, ["ident"] + list(t["mkeys"]), [z_k])
        zs[idx] = (z_t, z_k)

    pending = []

    def run_chain(f):
        r = f()
        while r is not None:
            r = r[0]() if isinstance(r, tuple) else r()

    def flush(resource, exclude):
        keep = []
        todo = []
        for p in pending:
            (todo if (resource in p[2] and p[3] != exclude) else keep).append(p)
        pending[:] = keep
        for p in todo:
            run_chain(p[1])

    def make_chain(g):
        ep = g["epilogue"]

        def stage_c():
            r = ep()
            if callable(r):
                r = r()
            return r
        return stage_c

    for idx in range(min(LA, len(flat))):
        emit_z(idx)
    for idx in range(len(flat)):
        if idx + LA < len(flat):
            emit_z(idx + LA)
        g, i, n, t = flat[idx]
        if i == 0:
            flush(g["oaccs"][0][1], g["gid"])
            flush(g["den"][1], g["gid"])
        z_t, z_k = zs.pop(idx)
        e_t, e_k = erot.next()
        cx.actf(e_t[:], z_t[:], ACT.Exp, [z_k], [e_k])
        for (o_ap, o_k), v in zip(g["oaccs"], t["vs"]):
            cx.mm(o_ap, v, e_t[:], i == 0, i == n - 1, [e_k] + list(t["vkeys"]), [o_k])
        cx.mm(g["den"][0], onesb, e_t[:], i == 0, i == n - 1, [e_k, "onesb"], [g["den"][1]])
        for p in pending:
            p[0] -= 1
        ready = [p for p in pending if p[0] <= 0]
        pending[:] = [p for p in pending if p[0] > 0]
        for p in ready:
            r = p[1]()
            if isinstance(r, tuple):
                pending.append([r[1], r[0], p[2], p[3]])
            elif callable(r):
                pending.append([defer, r, p[2], p[3]])
        if i == n - 1:
            pending.append([defer, make_chain(g), {g["den"][1], g["oaccs"][0][1]}, g["gid"]])
    while pending:
        p = pending.pop(0)
        run_chain(p[1])


def chunked_loads(cx, items, q="sp"):
    for c in range(NCH):
        for t_, d_, name, kind in items:
            if kind == "F":
                cx.load(t_[:, c * 512:(c + 1) * 512], d_[:, c * 512:(c + 1) * 512], (), ["%s_%d" % (name, c)], q=q)
            else:
                cx.load(t_[:, 4 * c:4 * c + 4, :], d_[:, 4 * c:4 * c + 4, :], (), ["%s_%d" % (name, c)], q=q)


def phase_nsa_attn(cx, FO, TO, GO, CD, KCMP, VCMP, SELB, OUT):
    with ExitStack() as ls:
        sb = lambda n, s, d: cx.sb(n, s, d, ls)
        PB, PK = cx.PB, cx.PBK
        qu = [sb("qu%d" % h, [128, S], BF16) for h in range(2)]
        qr = [sb("qr%d" % h, [128, S], BF16) for h in range(2)]
        gas = [sb("gas%d" % h, [128, S], BF16) for h in range(2)]
        kcmpT = sb("kcmpT", [128, 256], BF16)
        vcmp = sb("vcmp", [128, 2, 128], BF16)
        selbT = sb("selbT", [64, S], BF16)
        ebig = sb("ebig", [64, S], BF16)
        cx.load(kcmpT[:], KCMP, (), ["kcmpT"])
        cx.load(vcmp[:], VCMP, (), ["vcmp"])
        mcaus = sb("mcaus", [128, 4, 512], BF16)
        mfar = sb("mfar", [128, 4, 512], BF16)
        mcmp = sb("mcmp", [128, 5, 512], BF16)
        identb = sb("identb", [128, 128], BF16)
        cx.load(identb[:], CD["ident"], (), ["ident"], q="pool")
        cx.load(mcmp[:], CD["bcmp"], (), ["mcmp"], q="pool")
        cx.load(mcaus[:], CD["bcaus"], (), ["mcaus"], q="pool")
        cx.load(ebig[:], CD["ebig"], (), ["ebig"], q="pool")
        cx.load(mfar[:], CD["bfar"], (), ["mfar"], q="pool")
        ksT = sb("ksT", [128, S], BF16)
        kwT = sb("kwT", [128, S], BF16)
        vs = sb("vs", [128, NT, 128], BF16)
        vw = sb("vw", [128, NT, 128], BF16)
        TOv = TO.rearrange("(k p) c -> p k c", p=128)
        chunked_loads(cx, [(qu[0], FO[0], "qu0", "F"), (qr[0], FO[4], "qr0", "F"), (selbT, SELB, "selbT", "F"),
                           (ksT, FO[8], "ksT", "F"), (vs, TOv[:, :, 0:128], "vs", "T"),
                           (kwT, FO[9], "kwT", "F"), (vw, TOv[:, :, 128:256], "vw", "T"),
                           (qu[1], FO[1], "qu1", "F"), (qr[1], FO[5], "qr1", "F"),
                           (gas[0], FO[10], "gas0", "F"), (gas[1], FO[11], "gas1", "F")], q="pool")
        zrot = cx.banks([0, 1, 2], "z")
        sets = Rot("sets", [(5, 3), (6, 4), (7, 3), (5, 4), (6, 3), (7, 4)])
        erot = cx.rot_sb("e", 5, [128, 512], BF16, ls)
        esrot = cx.rot_sb("es", 4, [128, 512], F32, ls)
        hlrot = cx.rot_sb("hl", 6, [128, 512], BF16, ls)
        onesb = sb("onesb", [128, 128], BF16)
        cx.memset(onesb[:], 1.0, ["onesb"])
        gb = cx.rot_sb("gb", 4, [128, 512], F32, ls)
        dnc = cx.rot_sb("dnc", 3, [128, 512], F32, ls)
        tmp = cx.rot_sb("tmp", 2, [128, 512], F32, ls)
        osum = cx.rot_sb("osum", 2, [128, 512], F32, ls)
        ost = cx.rot_sb("ost", 2, [128, 512], BF16, ls)
        groups = []
        state = {}
        for h in range(2):
            for c in range(NCH):
                tsl = slice(c * 512, (c + 1) * 512)
                for br in range(3):
                    tiles = []
                    if br == 0:
                        q_ap, q_keys = qu[h][:, tsl], ["qu%d_%d" % (h, c)]
                        nts = [0] + ([1] if c >= 4 else [])
                        for nt in nts:
                            mk = None
                            if nt == 0 and c <= 4:
                                mk = mcmp[:, c, :]
                            if nt == 1:
                                mk = mcmp[:, c - 4, :]
                            tiles.append(dict(kT=kcmpT[:, nt * 128:(nt + 1) * 128], kkeys=["kcmpT"],
                                              vs=[vcmp[:, nt, :]], vkeys=["vcmp"], maskb=mk, mkeys=["mcmp"]))
                    elif br == 1:
                        q_ap, q_keys = qr[h][:, tsl], ["qr%d_%d" % (h, c)]
                        for kt in range(4 * c + 4):
                            mk = mcaus[:, kt - 4 * c, :] if kt >= 4 * c else None
                            tiles.append(dict(kT=ksT[:, kt * 128:(kt + 1) * 128], kkeys=["ksT_%d" % (kt // 4)],
                                              vs=[vs[:, kt, :]], vkeys=["vs_%d" % (kt // 4)], maskb=mk, mkeys=["mcaus"],
                                              sel=(ebig[:, kt * 128:(kt + 1) * 128], selbT[:, tsl],
                                                   ["ebig", "selbT_%d" % c])))
                    else:
                        q_ap, q_keys = qr[h][:, tsl], ["qr%d_%d" % (h, c)]
                        for kt in range(max(0, 4 * c - 4), 4 * c + 4):
                            if kt >= 4 * c:
                                mk, mkk = mcaus[:, kt - 4 * c, :], ["mcaus"]
                            else:
                                mk, mkk = mfar[:, kt - (4 * c - 4), :], ["mfar"]
                            tiles.append(dict(kT=kwT[:, kt * 128:(kt + 1) * 128], kkeys=["kwT_%d" % (kt // 4)],
                                              vs=[vw[:, kt, :]], vkeys=["vw_%d" % (kt // 4)], maskb=mk, mkeys=mkk))
                    (io, idn), _ = sets.next()

                    def epi(h=h, c=c, br=br, io=io, idn=idn, tsl=tsl):
                        def deferred():
                            if br == 0:
                                state["os"] = osum.next()
                            os_t, os_k = state["os"]
                            g_t, g_k = gb.next()
                            cx.load(g_t[:], GO[3 * h + br:3 * h + br + 1, tsl].partition_broadcast(128), (), [g_k])
                            d_t, d_k = dnc.next()
                            cx.ts(d_t[:], PB[idn][:], 1e-30, ALU.max, [PK[idn]], [d_k])
                            cx.actf(d_t[:], d_t[:], ACT.Ln, [d_k], [d_k])
                            cx.actf(d_t[:], d_t[:], ACT.Exp, [d_k], [d_k], scale=-1.0)
                            cx.tt(d_t[:], d_t[:], g_t[:], ALU.mult, [d_k, g_k], [d_k])
                            if br == 0:
                                cx.tt(os_t[:], PB[io][:], d_t[:], ALU.mult, [PK[io], d_k], [os_k])
                            else:
                                t_t, t_k = tmp.next()
                                cx.tt(t_t[:], PB[io][:], d_t[:], ALU.mult, [PK[io], d_k], [t_k])
                                cx.tt(os_t[:], os_t[:], t_t[:], ALU.add, [os_k, t_k], [os_k])
                            if br == 2:
                                o_t, o_k = ost.next()
                                cx.tt(o_t[:], os_t[:], gas[h][:, tsl], ALU.mult, [os_k, "gas%d_%d" % (h, c)], [o_k])
                                cx.store(OUT[c // 2, h * 128:(h + 1) * 128, (c % 2) * 512:(c % 2) * 512 + 512],
                                         o_t[:], [o_k], [("OUT", h, c)], q="sp")
                        return deferred

                    groups.append(dict(q_ap=q_ap, q_keys=q_keys, tiles=tiles, oaccs=[(PB[io][:], PK[io])],
                                       den=(PB[idn][:], PK[idn]), epilogue=epi))
        run_attn_stream(cx, groups, zrot, erot, esrot, identb[:], onesb[:], hlrot, LA=2, defer=2)
        cx.p.emit()


def phase_sb_attn(cx, FO, TO, CD, OUT, on_quarter=None, prefetch=None):
    with ExitStack() as ls:
        sb = lambda n, s, d: cx.sb(n, s, d, ls)
        PB, PK = cx.PB, cx.PBK
        TOv = TO.rearrange("(k p) c -> p k c", p=128)
        q = [sb("q%d" % h, [128, S], BF16) for h in range(2)]
        k = [sb("k%d" % h, [128, S], BF16) for h in range(2)]
        gbs = [sb("gbs%d" % h, [128, S], BF16) for h in range(2)]
        v = [sb("v%d" % h, [128, NT, 128], BF16) for h in range(2)]
        mstr = sb("mstr", [128, 4, 512], BF16)
        negtri = sb("negtri", [128, 128], BF16)
        cx.load(mstr[:], CD["mstrict"], (), ["mstr"], q="pool")
        cx.load(negtri[:], CD["negtri"], (), ["negtri"], q="pool")
        items = []
        for h in range(2):
            items += [(q[h], FO[12 + h], "q%d" % h, "F"), (k[h], FO[14 + h], "k%d" % h, "F"),
                      (v[h], TOv[:, :, 256 + 128 * h:384 + 128 * h], "v%d" % h, "T"),
                      (gbs[h], FO[16 + h], "gbs%d" % h, "F")]
        chunked_loads(cx, items)
        negones = sb("negones", [128, 128], BF16)
        cx.memset(negones[:], -1.0, ["negones"])
        zA = cx.banks([0, 1], "zA")
        zB = cx.banks([2, 3], "zB")
        oac = cx.banks([4, 5], "oac")
        csb = cx.banks([6, 7], "cs")
        Et = cx.rot_sb("Et", 2, [128, 512], F32, ls)
        spt = cx.rot_sb("spt", 4, [128, 512], BF16, ls)
        at = cx.rot_sb("at", 4, [128, 512], BF16, ls)
        Racc = cx.rot_sb("Racc", 3, [128, 512], F32, ls)
        zsb = cx.rot_sb("zsb", 2, [128, 512], F32, ls)
        ost = cx.rot_sb("ost", 2, [128, 512], BF16, ls)
        flat = []
        for c in range(NCH):
            for h in range(2):
                kts = list(range(4 * c + 3, -1, -1))
                for i, kt in enumerate(kts):
                    flat.append(dict(h=h, c=c, kt=kt, first=(i == 0), last=(kt == 0)))
        if prefetch is not None:
            prefetch()
        N = len(flat)
        st_ = [dict() for _ in range(N)]
        cur = {"R": None, "o": None}

        def stage1(s):
            t = flat[s]
            h, c, kt = t["h"], t["c"], t["kt"]
            tsl = slice(c * 512, (c + 1) * 512)
            ksl = slice(kt * 128, (kt + 1) * 128)
            a_t, a_k = zA.next()
            cx.mm(a_t[:], k[h][:, ksl], q[h][:, tsl], True, True, ["k%d_%d" % (h, kt // 4), "q%d_%d" % (h, c)], [a_k])
            E_t, E_k = Et.next()
            cx.actf(E_t[:], a_t[:], ACT.Exp, [a_k], [E_k])
            s_t, s_k = spt.next()
            cx.actf(s_t[:], E_t[:], ACT.Ln, [E_k], [s_k], bias=1.0)
            if kt >= 4 * c:
                cx.tt(s_t[:], s_t[:], mstr[:, kt - 4 * c, :], ALU.mult, [s_k, "mstr"], [s_k])
            st_[s]["sp"] = (s_t, s_k)
            st_[s]["zA"] = (a_t, a_k)

        def stage2(s):
            t = flat[s]
            h, c, kt = t["h"], t["c"], t["kt"]
            tsl = slice(c * 512, (c + 1) * 512)
            ksl = slice(kt * 128, (kt + 1) * 128)
            s_t, s_k = st_[s]["sp"]
            if t["first"]:
                cur["R"] = None
            R = cur["R"]
            b_t, b_k = zB.next()
            cx.mm(b_t[:], k[h][:, ksl], q[h][:, tsl], True, False,
                  ["k%d_%d" % (h, kt // 4), "q%d_%d" % (h, c)], [b_k])
            cx.mm(b_t[:], negtri[:], s_t[:], False, True, ["negtri", s_k], [b_k])
            p_t, p_k = at.next()
            if R is None:
                cx.actf(p_t[:], b_t[:], ACT.Exp, [b_k], [p_k])
            else:
                zs_t, zs_k = zsb.next()
                cx.tt(zs_t[:], b_t[:], R[0][:], ALU.add, [b_k, R[1]], [zs_k])
                cx.actf(p_t[:], zs_t[:], ACT.Exp, [zs_k], [p_k])
            if kt >= 4 * c:
                cx.tt(p_t[:], p_t[:], mstr[:, kt - 4 * c, :], ALU.mult, [p_k, "mstr"], [p_k])
            st_[s]["a"] = (p_t, p_k)
            if not t["last"]:
                c_t, c_k = csb.next()
                cx.mm(c_t[:], negones[:], s_t[:], True, True, ["negones", s_k], [c_k])
                Rn_t, Rn_k = Racc.next()
                if R is None:
                    cx.copy(Rn_t[:], c_t[:], [c_k], [Rn_k])
                else:
                    cx.tt(Rn_t[:], c_t[:], R[0][:], ALU.add, [c_k, R[1]], [Rn_k])
                cur["R"] = (Rn_t, Rn_k)

        def stage3(s):
            t = flat[s]
            h, c, kt = t["h"], t["c"], t["kt"]
            tsl = slice(c * 512, (c + 1) * 512)
            if t["first"]:
                cur["o"] = oac.next()
            o_t, o_k = cur["o"]
            p_t, p_k = st_[s]["a"]
            cx.mm(o_t[:], v[h][:, kt, :], p_t[:], t["first"], t["last"], ["v%d_%d" % (h, kt // 4), p_k], [o_k])
            if t["last"]:
                s_o, s_ok = ost.next()
                cx.tt(s_o[:], o_t[:], gbs[h][:, tsl], ALU.mult, [o_k, "gbs%d_%d" % (h, c)], [s_ok])
                cx.store(OUT[c // 2, (2 + h) * 128:(3 + h) * 128, (c % 2) * 512:(c % 2) * 512 + 512], s_o[:], [s_ok],
                         [("OUT", 2 + h, c)], q="pool")
                if on_quarter is not None and h == 1 and c % 2 == 1:
                    on_quarter(c // 2, [("OUT", 2 + hh, cc_) for hh in range(2) for cc_ in (c - 1, c)])

        for s in range(N + 2):
            if s < N:
                stage1(s)
            if 0 <= s - 1 < N:
                stage2(s - 1)
            if 0 <= s - 2 < N:
                stage3(s - 2)
        cx.p.emit()


def phase_outproj_ln(cx, oT, xrows, wout, lng, lnb, XO, ntok):
    with ExitStack() as ls:
        sb = lambda n, s, d: cx.sb(n, s, d, ls)
        PB, PK = cx.PB, cx.PBK
        Wo = sb("Wo", [128, 16, D], BF16)
        wv = wout.rearrange("(k p) c -> p k c", p=128)
        for k in range(16):
            cx.load(Wo[:, k, :], wv[:, k, :], (), ["Wo%d" % k], q="pool")
        oTs = sb("oTs", [128, 16, ntok], BF16)
        ov = oT.rearrange("(k p) t -> p k t", p=128)
        for k4 in range(4):
            cx.load(oTs[:, 4 * k4:4 * k4 + 4, :], ov[:, 4 * k4:4 * k4 + 4, :], (), ["oTs%d" % k4], q="sp")
        gbc = sb("gbc", [128, D], F32)
        bbc = sb("bbc", [128, D], F32)
        cx.load(gbc[:], lng.partition_broadcast(128), (), ["gbc"], q="sp")
        cx.load(bbc[:], lnb.partition_broadcast(128), (), ["bbc"], q="sp")
        xs = cx.rot_sb("xs", 2, [128, D], F32, ls)
        zs = cx.rot_sb("zs", 2, [128, D], F32, ls)
        st6 = cx.rot_sb("st6", 2, [128, 4, 6], F32, ls)
        mv = cx.rot_sb("mv", 2, [128, 2], F32, ls)
        sml = cx.rot_sb("sml", 6, [128, 1], F32, ls)
        banks = cx.banks(list(range(8)), "y")
        for tt in range(ntok // 128):
            x_t, x_k = xs.next()
            cx.load(x_t[:], xrows[tt * 128:(tt + 1) * 128, :], (), [x_k], q="sp")
            z_t, z_k = zs.next()
            s_t, s_k = st6.next()
            for cg in range(4):
                b_t, b_k = banks.next()
                for k in range(16):
                    cx.mm(b_t[:], oTs[:, k, tt * 128:(tt + 1) * 128], Wo[:, k, cg * 512:(cg + 1) * 512],
                          k == 0, k == 15, ["oTs%d" % (k // 4), "Wo%d" % k], [b_k])
                zc = z_t[:, cg * 512:(cg + 1) * 512]
                cx.stt(zc, x_t[:, cg * 512:(cg + 1) * 512], ALPHA, b_t[:], ALU.mult, ALU.add, [x_k, b_k],
                       [z_k + "_%d" % cg])
                cx.p.add("dve", lambda e, o=s_t[:, cg, :], i=zc: e.bn_stats(out=o, in_=i), [z_k + "_%d" % cg],
                         [s_k + "_%d" % cg])
            m_t, m_k = mv.next()
            cx.p.add("dve", lambda e, o=m_t[:], i=s_t[:]: e.bn_aggr(out=o, in_=i),
                     [s_k + "_%d" % cg for cg in range(4)], [m_k])
            sd, sd_k = sml.next()
            cx.ts(sd[:], m_t[:, 1:2], 1e-5, ALU.add, [m_k], [sd_k])
            cx.actf(sd[:], sd[:], ACT.Sqrt, [sd_k], [sd_k])
            rs, rs_k = sml.next()
            cx.recip(rs[:], sd[:], [sd_k], [rs_k])
            nb, nb_k = sml.next()
            cx.stt(nb[:], m_t[:, 0:1], -1.0, rs[:], ALU.mult, ALU.mult, [m_k, rs_k], [nb_k])
            zk = [z_k + "_%d" % cg for cg in range(4)]
            cx.ts(z_t[:], z_t[:], rs[:], ALU.mult, zk + [rs_k, nb_k], zk, s2=nb[:], op1=ALU.add)
            cx.tt(z_t[:], z_t[:], gbc[:], ALU.mult, zk + ["gbc"], zk)
            cx.tt(z_t[:], z_t[:], bbc[:], ALU.add, zk + ["bbc"], zk)
            cx.store(XO[tt * 128:(tt + 1) * 128, :], z_t[:], zk, [("XO", tt)], q="sp")
        cx.p.emit()


def build_B(ntok=1024):
    nc = bass.Bass("TRN2", target_bir_lowering=False)
    with ExitStack() as st:
        cx = Ctx(nc, st)
        oT = cx.dram("oT", [D, ntok], BF16, "ExternalInput")
        xr = cx.dram("xr", [ntok, D], F32, "ExternalInput")
        wout = cx.dram("wout", [D, D], F32, "ExternalInput")
        lng = cx.dram("lng", [1, D], F32, "ExternalInput")
        lnb = cx.dram("lnb", [1, D], F32, "ExternalInput")
        XO = cx.dram("XO", [ntok, D], F32, "ExternalOutput")
        phase_outproj_ln(cx, oT, xr, wout, lng, lnb, XO, ntok)
    return nc


SPEC1 = ([("rope_scale", i, None) for i in range(4)] + [("rope", 4 + i, None) for i in range(4)]
         + [("silu", 8 + i, None) for i in range(4)])
NFO1 = 12


def l1_weights(w_in, g):
    cols = []
    for base in (0, 2048):
        for h in (2 * g, 2 * g + 1):
            for c in range(2):
                o = base + (2 * h + c) * 128
                cols.append(w_in[:, o:o + 128])
    for h in (2 * g, 2 * g + 1):
        for j in range(2):
            o = 6144 + h * 256 + j * 128
            cols.append(w_in[:, o:o + 128])
    wF = np.ascontiguousarray(np.concatenate(cols, 1))
    wT = np.ascontiguousarray(w_in[:, 4096 + 2 * g * 256: 4096 + (2 * g + 2) * 256])
    return wF, wT


def phase_diff_attn(cx, FO, TO, CD, lq1, lk1, lq2, lk2, gng, OUT, on_quarter=None, prefetch=None):
    with ExitStack() as ls:
        sb = lambda n, s, d: cx.sb(n, s, d, ls)
        PB, PK = cx.PB, cx.PBK
        TOv = TO.rearrange("(k p) c -> p k c", p=128)
        q = [sb("q%d" % i, [128, S], BF16) for i in range(4)]
        k = [sb("k%d" % i, [128, S], BF16) for i in range(4)]
        gt = [sb("gt%d" % i, [128, S], BF16) for i in range(4)]
        v = [sb("v%d" % h, [128, NT, 256], BF16) for h in range(2)]
        mcaus = sb("mcaus", [128, 4, 512], BF16)
        cx.load(mcaus[:], CD["bcaus"], (), ["mcaus"], q="pool")
        identb = sb("identb", [128, 128], BF16)
        cx.load(identb[:], CD["ident"], (), ["ident"], q="pool")
        items = []
        for h in range(2):
            for cc in range(2):
                i = 2 * h + cc
                items += [(q[i], FO[i], "q%d" % i, "F"), (k[i], FO[4 + i], "k%d" % i, "F")]
            items.append((v[h], TOv[:, :, 256 * h:256 * (h + 1)], "v%d" % h, "T"))
        for i in range(4):
            items.append((gt[i], FO[8 + i], "gt%d" % i, "F"))
        chunked_loads(cx, items, q="pool")
        ones = sb("ones", [128, 128], BF16)
        cx.memset(ones[:], 1.0, ["ones", "onesb"])
        lam = sb("lam", [128, 1], F32)
        ex = [sb("ex%d" % i, [128, 1], F32) for i in range(2)]
        for i, (a, b_) in enumerate(((lq1, lk1), (lq2, lk2))):
            la = sb("la%d" % i, [128, 128], F32)
            lb = sb("lb%d" % i, [128, 128], F32)
            cx.load(la[:], a.partition_broadcast(128), (), ["la%d" % i])
            cx.load(lb[:], b_.partition_broadcast(128), (), ["lb%d" % i])
            cx.tt(la[:], la[:], lb[:], ALU.mult, ["la%d" % i, "lb%d" % i], ["la%d" % i])
            cx.p.add("dve", lambda e, o=ex[i][:], i_=la[:]: e.reduce_sum(out=o, in_=i_, axis=AX.X),
                     ["la%d" % i], ["ex%d" % i])
            cx.actf(ex[i][:], ex[i][:], ACT.Exp, ["ex%d" % i], ["ex%d" % i])
        cx.ts(lam[:], ex[0][:], ex[1][:], ALU.subtract, ["ex0", "ex1"], ["lam"], s2=-LAMBDA_INIT, op1=ALU.subtract)
        cx.ts(lam[:], lam[:], -1.0, ALU.mult, ["lam"], ["lam"])
        gsc = sb("gsc", [128, 4], F32)
        cx.load(gsc[:], gng, (), ["gsc"])
        cx.ts(gsc[:], gsc[:], 1.0 - LAMBDA_INIT, ALU.mult, ["gsc"], ["gsc"])
        zrot = cx.banks([0, 1], "z")
        sets = Rot("sets", [(4, 5, 2), (6, 7, 3)])
        erot = cx.rot_sb("e", 4, [128, 512], BF16, ls)
        esrot = cx.rot_sb("es", 3, [128, 512], F32, ls)
        hlrot = cx.rot_sb("hl", 6, [128, 512], BF16, ls)
        rd = cx.rot_sb("rd", 3, [128, 512], F32, ls)
        oA = cx.rot_sb("oA", 2, [128, 512], F32, ls)
        oB = cx.rot_sb("oB", 2, [128, 512], F32, ls)
        tm = cx.rot_sb("tm", 2, [128, 512], F32, ls)
        sq = cx.rot_sb("sq", 2, [128, 512], BF16, ls)
        ost = cx.rot_sb("ost", 2, [128, 512], BF16, ls)
        groups = []
        state = {}
        if prefetch is not None:
            prefetch()
        for c in range(NCH):
            for h in range(2):
                tsl = slice(c * 512, (c + 1) * 512)
                for cc in range(2):
                    qi = 2 * h + cc
                    tiles = []
                    for kt in range(4 * c + 4):
                        mk = mcaus[:, kt - 4 * c, :] if kt >= 4 * c else None
                        tiles.append(dict(kT=k[qi][:, kt * 128:(kt + 1) * 128], kkeys=["k%d_%d" % (qi, kt // 4)],
                                          vs=[v[h][:, kt, 0:128], v[h][:, kt, 128:256]],
                                          vkeys=["v%d_%d" % (h, kt // 4)],
                                          maskb=mk, mkeys=["mcaus"]))
                    (ia, ib, idn), _ = sets.next()

                    def epi(h=h, c=c, cc=cc, ia=ia, ib=ib, idn=idn, tsl=tsl):
                        def part1():
                            if cc == 0:
                                state["oh"] = [oA.next(), oB.next()]
                            oh = state["oh"]
                            r_t, r_k = rd.next()
                            cx.actf(r_t[:], PB[idn][:], ACT.Ln, [PK[idn]], [r_k])
                            cx.actf(r_t[:], r_t[:], ACT.Exp, [r_k], [r_k], scale=-1.0)
                            for j, ib_ in enumerate((ia, ib)):
                                o_t, o_k = oh[j]
                                if cc == 0:
                                    cx.tt(o_t[:], PB[ib_][:], r_t[:], ALU.mult, [PK[ib_], r_k], [o_k])
                                else:
                                    t_t, t_k = tm.next()
                                    cx.tt(t_t[:], PB[ib_][:], r_t[:], ALU.mult, [PK[ib_], r_k], [t_k])
                                    cx.stt(o_t[:], t_t[:], lam[:], o_t[:], ALU.mult, ALU.add, [t_k, "lam", o_k],
                                           [o_k])
                            return oh
                        if cc == 0:
                            def d0():
                                part1()
                                return None
                            return d0

                        def d1():
                            oh = part1()
                            return (lambda: part2(oh, h, c, tsl, idn), 8)
                        return d1

                    def part2(oh, h, c, tsl, idn):
                        sqs = []
                        for j in range(2):
                            s_t, s_k = sq.next()
                            cx.tt(s_t[:], oh[j][0][:], oh[j][0][:], ALU.mult, [oh[j][1]], [s_k])
                            sqs.append((s_t, s_k))
                        if True:
                            z_t, z_k = PB[idn], PK[idn]
                            for j in range(2):
                                cx.mm(z_t[:], ones[:], sqs[j][0][:], j == 0, j == 1, ["ones", sqs[j][1]], [z_k])
                            r2, r2_k = rd.next()
                            cx.ts(r2[:], z_t[:], 1.0 / 256.0, ALU.mult, [z_k], [r2_k], s2=1e-5, op1=ALU.add)
                            cx.actf(r2[:], r2[:], ACT.Ln, [r2_k], [r2_k])
                            cx.actf(r2[:], r2[:], ACT.Exp, [r2_k], [r2_k], scale=-0.5)
                            for j in range(2):
                                o_t, o_k = oh[j]
                                cx.stt(o_t[:], o_t[:], gsc[:, 2 * h + j:2 * h + j + 1], r2[:], ALU.mult, ALU.mult,
                                       [o_k, "gsc", r2_k], [o_k])
                                s_o, s_ok = ost.next()
                                cx.tt(s_o[:], o_t[:], gt[2 * h + j][:, tsl], ALU.mult, [o_k, "gt%d_%d" % (2 * h + j, c)],
                                      [s_ok])
                                cx.store(OUT[c // 2, (2 * h + j) * 128:(2 * h + j + 1) * 128,
                                             (c % 2) * 512:(c % 2) * 512 + 512],
                                         s_o[:], [s_ok], [("OUT", 2 * h + j, c)], q="sp")
                        if on_quarter is not None and h == 1 and c % 2 == 1:
                            on_quarter(c // 2, [("OUT", rb, cc_) for rb in range(4) for cc_ in (c - 1, c)])

                    groups.append(dict(q_ap=q[qi][:, tsl], q_keys=["q%d_%d" % (qi, c)], tiles=tiles,
                                       oaccs=[(PB[ia][:], PK[ia]), (PB[ib][:], PK[ib])],
                                       den=(PB[idn][:], PK[idn]), epilogue=epi))
        run_attn_stream(cx, groups, zrot, erot, esrot, identb[:], ones[:], hlrot, LA=1, defer=2)
        cx.p.emit()


def build_C(level=9):
    nc = bass.Bass("TRN2", target_bir_lowering=False)
    with ExitStack() as st:
        cx = Ctx(nc, st)
        C = consts()
        xT = cx.dram("xT", [D, S], F32, "ExternalInput")
        wF = cx.dram("wF", [D, 12 * 128], F32, "ExternalInput")
        wT = cx.dram("wT", [D, 512], F32, "ExternalInput")
        lq1 = cx.dram("lq1", [1, 128], F32, "ExternalInput")
        lk1 = cx.dram("lk1", [1, 128], F32, "ExternalInput")
        lq2 = cx.dram("lq2", [1, 128], F32, "ExternalInput")
        lk2 = cx.dram("lk2", [1, 128], F32, "ExternalInput")
        gng = cx.dram("gng", [128, 4], F32, "ExternalInput")
        CD = {n: cx.dram("c_" + n, list(C[n].shape), F32, "ExternalInput")
              for n in ("ropec", "ropes", "pswap", "bcaus", "ident")}
        dbg = "ExternalOutput" if level < 9 else None
        FO = cx.dram("FO", [NFO1, 128, S], BF16, dbg)
        TO = cx.dram("TO", [S, 512], BF16, dbg)
        OUT = cx.dram("OUT", [4, 512, 1024], BF16, "ExternalOutput")
        phase_proj(cx, xT, wF, wT, None, CD["ropec"], CD["ropes"], CD["pswap"], FO, TO, None, SPEC1, 0)
        phase_diff_attn(cx, FO, TO, CD, lq1, lk1, lq2, lk2, gng, OUT)
    return nc


def inputs_C(x1T_b, od, g):
    C = consts()
    wF, wT = l1_weights(od["w_in"], g)
    gsl = od["gn_g"][2 * g * 256:(2 * g + 2) * 256]
    m = {"xT": x1T_b, "wF": wF, "wT": wT,
         "lq1": od["lq1"][None, :], "lk1": od["lk1"][None, :], "lq2": od["lq2"][None, :], "lk2": od["lk2"][None, :],
         "gng": np.ascontiguousarray(gsl.reshape(4, 128).T)}
    for n in ("ropec", "ropes", "pswap", "bcaus", "ident"):
        m["c_" + n] = C[n]
    return m


_NC = {}


def _get(name, fn):
    if name not in _NC:
        _NC[name] = fn()
    return _NC[name]


def _gather_oT(res, b, r, order):
    blocks = [None] * 16
    for g in range(4):
        O = np.asarray(res[4 * b + g]["OUT"])
        for i in range(4):
            blocks[order(g, i)] = O[i * 128:(i + 1) * 128, r * 1024:(r + 1) * 1024]
    return np.ascontiguousarray(np.concatenate(blocks, 0))


def kernel(x, ev_w_in, ev_pe_k, ev_pe_v, ev_w1_k, ev_w2_k, ev_w1_v, ev_w2_v, ev_w_out, ev_ln_g, ev_ln_b,
           od_w_in, od_lq1, od_lk1, od_lq2, od_lk2, od_gn_g, od_w_out, od_ln_g, od_ln_b):
    x = np.asarray(x, dtype=np.float32)
    ev = dict(w_in=np.asarray(ev_w_in)[0], pe_k=np.asarray(ev_pe_k)[0], pe_v=np.asarray(ev_pe_v)[0],
              w1_k=np.asarray(ev_w1_k)[0], w2_k=np.asarray(ev_w2_k)[0], w1_v=np.asarray(ev_w1_v)[0],
              w2_v=np.asarray(ev_w2_v)[0])
    od = dict(w_in=np.asarray(od_w_in)[0], lq1=np.asarray(od_lq1)[0], lk1=np.asarray(od_lk1)[0],
              lq2=np.asarray(od_lq2)[0], lk2=np.asarray(od_lk2)[0], gn_g=np.asarray(od_gn_g)[0])
    cores = list(range(8))
    ncA = _get("A", build_A)
    resA = run_bass_kernel_spmd(ncA, [inputs_A(x, ev, c) for c in cores], core_ids=cores).results
    ordA = lambda g, i: (2 * g + i) if i < 2 else (8 + 2 * g + (i - 2))
    ncB = _get("B", build_B)
    inB = []
    for c in cores:
        b, r = c // 4, c % 4
        inB.append({"oT": _gather_oT(resA, b, r, ordA), "xr": np.ascontiguousarray(x[b, r * 1024:(r + 1) * 1024]),
                    "wout": np.asarray(ev_w_out)[0], "lng": np.asarray(ev_ln_g), "lnb": np.asarray(ev_ln_b)})
    resB = run_bass_kernel_spmd(ncB, inB, core_ids=cores).results
    x1 = np.stack([np.concatenate([np.asarray(resB[4 * b + r]["XO"]) for r in range(4)], 0) for b in range(2)], 0)
    ncC = _get("C", build_C)
    x1T = [np.ascontiguousarray(x1[b].T) for b in range(2)]
    resC = run_bass_kernel_spmd(ncC, [inputs_C(x1T[c // 4], od, c % 4) for c in cores], core_ids=cores).results
    ordC = lambda g, i: 4 * g + i
    inD = []
    for c in cores:
        b, r = c // 4, c % 4
        inD.append({"oT": _gather_oT(resC, b, r, ordC), "xr": np.ascontiguousarray(x1[b, r * 1024:(r + 1) * 1024]),
                    "wout": np.asarray(od_w_out)[0], "lng": np.asarray(od_ln_g), "lnb": np.asarray(od_ln_b)})
    resD = run_bass_kernel_spmd(ncB, inD, core_ids=cores).results
    out = np.stack([np.concatenate([np.asarray(resD[4 * b + r]["XO"]) for r in range(4)], 0) for b in range(2)], 0)
    return out.astype(np.float32)


I32 = mybir.dt.int32


def load_wo(cx, Wo, wout):
    wv = wout.rearrange("(k p) c -> p k c", p=128)
    for cg in range(4):
        for kh in range(2):
            cx.load(Wo[:, 8 * kh:8 * kh + 8, cg * 512:(cg + 1) * 512], wv[:, 8 * kh:8 * kh + 8, cg * 512:(cg + 1) * 512],
                    (), ["Wo%d" % cg], q="pool")


def phase_outproj_ln2(cx, G, roff, xrows, wout, lng, lnb, XO, identD=None, X1T=None, ntok=1024, Wo_pre=None,
                      on_piece=None):
    with ExitStack() as ls:
        sb = lambda n, s, d: cx.sb(n, s, d, ls)
        ri = sb("ri", [1, 1], I32)
        cx.load(ri[:], roff, (), ["ri"], q="pool")
        oTs = sb("oTs", [128, 16, ntok], BF16)
        Gv = G.rearrange("j (k p) t -> p (j k) t", p=128)

        def dyn_load(k4):
            def fn(e):
                with e.register("roff%d_%d" % (cx.p.nphase, k4)) as rr:
                    e.reg_load(rr, ri[0:1, 0:1])
                    v = e.snap(rr)
                    return e.dma_start(out=oTs[:, 4 * k4:4 * k4 + 4, :], in_=Gv[:, bass.ds(v + 4 * k4, 4), :])
            return fn

        import os
        if Wo_pre is None:
            Wo = sb("Wo", [128, 16, D], BF16)
            load_wo(cx, Wo, wout)
        else:
            Wo = Wo_pre
        for k4 in range(4):
            if os.environ.get("MK_STATIC"):
                cx.load(oTs[:, 4 * k4:4 * k4 + 4, :], Gv[:, 4 * k4:4 * k4 + 4, :], ["G"], ["oTs%d" % k4], q="pool")
            else:
                cx.p.add("pool", dyn_load(k4), ["ri", "G0", "G1", "G2", "G3"], ["oTs%d" % k4], dma=True)
        gbc = sb("gbc", [128, D], F32)
        bbc = sb("bbc", [128, D], F32)
        cx.load(gbc[:], lng.partition_broadcast(128), (), ["gbc"], q="sp")
        cx.load(bbc[:], lnb.partition_broadcast(128), (), ["bbc"], q="sp")
        if X1T is not None:
            idb = sb("idb", [128, 128], BF16)
            cx.load(idb[:], identD, (), ["idb"], q="pool")
            x1Ts = sb("x1Ts", [128, 16, ntok], BF16)
            zb = cx.rot_sb("zb", 2, [128, D], BF16, ls)
        xs = cx.rot_sb("xs", 2, [128, D], F32, ls)
        zs = cx.rot_sb("zs", 2, [128, D], F32, ls)
        st6 = cx.rot_sb("st6", 2, [128, 4, 6], F32, ls)
        mv = cx.rot_sb("mv", 2, [128, 2], F32, ls)
        sml = cx.rot_sb("sml", 6, [128, 1], F32, ls)
        banks = cx.banks(list(range(8)), "y")
        for tt in range(ntok // 128):
            x_t, x_k = xs.next()
            cx.load(x_t[:], xrows[tt * 128:(tt + 1) * 128, :], ["X1R"], [x_k], q="sp")
            z_t, z_k = zs.next()
            s_t, s_k = st6.next()
            for cg in range(4):
                b_t, b_k = banks.next()
                for k in range(16):
                    cx.mm(b_t[:], oTs[:, k, tt * 128:(tt + 1) * 128], Wo[:, k, cg * 512:(cg + 1) * 512],
                          k == 0, k == 15, ["oTs%d" % (k // 4), "Wo%d" % cg], [b_k])
                zc = z_t[:, cg * 512:(cg + 1) * 512]
                cx.stt(zc, x_t[:, cg * 512:(cg + 1) * 512], ALPHA, b_t[:], ALU.mult, ALU.add, [x_k, b_k],
                       [z_k + "_%d" % cg])
                cx.p.add("dve", lambda e, o=s_t[:, cg, :], i=zc: e.bn_stats(out=o, in_=i), [z_k + "_%d" % cg],
                         [s_k + "_%d" % cg])
            m_t, m_k = mv.next()
            cx.p.add("dve", lambda e, o=m_t[:], i=s_t[:]: e.bn_aggr(out=o, in_=i),
                     [s_k + "_%d" % cg for cg in range(4)], [m_k])
            sd, sd_k = sml.next()
            cx.ts(sd[:], m_t[:, 1:2], 1e-5, ALU.add, [m_k], [sd_k])
            cx.actf(sd[:], sd[:], ACT.Sqrt, [sd_k], [sd_k])
            rs, rs_k = sml.next()
            cx.recip(rs[:], sd[:], [sd_k], [rs_k])
            nb, nb_k = sml.next()
            cx.stt(nb[:], m_t[:, 0:1], -1.0, rs[:], ALU.mult, ALU.mult, [m_k, rs_k], [nb_k])
            zk = [z_k + "_%d" % cg for cg in range(4)]
            cx.ts(z_t[:], z_t[:], rs[:], ALU.mult, zk + [rs_k, nb_k], zk, s2=nb[:], op1=ALU.add)
            cx.tt(z_t[:], z_t[:], gbc[:], ALU.mult, zk + ["gbc"], zk)
            cx.tt(z_t[:], z_t[:], bbc[:], ALU.add, zk + ["bbc"], zk)
            cx.store(XO[tt * 128:(tt + 1) * 128, :], z_t[:], zk, [("XO", tt)], q="sp")
            if X1T is not None:
                zb_t, zb_k = zb.next()
                cx.actf(zb_t[:], z_t[:], ACT.Copy, zk, [zb_k])
                for k4 in range(4):
                    b_t, b_k = banks.next()
                    for j in range(4):
                        kk = 4 * k4 + j
                        cx.mm(b_t[:, j * 128:(j + 1) * 128], zb_t[:, kk * 128:(kk + 1) * 128], idb[:], True, True,
                              [zb_k, "idb"], [b_k])
                    cx.actf(x1Ts[:, 4 * k4:4 * k4 + 4, tt * 128:(tt + 1) * 128],
                            b_t[:].rearrange("p (j t) -> p j t", j=4), ACT.Copy, [b_k], ["x1Ts%d_%d" % (tt, k4)])
                X1Tv = X1T.rearrange("j (k p) t -> p j k t", p=128)
                cx.store(X1Tv[:, tt // 2, :, (tt % 2) * 128:(tt % 2) * 128 + 128], x1Ts[:, :, tt * 128:(tt + 1) * 128],
                         ["x1Ts%d_%d" % (tt, k4) for k4 in range(4)], [("X1T", tt)], q="sp")
                if on_piece is not None and tt % 2 == 1:
                    on_piece(tt // 2, [("X1T", tt - 1), ("X1T", tt)])
        cx.p.emit()


def allgather(cx, src, dst, name):
    for j in range(4):
        cx.p.collective(cx.st, lambda e, s=src[j], d=dst[j]: e.collective_compute(
            "AllGather", ALU.bypass, replica_groups=[[0, 1, 2, 3], [4, 5, 6, 7]], ins=[s], outs=[d]),
            (), ["%s%d" % (name, j)])


def allgather_j(cx, src, dst, name, j, reads):
    cx.p.collective(cx.st, lambda e, s=src[j], d=dst[j]: e.collective_compute(
        "AllGather", ALU.bypass, replica_groups=[[0, 1, 2, 3], [4, 5, 6, 7]], ins=[s], outs=[d]),
        list(reads), ["%s%d" % (name, j)])


CONST_F = CONST_A


def build_fused():
    nc = bass.Bass("TRN2", target_bir_lowering=False)
    with ExitStack() as st:
        cx = Ctx(nc, st)
        C = consts()
        ein = lambda n, s, d=F32: cx.dram(n, s, d, "ExternalInput")
        xT = ein("xT", [D, S])
        xr = ein("xr", [1024, D])
        roff = ein("roff", [1, 1], I32)
        wF = ein("wF", [D, 16 * 128])
        wT = ein("wT", [D, 512])
        wG = ein("wG", [D, 6])
        w1k = ein("w1k", [4096, 128])
        w1v = ein("w1v", [4096, 128])
        w2k = ein("w2k", [128, 128])
        w2v = ein("w2v", [128, 128])
        pekT = ein("pekT", [128, 32])
        pevT = ein("pevT", [128, 32])
        wo0 = ein("wo0", [D, D])
        lng0 = ein("lng0", [1, D])
        lnb0 = ein("lnb0", [1, D])
        wF1 = ein("wF1", [D, 12 * 128])
        wT1 = ein("wT1", [D, 512])
        lq1 = ein("lq1", [1, 128])
        lk1 = ein("lk1", [1, 128])
        lq2 = ein("lq2", [1, 128])
        lk2 = ein("lk2", [1, 128])
        gng = ein("gng", [128, 4])
        wo1 = ein("wo1", [D, D])
        lng1 = ein("lng1", [1, D])
        lnb1 = ein("lnb1", [1, D])
        CD = {n: ein("c_" + n, list(C[n].shape)) for n in CONST_F}
        FO = cx.dram("FO", [NFO0, 128, S], BF16)
        TO = cx.dram("TO", [S, 512], BF16)
        GO = cx.dram("GO", [8, S], F32)
        KCMP = cx.dram("KCMP", [128, 256], BF16)
        VCMP = cx.dram("VCMP", [128, 2, 128], BF16)
        SELB = cx.dram("SELB", [64, S], BF16)
        OUT0 = cx.dram("OUT0", [4, 512, 1024], BF16)
        G0 = cx.dram("G0", [4, 4 * 512, 1024], BF16)
        X1R = cx.dram("X1R", [1024, D], F32)
        X1T = cx.dram("X1T", [4, D, 256], BF16)
        X1G = cx.dram("X1G", [4, 4 * D, 256], BF16)
        FO1 = cx.dram("FO1", [NFO1, 128, S], BF16)
        TO1 = cx.dram("TO1", [S, 512], BF16)
        OUT1 = cx.dram("OUT1", [4, 512, 1024], BF16)
        G1 = cx.dram("G1", [4, 4 * 512, 1024], BF16)
        XO = cx.dram("XO", [1024, D], F32, "ExternalOutput")
        phase_proj(cx, xT, wF, wT, wG, CD["ropec"], CD["ropes"], CD["pswap"], FO, TO, GO, SPEC0, 6)
        phase_nsa_prep(cx, FO, w1k, w1v, w2k, w2v, pekT, pevT, CD, KCMP, VCMP, SELB)
        phase_nsa_attn(cx, FO, TO, GO, CD, KCMP, VCMP, SELB, OUT0)
        with ExitStack() as span:
            Wo0 = cx.sb("Wo0", [128, 16, D], BF16, span)
            phase_sb_attn(cx, FO, TO, CD, OUT0,
                          on_quarter=lambda j, reads: allgather_j(cx, OUT0, G0, "G", j, reads),
                          prefetch=lambda: load_wo(cx, Wo0, wo0))
            phase_outproj_ln2(cx, G0, roff, xr, wo0, lng0, lnb0, X1R, CD["ident"], X1T, Wo_pre=Wo0,
                              on_piece=lambda j, reads: allgather_j(cx, X1T, X1G, "X1G", j, reads))
        X1Gv = X1G.rearrange("j (r k p) t -> p j r k t", r=4, p=128)
        xsrc = lambda c, k4: [((jj * 256, jj * 256 + 256), X1Gv[:, (c % 2) * 2 + jj, c // 2, 4 * k4:4 * k4 + 4, :])
                              for jj in range(2)]
        phase_proj(cx, None, wF1, wT1, None, CD["ropec"], CD["ropes"], CD["pswap"], FO1, TO1, None, SPEC1, 0,
                   xsrc=xsrc)
        phase_diff_attn(cx, FO1, TO1, CD, lq1, lk1, lq2, lk2, gng, OUT1,
                        on_quarter=lambda j, reads: allgather_j(cx, OUT1, G1, "G", j, reads))
        phase_outproj_ln2(cx, G1, roff, X1R, wo1, lng1, lnb1, XO)
    return nc


def _perm_wout0(w):
    rows = []
    for g in range(4):
        for i in range(4):
            blk = (2 * g + i) if i < 2 else (8 + 2 * g + (i - 2))
            rows.append(w[blk * 128:(blk + 1) * 128])
    return np.ascontiguousarray(np.concatenate(rows, 0))


def kernel(x, ev_w_in, ev_pe_k, ev_pe_v, ev_w1_k, ev_w2_k, ev_w1_v, ev_w2_v, ev_w_out, ev_ln_g, ev_ln_b,
           od_w_in, od_lq1, od_lk1, od_lq2, od_lk2, od_gn_g, od_w_out, od_ln_g, od_ln_b):
    x = np.asarray(x, dtype=np.float32)
    ev = dict(w_in=np.asarray(ev_w_in)[0], pe_k=np.asarray(ev_pe_k)[0], pe_v=np.asarray(ev_pe_v)[0],
              w1_k=np.asarray(ev_w1_k)[0], w2_k=np.asarray(ev_w2_k)[0], w1_v=np.asarray(ev_w1_v)[0],
              w2_v=np.asarray(ev_w2_v)[0])
    od = dict(w_in=np.asarray(od_w_in)[0], lq1=np.asarray(od_lq1)[0], lk1=np.asarray(od_lk1)[0],
              lq2=np.asarray(od_lq2)[0], lk2=np.asarray(od_lk2)[0], gn_g=np.asarray(od_gn_g)[0])
    C = consts()
    wo0 = _perm_wout0(np.asarray(ev_w_out)[0])
    xTs = [np.ascontiguousarray(x[b].T) for b in range(2)]
    cores = list(range(8))
    maps = []
    for c in cores:
        b, g = c // 4, c % 4
        m = inputs_A(x, ev, c)
        m["xT"] = xTs[b]
        m["xr"] = np.ascontiguousarray(x[b, g * 1024:(g + 1) * 1024])
        m["roff"] = np.array([[g * 16]], np.int32)
        m["wo0"] = wo0
        m["lng0"] = np.asarray(ev_ln_g, np.float32).reshape(1, D)
        m["lnb0"] = np.asarray(ev_ln_b, np.float32).reshape(1, D)
        mc = inputs_C(None, od, g)
        m["wF1"], m["wT1"] = mc["wF"], mc["wT"]
        for n in ("lq1", "lk1", "lq2", "lk2", "gng"):
            m[n] = mc[n]
        m["wo1"] = np.asarray(od_w_out)[0]
        m["lng1"] = np.asarray(od_ln_g, np.float32).reshape(1, D)
        m["lnb1"] = np.asarray(od_ln_b, np.float32).reshape(1, D)
        maps.append(m)
    nc = _get("F", build_fused)
    res = run_bass_kernel_spmd(nc, maps, core_ids=cores).results
    out = np.stack([np.concatenate([np.asarray(res[4 * b + r]["XO"]) for r in range(4)], 0) for b in range(2)], 0)
    return out.astype(np.float32)
```
